# Optimizing a Trainium2 kernel written in Bass

```python
import math
import jax, jax.numpy as jnp
from jax import lax
import numpy as np

D_MODEL = 1024
BATCH = 16
SEQ = 4096
DEPTH = 1
DEC_BATCH = 8
DEC_SEQ = 64
PAST_LEN = 1024

CHUNK = 64
Q_BLOCK = 128
EPS = 1e-6
NEG = -1e30
D_MLSTM = D_MODEL
H_M = 4
DH_M = D_MLSTM // H_M
CONV_W = 4
H_A = 8
NOPE = 128
ROPE = 64
V_DIM = 128
Q_LORA = 384
KV_LORA = 256
D_ATT = H_A * V_DIM
ROPE_THETA = 10000.0
ATT_SCALE = (NOPE + ROPE) ** -0.5
SPLITS = (D_MLSTM, D_MLSTM, H_M, H_M, D_MLSTM, D_MLSTM, Q_LORA, KV_LORA, ROPE, D_ATT, D_MODEL, D_MODEL)
N_IN = D_MLSTM * 4 + H_M * 2 + Q_LORA + KV_LORA + ROPE + D_ATT + D_MODEL * 2

kernel_name = "mlstm_mla_gated_hybrid_stream_step"


def rmsnorm(x, g):
    xf = x.astype(jnp.float32)
    r = lax.rsqrt(jnp.mean(xf * xf, axis=-1, keepdims=True) + EPS)
    return (xf * r).astype(x.dtype) * g


def rope(x, pos):
    half = ROPE // 2
    inv = jnp.power(ROPE_THETA, -jnp.arange(half, dtype=jnp.float32) / half)
    ang = pos.astype(jnp.float32)[:, None] * inv[None, :]
    shape = (pos.shape[0],) + (1,) * (x.ndim - 3) + (half,)
    cos = jnp.cos(ang).reshape(shape)
    sin = jnp.sin(ang).reshape(shape)
    xf = x.astype(jnp.float32)
    x1, x2 = xf[..., :half], xf[..., half:]
    return jnp.concatenate([x1 * cos - x2 * sin, x2 * cos + x1 * sin], axis=-1).astype(x.dtype)


def split_cols(proj):
    out, start = [], 0
    for w in SPLITS:
        out.append(proj[..., start:start + w])
        start += w
    return out


def causal_conv(u, buf, w, b):
    L = u.shape[1]
    full = jnp.concatenate([buf.astype(u.dtype), u], axis=1)
    out = b + sum(full[:, j:j + L, :] * w[j] for j in range(CONV_W))
    return out, full[:, -(CONV_W - 1):, :]


def mlstm_chunk(carry, xs):
    C0, n0, m0 = carry
    q, k, v, ig, lf = xs
    f32 = jnp.float32
    qf, kf, vf = q.astype(f32), k.astype(f32), v.astype(f32)
    L = q.shape[2]
    b = jnp.cumsum(lf, axis=-1)
    causal = jnp.tril(jnp.ones((L, L), dtype=bool))
    dlog = b[..., :, None] - b[..., None, :] + ig[..., None, :]
    dlog = jnp.where(causal, dlog, NEG)
    g = b + m0[..., None]
    m = jnp.maximum(g, jnp.max(dlog, axis=-1))
    w_intra = jnp.exp(dlog - m[..., None])
    w_inter = jnp.exp(g - m)
    s = jnp.einsum('bhtd,bhsd->bhts', qf, kf) * w_intra
    num = w_inter[..., None] * jnp.einsum('bhtd,bhde->bhte', qf, C0) + jnp.einsum('bhts,bhse->bhte', s, vf)
    den = w_inter * jnp.einsum('bhtd,bhd->bht', qf, n0) + jnp.sum(s, axis=-1)
    h = num / jnp.maximum(jnp.abs(den), jnp.exp(-m))[..., None]
    m_last = m[..., -1]
    w_state = jnp.exp(b[..., -1:] - b + ig - m_last[..., None])
    decay = jnp.exp(b[..., -1] + m0 - m_last)
    C1 = decay[..., None, None] * C0 + jnp.einsum('bhsd,bhse->bhde', kf * w_state[..., None], vf)
    n1 = decay[..., None] * n0 + jnp.einsum('bhs,bhsd->bhd', w_state, kf)
    return (C1, n1, m_last), h


def mlstm_seq(q, k, v, ig, lf, C0, n0, m0):
    B, H, L, _ = q.shape
    if L <= CHUNK:
        (C1, n1, m1), h = mlstm_chunk((C0, n0, m0), (q, k, v, ig, lf))
        return h, C1, n1, m1
    nc = L // CHUNK

    def to_chunks(a):
        return jnp.moveaxis(a.reshape((B, H, nc, CHUNK) + a.shape[3:]), 2, 0)

    xs = (to_chunks(q), to_chunks(k), to_chunks(v), to_chunks(ig), to_chunks(lf))
    (C1, n1, m1), hc = lax.scan(mlstm_chunk, (C0, n0, m0), xs)
    h = jnp.moveaxis(hc, 0, 2).reshape(B, H, L, -1)
    return h, C1, n1, m1


def mlstm_branch(xcm, vm, i_pre, f_pre, o_pre, conv_buf, C0, n0, m0,
                 b_if, conv_w, conv_b, wq_m, wk_m, hnorm_g):
    f32 = jnp.float32
    B, L, _ = xcm.shape
    u, new_buf = causal_conv(xcm, conv_buf, conv_w, conv_b)
    uh = jax.nn.silu(u).reshape(B, L, H_M, DH_M)
    q = jnp.einsum('blhd,hde->bhle', uh, wq_m)
    k = jnp.einsum('blhd,hde->bhle', uh, wk_m) * (DH_M ** -0.5)
    v = vm.reshape(B, L, H_M, DH_M).transpose(0, 2, 1, 3)
    ig = (i_pre + b_if[:H_M]).astype(f32).transpose(0, 2, 1)
    lf = jax.nn.log_sigmoid((f_pre + b_if[H_M:]).astype(f32)).transpose(0, 2, 1)
    h, C1, n1, m1 = mlstm_seq(q, k, v, ig, lf, C0.astype(f32), n0.astype(f32), m0.astype(f32))
    h = rmsnorm(h.transpose(0, 2, 1, 3), hnorm_g).astype(xcm.dtype)
    h = h.reshape(B, L, D_MLSTM) * jax.nn.sigmoid(o_pre)
    return h, new_buf, C1, n1, m1


def mla_attend(qn, qr, q_pos, kn, kr, v, k_pos):
    s = (jnp.einsum('bqhd,bkhd->bhqk', qn, kn) + jnp.einsum('bqhd,bkd->bhqk', qr, kr)).astype(jnp.float32) * ATT_SCALE
    mask = (k_pos[None, :] // CHUNK) <= (q_pos[:, None] // CHUNK)
    s = jnp.where(mask[None, None], s, NEG)
    p = jax.nn.softmax(s, axis=-1).astype(v.dtype)
    return jnp.einsum('bhqk,bkhd->bqhd', p, v)


def mla_branch(c_q, c_kv, k_r, past_ckv, past_kr, qn_g, w_uq, kvn_g, w_ukv, g_qn, g_qr, g_kn, g_kr):
    B, L, _ = c_q.shape
    P = past_ckv.shape[1]
    q_pos = P + jnp.arange(L)
    k_pos = jnp.arange(P + L)
    q = (rmsnorm(c_q, qn_g) @ w_uq).reshape(B, L, H_A, NOPE + ROPE)
    q_nope = rmsnorm(q[..., :NOPE], g_qn)
    q_rope = rope(rmsnorm(q[..., NOPE:], g_qr), q_pos)
    ckv_new = rmsnorm(c_kv, kvn_g)
    kr_new = rope(rmsnorm(k_r, g_kr), q_pos)
    ckv = jnp.concatenate([past_ckv.astype(ckv_new.dtype), ckv_new], axis=1)
    kr = jnp.concatenate([past_kr.astype(kr_new.dtype), kr_new], axis=1)
    kv = (ckv @ w_ukv).reshape(B, P + L, H_A, NOPE + V_DIM)
    k_nope = rmsnorm(kv[..., :NOPE], g_kn)
    v = kv[..., NOPE:]
    if L > Q_BLOCK and L % Q_BLOCK == 0:
        nb = L // Q_BLOCK
        qn_b = q_nope.reshape(B, nb, Q_BLOCK, H_A, NOPE).transpose(1, 0, 2, 3, 4)
        qr_b = q_rope.reshape(B, nb, Q_BLOCK, H_A, ROPE).transpose(1, 0, 2, 3, 4)
        pos_b = q_pos.reshape(nb, Q_BLOCK)
        ob = lax.map(lambda a: mla_attend(a[0], a[1], a[2], k_nope, kr, v, k_pos), (qn_b, qr_b, pos_b))
        o = ob.transpose(1, 0, 2, 3, 4).reshape(B, L, H_A, V_DIM)
    else:
        o = mla_attend(q_nope, q_rope, q_pos, k_nope, kr, v, k_pos)
    return o.reshape(B, L, D_ATT), ckv_new, kr_new


def hybrid_layer(x, past_ckv, past_kr, conv_buf, C0, n0, m0,
                 norm_g, w_in, b_if, conv_w, conv_b, wq_m, wk_m, hnorm_g,
                 qn_g, w_uq, kvn_g, w_ukv, g_qn, g_qr, g_kn, g_kr, w_pm, w_pa, w_out):
    h = rmsnorm(x, norm_g)
    (xcm, vm, i_pre, f_pre, o_pre, z_m, c_q, c_kv, k_r, z_a, g_m, g_a) = split_cols(h @ w_in)
    hm, conv_new, C1, n1, m1 = mlstm_branch(xcm, vm, i_pre, f_pre, o_pre, conv_buf, C0, n0, m0,
                                            b_if, conv_w, conv_b, wq_m, wk_m, hnorm_g)
    ha, ckv_new, kr_new = mla_branch(c_q, c_kv, k_r, past_ckv, past_kr,
                                     qn_g, w_uq, kvn_g, w_ukv, g_qn, g_qr, g_kn, g_kr)
    p_m = (hm * jax.nn.silu(z_m)) @ w_pm
    p_a = (ha * jax.nn.silu(z_a)) @ w_pa
    merged = jax.nn.sigmoid(g_m) * p_m + jax.nn.sigmoid(g_a) * p_a
    y = x + merged @ w_out
    return y, ckv_new, kr_new, conv_new, C1, n1, m1


def setup_inputs(seed: int = 0) -> dict:
    key = jax.random.key(seed)
    ks = jax.random.split(key, 32)
    f32 = jnp.float32

    def nrm(k, shape, scale):
        return jax.random.normal(k, shape, f32) * scale

    def gain(k, shape):
        return 1.0 + 0.02 * jax.random.normal(k, shape, f32)

    b_i = nrm(ks[0], (DEPTH, H_M), 0.1)
    b_f = jnp.linspace(3.0, 6.0, H_M, dtype=f32)[None, :] + nrm(ks[1], (DEPTH, H_M), 0.1)
    return {
        "x_prompt": nrm(ks[2], (BATCH, SEQ, D_MODEL), 1.0),
        "x_sample": nrm(ks[3], (DEC_BATCH, DEC_SEQ, D_MODEL), 1.0),
        "cache_ckv": nrm(ks[4], (DEPTH, DEC_BATCH, PAST_LEN, KV_LORA), 1.0),
        "cache_kr": nrm(ks[5], (DEPTH, DEC_BATCH, PAST_LEN, ROPE), 1.0),
        "state_conv": nrm(ks[6], (DEPTH, DEC_BATCH, CONV_W - 1, D_MLSTM), 1.0),
        "state_C": nrm(ks[7], (DEPTH, DEC_BATCH, H_M, DH_M, DH_M), 0.5),
        "state_n": nrm(ks[8], (DEPTH, DEC_BATCH, H_M, DH_M), 0.5),
        "state_m": nrm(ks[9], (DEPTH, DEC_BATCH, H_M), 1.0),
        "norm_g": gain(ks[10], (DEPTH, D_MODEL)),
        "w_in": nrm(ks[11], (DEPTH, D_MODEL, N_IN), D_MODEL ** -0.5),
        "b_if": jnp.concatenate([b_i, b_f], axis=-1),
        "conv_w": nrm(ks[12], (DEPTH, CONV_W, D_MLSTM), CONV_W ** -0.5),
        "conv_b": nrm(ks[13], (DEPTH, D_MLSTM), 0.02),
        "wq_m": nrm(ks[14], (DEPTH, H_M, DH_M, DH_M), DH_M ** -0.5),
        "wk_m": nrm(ks[15], (DEPTH, H_M, DH_M, DH_M), DH_M ** -0.5),
        "hnorm_g": gain(ks[16], (DEPTH, H_M, DH_M)),
        "qn_g": gain(ks[17], (DEPTH, Q_LORA)),
        "w_uq": nrm(ks[18], (DEPTH, Q_LORA, H_A * (NOPE + ROPE)), Q_LORA ** -0.5),
        "kvn_g": gain(ks[19], (DEPTH, KV_LORA)),
        "w_ukv": nrm(ks[20], (DEPTH, KV_LORA, H_A * (NOPE + V_DIM)), KV_LORA ** -0.5),
        "g_qn": gain(ks[21], (DEPTH, NOPE)),
        "g_qr": gain(ks[22], (DEPTH, ROPE)),
        "g_kn": gain(ks[23], (DEPTH, NOPE)),
        "g_kr": gain(ks[24], (DEPTH, ROPE)),
        "w_pm": nrm(ks[25], (DEPTH, D_MLSTM, D_MODEL), D_MLSTM ** -0.5),
        "w_pa": nrm(ks[26], (DEPTH, D_ATT, D_MODEL), D_ATT ** -0.5),
        "w_out": nrm(ks[27], (DEPTH, D_MODEL, D_MODEL), D_MODEL ** -0.5),
    }


def reference(x_prompt, x_sample, cache_ckv, cache_kr, state_conv, state_C, state_n, state_m,
              norm_g, w_in, b_if, conv_w, conv_b, wq_m, wk_m, hnorm_g,
              qn_g, w_uq, kvn_g, w_ukv, g_qn, g_qr, g_kn, g_kr, w_pm, w_pa, w_out):
    f32 = jnp.float32
    dt = x_prompt.dtype
    B = x_prompt.shape[0]
    yp, ys = x_prompt, x_sample
    outs_p = [[] for _ in range(6)]
    outs_s = [[] for _ in range(6)]
    for l in range(DEPTH):
        pw = (norm_g[l], w_in[l], b_if[l], conv_w[l], conv_b[l], wq_m[l], wk_m[l], hnorm_g[l],
              qn_g[l], w_uq[l], kvn_g[l], w_ukv[l], g_qn[l], g_qr[l], g_kn[l], g_kr[l],
              w_pm[l], w_pa[l], w_out[l])
        yp, ckv_a, kr_a, conv_a, C_a, n_a, m_a = hybrid_layer(
            yp, jnp.zeros((B, 0, KV_LORA), dt), jnp.zeros((B, 0, ROPE), dt),
            jnp.zeros((B, CONV_W - 1, D_MLSTM), dt), jnp.zeros((B, H_M, DH_M, DH_M), f32),
            jnp.zeros((B, H_M, DH_M), f32), jnp.zeros((B, H_M), f32), *pw)
        ys, ckv_b, kr_b, conv_b_, C_b, n_b, m_b = hybrid_layer(
            ys, cache_ckv[l], cache_kr[l], state_conv[l], state_C[l], state_n[l], state_m[l], *pw)
        for lst, a in zip(outs_p, (ckv_a, kr_a, conv_a, C_a, n_a, m_a)):
            lst.append(a.astype(dt))
        for lst, a in zip(outs_s, (ckv_b, kr_b, conv_b_, C_b, n_b, m_b)):
            lst.append(a.astype(dt))
    ckv_p, kr_p, conv_p, C_p, n_p, m_p = [jnp.stack(a) for a in outs_p]
    ckv_s, kr_s, conv_s, C_s, n_s, m_s = [jnp.stack(a) for a in outs_s]
    return (yp, ys, ckv_p, kr_p, conv_p, C_p, n_p, m_p, ckv_s, kr_s, conv_s, C_s, n_s, m_s)
```

```python
import numpy as np
from contextlib import ExitStack
import concourse.bass as bass
import concourse.mybir as mybir
from concourse.bass_utils import run_bass_kernel_spmd

F32 = mybir.dt.float32
BF16 = mybir.dt.bfloat16
AF = mybir.ActivationFunctionType
ALU = mybir.AluOpType
AX = mybir.AxisListType

D = 1024
HM, DH = 4, 256
HA, NOPE, ROPE, VD = 8, 128, 64, 128
QL, KVL = 384, 256
EPS = 1e-6
ATT_SCALE = float((NOPE + ROPE) ** -0.5)
NUNIT = 22
RING = 3
NKVS = 4
NONLEGACY = ("A.", "C.", "D.")
NPOS = 4096 + 1024


class Sig:
    def __init__(self, sem, unit, name):
        self.sem, self.unit, self.name, self.n = sem, unit, name, 0


class Buf:
    def __init__(self, name, rng=None):
        self.name, self.rng = name, rng
        self.w = None
        self.r = {}
        self.over = []
        self.legacy = True
        self.psum = False


class Op:
    __slots__ = ("eng", "meth", "kw", "deps", "sig", "inc", "val", "dma", "w_bufs")

    def __init__(self, eng, meth, kw, sig, dma):
        self.eng, self.meth, self.kw, self.sig, self.dma = eng, meth, kw, sig, dma
        self.deps, self.inc, self.val = [], dma, 0


class Prog:
    def __init__(self, nc, es):
        self.nc, self.es = nc, es
        self.h = {"pe": nc.tensor, "act": nc.scalar, "dve": nc.vector, "pool": nc.gpsimd, "sp": nc.sync}
        self.sig = {k: Sig(es.enter_context(nc.semaphore("s_" + k)), 1, k) for k in ("pe", "act", "dve", "pool")}
        self.ops = []
        self.bufs = []
        self.dsigs = []

    def buf(self, name, rng=None):
        b = Buf(name, rng)
        if rng is not None:
            for o in self.bufs:
                if o.rng is not None and o.rng[0] == rng[0] and o.rng[1] < rng[2] and rng[1] < o.rng[2]:
                    o.over.append(b)
                    b.over.append(o)
        self.bufs.append(b)
        if any(name.startswith(p) for p in NONLEGACY):
            b.legacy = False
        return b

    def dma_sig(self, name):
        s = Sig(self.es.enter_context(self.nc.semaphore("d_" + name)), 16, name)
        self.dsigs.append(s)
        return s

    def _need(self, o, p, raw):
        if p is o:
            return False
        if not o.dma and not p.dma and o.eng == p.eng:
            return o.eng != "pe"
        if o.dma and p.dma and o.sig is p.sig and raw == "waw":
            return False
        return True

    def op(self, eng, meth, kw, reads=(), writes=(), sig=None):
        dma = sig is not None
        o = Op(eng, meth, kw, sig if dma else self.sig[eng], dma)
        deps = {}
        for b in reads:
            for x in [b] + b.over:
                if x.w is not None and self._need(o, x.w, "raw"):
                    deps[id(x.w)] = x.w
                if x.psum:
                    for r in x.r.values():
                        if r.eng != o.eng:
                            deps[id(r)] = r
        for b in writes:
            for x in [b] + b.over:
                for r in x.r.values():
                    if self._need(o, r, "war"):
                        deps[id(r)] = r
                if x.w is not None and self._need(o, x.w, "waw"):
                    deps[id(x.w)] = x.w
        for b in reads:
            for x in ([b] + b.over) if b.legacy else [b]:
                x.r[id(o.sig)] = o
        for b in writes:
            for x in ([b] + b.over) if b.legacy else [b]:
                x.w = o
                x.r = {}
        o.deps = list(deps.values())
        self.ops.append(o)
        return o

    def lower(self):
        for o in self.ops:
            for d in o.deps:
                d.inc = True
        for o in self.ops:
            if o.inc:
                o.sig.n += o.sig.unit
                o.val = o.sig.n
        seen = {k: {} for k in self.h}
        nwait = 0
        for o in self.ops:
            hd = self.h[o.eng]
            sn = seen[o.eng]
            best = {}
            for d in o.deps:
                k = id(d.sig)
                if sn.get(k, 0) < d.val and best.get(k, (0, None))[0] < d.val:
                    best[k] = (d.val, d.sig)
            for k, (v, s) in best.items():
                hd.wait_ge(s.sem, v)
                sn[k] = v
                nwait += 1
            ins = getattr(hd, o.meth)(**o.kw)
            if o.inc:
                ins.then_inc(o.sig.sem, o.sig.unit)
        for s in self.dsigs:
            if s.n > 0:
                self.nc.sync.wait_ge(s.sem, s.n)
        return len(self.ops), nwait


class LB:
    def __init__(self, ap, buf):
        self.ap, self.buf = ap, buf

    def __getitem__(self, k):
        return self.ap[k]


def build(seqs, ntok, debug=False):
    nc = bass.Bass("TRN2", target_bir_lowering=False)
    es = ExitStack()
    nseq = len(seqs)
    maxkeys = max(L + npast for (_, L, _, npast) in seqs)
    maxkeys = ((maxkeys + 511) // 512) * 512

    def din(name, shape, dt=F32):
        return nc.dram_tensor(name, list(shape), dt, kind="ExternalInput").ap()

    def dout(name, shape, dt=F32):
        return nc.dram_tensor(name, list(shape), dt, kind="ExternalOutput").ap()

    xs = din("xs", [ntok, D])
    wcat = din("wcat", [NUNIT, 128, 8, 512])
    wuqn_d = din("wuqn", [128, 3, 1024])
    wuqr_d = din("wuqr", [128, 3, 512])
    wukvk_d = din("wukvk", [128, 2, 1024])
    wukvv_d = din("wukvv", [128, 2, 1024])
    wq_d = din("wq", [128, 4, 2, 256])
    wk_d = din("wk", [128, 4, 2, 256])
    cf_d = din("cf", [128, 512])
    cb_d = din("cb", [128, 448])
    bc_d = din("bc", [128, 1024 + 256 + 64 + 8])
    cosT_d = din("cosT", [64, NPOS])
    sinT_d = din("sinT", [64, NPOS])
    cstm_d = din("cstm", [NPOS, 64])
    sntm_d = din("sntm", [NPOS, 64])
    cckv_d = din("cckv", [1024, KVL])
    ckr_d = din("ckr", [1024, ROPE])
    sconv_d = din("sconv", [3, D])
    sC_d = din("sC", [HM, DH, DH])
    sn_d = din("sn", [HM, DH])
    sm_d = din("sm", [HM, 1])

    y_d = dout("y", [ntok, D])
    ckv_o = dout("ckv_o", [ntok, KVL])
    kr_o = dout("kr_o", [ntok, ROPE])
    conv_o = dout("conv_o", [nseq, 3, D])
    C_o = dout("C_o", [nseq, HM, DH, DH])
    n_o = dout("n_o", [nseq, HM, DH])
    m_o = dout("m_o", [nseq, HM, 1])

    if debug:
        dbg_d = {n: nc.dram_tensor("dbg_" + n, [128, 8, 512], BF16, kind="ExternalOutput").ap() for n in ("hm", "ha", "mg", "qn", "kn", "hT")}
    wbf = nc.dram_tensor("wbf", [NUNIT, 128, 8 * 512], BF16, kind="Internal").ap()
    cckv_bf = nc.dram_tensor("cckv_bf", [1024, KVL], BF16, kind="Internal").ap()
    ckr_bf = nc.dram_tensor("ckr_bf", [1024, ROPE], BF16, kind="Internal").ap()
    kscr = nc.dram_tensor("kscr", [HA, 128, maxkeys], BF16, kind="Internal").ap()
    vscr = nc.dram_tensor("vscr", [HA, 128, maxkeys // 128, 128], BF16, kind="Internal").ap()
    rscr = nc.dram_tensor("rscr", [128, maxkeys], BF16, kind="Internal").ap()

    P = Prog(nc, es)
    dbg_sig = P.dma_sig("dbg") if debug else None

    def sbt(name, shape, dt):
        return es.enter_context(nc.sbuf_tensor(name, list(shape), dt))

    def pst(name, shape, dt):
        return es.enter_context(nc.psum_tensor(name, list(shape), dt))

    def pers(name, shape, dt):
        t = sbt(name, shape, dt)
        return LB(t[:], P.buf(name))

    ARENA_BYTES = 68 * 1024
    arena = sbt("arena", [128, ARENA_BYTES // 2], BF16)
    aoff = {}

    aofs = {}

    def ar(phase, name, shape, dt, at=None):
        nb = int(np.prod(shape[1:])) * (4 if dt == F32 else 2)
        nb = (nb + 63) // 64 * 64
        o = aoff.get(phase, 0) if at is None else at
        assert o + nb <= ARENA_BYTES, (phase, name, o + nb)
        if at is None:
            aoff[phase] = o + nb
        aofs[name] = o
        ap = arena[0:shape[0], o // 2:(o + nb) // 2]
        n_el = int(np.prod(shape[1:]))
        if dt == F32:
            ap = ap.bitcast(F32)[:, 0:n_el]
        else:
            ap = ap[:, 0:n_el]
        if len(shape) > 2:
            names = " ".join("d%d" % i for i in range(1, len(shape)))
            ap = ap.rearrange("p (%s) -> p %s" % (names, names), **{"d%d" % i: shape[i] for i in range(2, len(shape))})
        return LB(ap, P.buf(phase + "." + name, ("arena", o, o + nb)))

    pb = []
    for i in range(6):
        t = pst("pb%d" % i, [128, 512], F32)
        pb.append(LB(t[:], P.buf("pb%d" % i)))
        pb[-1].buf.psum = True
    t = pst("ptr", [128, 1024], BF16)
    ptr = LB(t[:], P.buf("ptr"))
    ptr.buf.psum = True
    t = pst("pmisc", [128, 512], F32)
    pmisc = LB(t[:], P.buf("pmisc"))
    pmisc.buf.psum = True
    pms = [LB(pb[k].ap[:, 0:64], pb[k].buf) for k in range(6)]

    ring = [pers("ring%d" % i, [128, 8, 512], BF16) for i in range(RING)]
    ring_sig = [P.dma_sig("ring%d" % i) for i in range(RING)]
    xin = [pers("xin%d" % i, [128, D], F32) for i in range(2)]
    xin_sig = [P.dma_sig("xin%d" % i) for i in range(2)]
    xn = pers("xn", [128, D], BF16)
    junk = pers("junk", [128, D], BF16)
    hT = pers("hT", [128, 8, 512], BF16)
    hmT = pers("hmT", [128, 8, 512], BF16)
    haT = pers("haT", [128, 8, 512], BF16)
    Cst_t = sbt("Cst", [128, HM, 2, 257], F32)
    Cst = LB(Cst_t[:], None)
    CstH = [LB(Cst_t[:, h], P.buf("Cst%d" % h)) for h in range(HM)]
    Cbf = [[pers("Cbf%d_%d" % (h, dc), [128, 257], BF16) for dc in range(2)] for h in range(HM)]
    hist = pers("hist", [128, 8, 3], F32)
    wuqn = pers("wuqn_s", [128, 3, 1024], BF16)
    wuqr = pers("wuqr_s", [128, 3, 512], BF16)
    wukvk = pers("wukvk_s", [128, 2, 1024], BF16)
    wukvv = pers("wukvv_s", [128, 2, 1024], BF16)
    wq = pers("wq_s", [128, 4, 2, 256], BF16)
    wk = pers("wk_s", [128, 4, 2, 256], BF16)
    cf = pers("cf_s", [128, 512], F32)
    cb = pers("cb_s", [128, 448], BF16)
    bc = pers("bc_s", [128, 1024 + 256 + 64 + 8], F32)
    cosT = pers("cosT_s", [64, 512], F32)
    sinT = pers("sinT_s", [64, 512], F32)
    cstm = pers("cstm_s", [128, 4, 64], F32)
    sntm = pers("sntm_s", [128, 4, 64], F32)
    tab_sig = P.dma_sig("tab")
    small = pers("small", [128, 64], F32)
    smallb = [P.buf("small%d" % i) for i in range(8)]
    gcol = pers("gcol", [128, 4, 16], F32)
    grow = pers("grow", [4, 2, 512], F32)
    mall = pers("mall", [4, 40], F32)
    decbd = pers("decbd", [4, 8, 4], F32)
    decbc = pers("decbc", [128, 8, 4], F32)
    wkc = pers("wkc", [128, 4, 12], F32)
    init_sig = P.dma_sig("init")
    sigC = P.dma_sig("stC")
    sigH = P.dma_sig("stH")
    sigM = P.dma_sig("stM")

    identf = cf[:, 0:128]
    tri = cf[:, 128:256]
    cmask = cf[:, 256:320]
    ng_c = cf[:, 320:328]
    cw_c = cf[:, 328:360]
    cbias_c = cf[:, 360:368]
    qng_c = cf[:, 368:371]
    gqn_c = cf[:, 371:372]
    gkn_c = cf[:, 372:373]
    gqr_c = cf[:, 373:374]
    eye4 = cf[0:4, 0:4]
    ones4 = cf[0:4, 384:512]
    identb = cb[:, 0:128]
    onesb = cb[:, 128:256]
    RTb = cb[0:64, 256:320]
    hng_bc = bc[:, 0:1024]
    kvng_bc = bc[:, 1024:1280]
    gkr_bc = bc[:, 1280:1344]
    bif_bc = bc[:, 1344:1352]

    def bl(xs_):
        return [x.buf if isinstance(x, LB) else x for x in xs_]

    def MM(out, lhsT, rhs, start, stop, R, W):
        P.op("pe", "matmul", dict(out=out, lhsT=lhsT, rhs=rhs, start=start, stop=stop), bl(R), bl(W))

    def TR(out, in_, ident, R, W):
        P.op("pe", "transpose", dict(out=out, in_=in_, identity=ident), bl(R), bl(W))

    def ACT(out, in_, func, R, W, **kw):
        P.op("act", "activation", dict(out=out, in_=in_, func=func, **kw), bl(R), bl(W))

    def TT(out, in0, in1, op, R, W, eng="dve"):
        P.op(eng, "tensor_tensor", dict(out=out, in0=in0, in1=in1, op=op), bl(R), bl(W))

    def TS(out, in0, s1, s2, op0, op1, R, W, eng="dve"):
        P.op(eng, "tensor_scalar", dict(out=out, in0=in0, scalar1=s1, scalar2=s2, op0=op0, op1=op1), bl(R), bl(W))

    def TSA(out, in0, s1, R, W, eng="dve"):
        P.op(eng, "tensor_scalar_add", dict(out=out, in0=in0, scalar1=s1), bl(R), bl(W))

    def TSM(out, in0, s1, R, W, eng="dve"):
        P.op(eng, "tensor_scalar_mul", dict(out=out, in0=in0, scalar1=s1), bl(R), bl(W))

    def STT(out, in0, scalar, in1, op0, op1, R, W, eng="dve"):
        P.op(eng, "scalar_tensor_tensor", dict(out=out, in0=in0, scalar=scalar, in1=in1, op0=op0, op1=op1), bl(R), bl(W))

    def CP(out, in_, R, W, eng="dve"):
        P.op(eng, "tensor_copy", dict(out=out, in_=in_), bl(R), bl(W))

    def RCP(out, in_, R, W):
        P.op("dve", "reciprocal", dict(out=out, in_=in_), bl(R), bl(W))

    def MEMSET(ap, v, W, eng="dve"):
        P.op(eng, "memset", dict(ap=ap, constant=v), [], bl(W))

    def DMA(q, out, in_, R, W, sig, **kw):
        P.op(q, "dma_start", dict(out=out, in_=in_, **kw), bl(R), bl(W), sig=sig)

    def rsqrt_act(out, in_, scale, R, W, tmpbuf):
        ACT(out, in_, AF.Ln, R, W, scale=scale, bias=eps_ap(out))
        ACT(out, out, AF.Exp, W, W, scale=-0.5)

    epsb = P.buf("epsb")

    def eps_ap(like):
        np_ = like.shape[0]
        p0 = like.base_partition() if hasattr(like, "base_partition") else 0
        return cf[p0:p0 + np_, 374:375]

    init_sig2 = P.dma_sig("init2")
    DMA("sp", cf.ap, cf_d, [], [cf], init_sig2)
    DMA("sp", bc.ap, bc_d, [], [bc], init_sig2)
    for b_ in (cf, bc):
        b_.buf.w = P.ops[-1]
    DMA("pool", cb.ap, cb_d, [], [cb], init_sig)
    DMA("pool", wuqn.ap, wuqn_d, [], [wuqn], init_sig)
    DMA("pool", wuqr.ap, wuqr_d, [], [wuqr], init_sig)
    DMA("pool", wukvk.ap, wukvk_d, [], [wukvk], init_sig)
    DMA("pool", wukvv.ap, wukvv_d, [], [wukvv], init_sig)
    DMA("pool", wq.ap, wq_d, [], [wq], init_sig)
    DMA("pool", wk.ap, wk_d, [], [wk], init_sig)
    for b_ in (cb, wuqn, wuqr, wukvk, wukvv, wq, wk):
        b_.buf.w = P.ops[-1]

    wbf_sig = P.dma_sig("wbf")
    wbfB = P.buf("wbfB")
    wbfA_sig = P.dma_sig("wbfA")
    wbfA = P.buf("wbfA")
    NEARLY = 4
    for k_ in range(NUNIT):
        DMA("pool", wbf[k_], wcat[k_].rearrange("p k c -> p (k c)"), [], [wbfA if k_ < NEARLY else wbfB],
            wbfA_sig if k_ < NEARLY else wbf_sig)
    DMA("pool", cckv_bf, cckv_d, [], [wbfB], wbf_sig)
    DMA("pool", ckr_bf, ckr_d, [], [wbfB], wbf_sig)
    ring_state = {"issued": 0, "total": 0}

    def ring_prefetch(upto):
        while ring_state["issued"] < min(upto + 1, ring_state["total"]):
            k = ring_state["issued"]
            s = k % RING
            DMA("sp", ring[s].ap, wbf[k % NUNIT].rearrange("p (k c) -> p k c", k=8),
                [wbfA if (k % NUNIT) < NEARLY else wbfB], [ring[s]], ring_sig[s])
            ring_state["issued"] += 1

    def unit(k, oldest=None):
        ring_prefetch((k if oldest is None else oldest) + RING - 1)
        return ring[k % RING]

    uhT = ar("A", "uhT", [128, 8, 512], BF16)
    qTm = ar("A", "qTm", [128, 4, 2, 512], BF16)
    kTm = ar("A", "kTm", [128, 4, 2, 512], BF16)
    kw = ar("A", "kw", [128, 4, 4, 256], BF16)
    gmg = ar("A", "gmg", [128, 4, 1024], BF16)
    xcb = [ar("A", "xcb%d" % i, [128, 515], F32) for i in range(2)]
    ucv = [ar("A", "ucv%d" % i, [128, 512], F32) for i in range(2)]
    ta = [ar("A", "ta%d" % i, [128, 512], F32) for i in range(3)]
    _o = aofs["xcb0"]
    hraw = [ar("A", "hraw%d" % i, [128, 4, 257], F32, at=_o + i * 4160) for i in range(2)]
    hh = ar("A", "hh", [128, 4, 256], F32, at=_o + 8320)
    hmg = ar("A", "hmg", [128, 1024], BF16, at=_o + 8320 + 4096)
    assert _o + 8320 + 4096 + 2048 <= aoff["A"]
    qnT = ar("C", "qnT", [128, 8, 512], BF16)
    qrT = ar("C", "qrT", [128, 8, 512], BF16)
    knT = ar("C", "knT", [128, 8, 512], BF16)
    Vcur = ar("C", "Vcur", [128, 4, 1024], BF16)
    cqf = ar("C", "cqf", [128, 3, 512], F32)
    cqn = ar("C", "cqn", [128, 3, 512], BF16)
    sqb = [ar("C", "sqb%d" % i, [128, 512], BF16) for i in range(3)]
    rr = [ar("C", "rr%d" % i, [128, 512], F32) for i in range(2)]
    tc_ = [ar("C", "tc%d" % i, [128, 512], F32) for i in range(2)]
    xrb = ar("C", "xrb", [64, 512], BF16)
    ckvnT = pers("ckvnT", [128, 2, 512], BF16)
    krT = pers("krT", [128, 512], BF16)
    ckvf = [ar("A", "ckvf%d" % i, [128, 256], F32) for i in range(2)]
    ckvb = ar("A", "ckvb", [128, 256], BF16)
    krf = [ar("A", "krf%d" % i, [128, 64], F32) for i in range(2)]
    krw = [ar("A", "krw%d" % i, [128, 64], F32) for i in range(3)]
    krb = ar("A", "krb", [128, 64], BF16)
    kvs = [dict(K=ar("C", "kvK%d" % i, [128, 512], BF16), R=ar("C", "kvR%d" % i, [128, 512], BF16),
                V=ar("C", "kvV%d" % i, [128, 4, 128], BF16)) for i in range(NKVS)]
    kvs_sig = [P.dma_sig("kvs%d" % i) for i in range(NKVS)]
    _k0 = aofs["kvK0"]
    qsq = [sqb[0], sqb[1], sqb[2], ar("C", "qsq3", [128, 512], BF16, at=_k0)]
    qrr = [rr[0], rr[1], ar("C", "qrr2", [128, 512], F32, at=_k0 + 1024), ar("C", "qrr3", [128, 512], F32, at=_k0 + 3072)]
    qxr = [xrb, ar("C", "qxr1", [64, 512], BF16, at=_k0 + 5120)]
    qtc = [tc_[0], tc_[1], ar("C", "qtc2", [128, 512], F32, at=_k0 + 6144), ar("C", "qtc3", [128, 512], F32, at=_k0 + 8192)]
    assert _k0 + 10240 <= aofs["kvK0"] + NKVS * 3072
    Pt = [pers("Pt%d" % i, [128, 512], BF16) for i in range(3)]
    Pacc = [pers("Pacc%d" % i, [128, 512], F32) for i in range(2)]
    pS3 = [pb[2], pb[3], LB(ptr.ap.bitcast(F32), ptr.buf)]
    swA = ar("C", "swA", [128, 4, 4, 64], BF16)
    pcb = ar("C", "pcb", [128, 256], BF16)
    pcr = ar("C", "pcr", [128, 64], BF16)
    sgm = ar("D", "sgm", [128, 4, 512], F32)
    sga = ar("D", "sga", [128, 4, 512], F32)
    tmpm = ar("D", "tmpm", [128, 4, 512], F32)
    td = [ar("D", "td%d" % i, [128, 512], F32) for i in range(2)]
    mergedT = ar("D", "mergedT", [128, 8, 512], BF16)
    yout = [ar("D", "yout%d" % i, [128, 1024], F32) for i in range(2)]
    xres = [ar("D", "xres%d" % i, [128, D], F32) for i in range(4)]
    xres_sig = [P.dma_sig("xres%d" % i) for i in range(4)]
    v_ext = pers("v_ext", [128, 4, 4, 257], BF16)
    Pd = [pers("Pd%d" % i, [128, 512], BF16) for i in range(4)]

    scr_sig = {k: P.dma_sig("scrw" + k) for k in "KRV"}
    ckvo_sig = [P.dma_sig("ckvo%d" % i) for i in range(2)]
    kro_sig = [P.dma_sig("kro%d" % i) for i in range(2)]
    yo_sig = [P.dma_sig("yo%d" % i) for i in range(2)]
    pcb_sig = P.dma_sig("pcb")
    pcr_sig = P.dma_sig("pcr")
    kscr_b = {}

    def scr_buf(si, kb):
        key = ("scr", kb)
        if key not in kscr_b:
            kscr_b[key] = {k: P.buf("scr%s_%d" % (k, kb)) for k in "KRV"}
        return kscr_b[key]

    MEMSET(v_ext[:, :, :, 256:257], 1.0, [v_ext])
    MEMSET(krT[64:128, :], 0.0, [krT])
    for i in range(4):
        MEMSET(Pd[i].ap, 0.0, [Pd[i]])

    def kv_block(si, kb, NK, to_scratch):
        TSk = min(NK, 128)
        for h in range(HA):
            pk = pb[h % 2]
            for c2 in range(2):
                MM(pk[:, 0:NK], wukvk[:, c2, h * 128:(h + 1) * 128], ckvnT[:, c2, 0:NK], c2 == 0, c2 == 1,
                   [wukvk, ckvnT], [pk])
            sq = sqb[h % 2]
            ACT(sq[:, 0:NK], pk[:, 0:NK], AF.Square, [pk], [sq])
            ps = pb[2 + h % 2]
            MM(ps[:, 0:NK], onesb, sq[:, 0:NK], True, True, [cb, sq], [ps])
            r = rr[h % 2]
            rsqrt_act(r[:, 0:NK], ps[:, 0:NK], 1.0 / NOPE, [ps, cf], [r], None)
            STT(knT[:, h, 0:NK], pk[:, 0:NK], gkn_c, r[:, 0:NK], ALU.mult, ALU.mult, [pk, r, cf], [knT])
        for jt in range((NK + 127) // 128):
            for g in range(2):
                pv = pb[4 + g]
                for c2 in range(2):
                    MM(pv[0:TSk, :], ckvnT[:, c2, jt * 128:jt * 128 + TSk], wukvv[:, c2, g * 512:(g + 1) * 512],
                       c2 == 0, c2 == 1, [ckvnT, wukvv], [pv])
                ACT(Vcur[0:TSk, jt, g * 512:(g + 1) * 512], pv[0:TSk, :], AF.Copy, [pv], [Vcur])
        if to_scratch:
            sb_ = scr_buf(si, kb)
            k0 = kb * 512
            DMA("sp", kscr[:, :, k0:k0 + NK].rearrange("h d k -> d h k"), knT[:, :, 0:NK], [knT], [sb_["K"]], scr_sig["K"])
            DMA("sp", rscr[:, k0:k0 + NK], krT[:, 0:NK], [krT], [sb_["R"]], scr_sig["R"])
            nt = NK // 128
            for t_ in range(nt):
                DMA("sp", vscr[:, :, kb * 4 + t_, :].rearrange("h p d -> p h d"),
                    Vcur[:, t_, :].rearrange("p (h d) -> p h d", h=HA), [Vcur], [sb_["V"]], scr_sig["V"])

    def attention(si, T, nkb_past, ucount):
        TSq = min(T, 128)
        ntl = (T + 127) // 128
        loads = [(h_, kb_) for h_ in range(HA) for kb_ in range(nkb_past)]
        nload = [0]

        def issue_load():
            li = nload[0]
            if li >= len(loads):
                return
            nload[0] += 1
            h_, kb_ = loads[li]
            s_ = li % NKVS
            sl = kvs[s_]
            slb = [sl["K"], sl["R"], sl["V"]]
            sb_ = scr_buf(si, kb_)
            k0 = kb_ * 512
            DMA("sp", sl["K"].ap, kscr[h_, :, k0:k0 + 512], list(sb_.values()), slb, kvs_sig[s_])
            DMA("sp", sl["R"].ap, rscr[:, k0:k0 + 512], list(sb_.values()), slb, kvs_sig[s_])
            DMA("sp", sl["V"].ap, vscr[h_, :, kb_ * 4:kb_ * 4 + 4, :], list(sb_.values()), slb, kvs_sig[s_])

        for _ in range(NKVS - 1):
            issue_load()
        pending = [None]
        for h in range(HA):
            po, pden = (pb[4], pb[5]) if h % 2 == 0 else (pb[0], pb[1])
            tasks = []
            for kb in range(nkb_past):
                li = h * nkb_past + kb
                sl = kvs[li % NKVS]
                slb = [sl["K"], sl["R"], sl["V"]]
                for jt in range(4):
                    tasks.append(("past", sl, slb, jt))
            for jt in range(ntl):
                tasks.append(("diag", None, None, jt))
            nt_ = len(tasks)

            pacc = Pacc[h % 2]

            def emit_S(i):
                kind_, sl, slb, jt = tasks[i]
                pS = pS3[i % 3]
                if kind_ == "past":
                    MM(pS[:, 0:T], sl["K"][:, jt * 128:(jt + 1) * 128], qnT[:, h, 0:T], True, False, slb + [qnT], [pS])
                    MM(pS[:, 0:T], sl["R"][:, jt * 128:(jt + 1) * 128], qrT[:, h, 0:T], False, True,
                       slb + [qrT], [pS])
                else:
                    c0 = jt * 128
                    MM(pS[0:TSq, c0:T], knT[:, h, c0:c0 + TSq], qnT[:, h, c0:T], True, False, [knT, qnT], [pS])
                    MM(pS[0:TSq, c0:T], krT[:, c0:c0 + TSq], qrT[:, h, c0:T], False, True, [krT, qrT], [pS])

            def emit_rest(i):
                kind_, sl, slb, jt = tasks[i]
                pS = pS3[i % 3]
                first, last = i == 0, i == nt_ - 1
                if kind_ == "past":
                    pt_ = Pt[i % 3]
                    ACT(pt_[:, 0:T], pS[:, 0:T], AF.Exp, [pS], [pt_], scale=ATT_SCALE)
                    MM(po[:, 0:T], sl["V"][:, jt, :], pt_[:, 0:T], first, last, slb + [pt_], [po])
                    if first:
                        CP(pacc[:, 0:T], pt_[:, 0:T], [pt_], [pacc])
                    else:
                        TT(pacc[:, 0:T], pacc[:, 0:T], pt_[:, 0:T], ALU.add, [pacc, pt_], [pacc])
                else:
                    c0 = jt * 128
                    pd_ = Pd[jt]
                    ACT(pd_[0:64, c0:T], pS[0:64, c0:T], AF.Exp, [pS], [pd_], scale=ATT_SCALE)
                    if TSq == 128:
                        ACT(pd_[64:128, c0 + 64:T], pS[64:128, c0 + 64:T], AF.Exp, [pS], [pd_], scale=ATT_SCALE)
                    MM(po[:, c0:T], Vcur[0:TSq, jt, h * 128:(h + 1) * 128], pd_[0:TSq, c0:T], first, last,
                       [Vcur, pd_], [po])
                    if first:
                        if TSq < 128:
                            MEMSET(pacc[:, 0:T], 0.0, [pacc], eng="dve")
                        CP(pacc[0:TSq, 0:T], pd_[0:TSq, 0:T], [pd_], [pacc])
                    else:
                        TT(pacc[0:TSq, c0:T], pacc[0:TSq, c0:T], pd_[0:TSq, c0:T], ALU.add, [pacc, pd_], [pacc])

            emit_S(0)
            if nt_ > 1:
                emit_S(1)
            if pending[0] is not None:
                pending[0]()
                pending[0] = None
            for i in range(nt_):
                if i + 2 < nt_:
                    emit_S(i + 2)
                emit_rest(i)
                if tasks[i][0] == "past" and tasks[i][3] == 1:
                    issue_load()
            def fin(h=h, po=po, pden=pden, pacc=pacc):
                hi_, lo_ = sqb[0], sqb[1]
                CP(hi_[:, 0:T], pacc[:, 0:T], [pacc], [hi_])
                TT(lo_[:, 0:T], pacc[:, 0:T], hi_[:, 0:T], ALU.subtract, [pacc, hi_], [lo_])
                MM(pden[:, 0:T], onesb, hi_[:, 0:T], True, False, [cb, hi_], [pden])
                MM(pden[:, 0:T], onesb, lo_[:, 0:T], False, True, [cb, lo_], [pden])
                u = unit(ucount + 10 + h // 4)
                pz = pmisc
                for kc in range(8):
                    MM(pz[:, 0:T], u[:, kc, (h % 4) * 128:(h % 4 + 1) * 128], hT[:, kc, 0:T], kc == 0, kc == 7,
                       [u, hT], [pz])
                e, zf, d1, n1 = tc_[0], tc_[1], rr[0], rr[1]
                ACT(e[:, 0:T], pz[:, 0:T], AF.Exp, [pz], [e], scale=-1.0)
                ACT(zf[:, 0:T], pz[:, 0:T], AF.Copy, [pz], [zf])
                STT(d1[:, 0:T], e[:, 0:T], 1.0, pden[:, 0:T], ALU.add, ALU.mult, [e, pden], [d1])
                ACT(d1[:, 0:T], d1[:, 0:T], AF.Ln, [d1], [d1])
                ACT(d1[:, 0:T], d1[:, 0:T], AF.Exp, [d1], [d1], scale=-1.0)
                TT(n1[:, 0:T], po[:, 0:T], zf[:, 0:T], ALU.mult, [po, zf], [n1])
                TT(haT[:, h, 0:T], n1[:, 0:T], d1[:, 0:T], ALU.mult, [n1, d1], [haT])
            pending[0] = fin
        if pending[0] is not None:
            pending[0]()
            pending[0] = None

    def block(si, kind, row0, T, pos0, nkb_past, kb_cur, ucount, last_block, write_scratch):
        TSz = min(T, 128)
        ntl = (T + 127) // 128
        NCH = T // 64
        DMA("sp", cosT[:, 0:T], cosT_d[:, pos0:pos0 + T], [], [cosT], tab_sig)
        DMA("sp", sinT[:, 0:T], sinT_d[:, pos0:pos0 + T], [], [cosT], tab_sig)
        for j in range(ntl):
            DMA("sp", cstm[0:TSz, j, :], cstm_d[pos0 + j * 128:pos0 + j * 128 + TSz, :], [], [cosT], tab_sig)
            DMA("sp", sntm[0:TSz, j, :], sntm_d[pos0 + j * 128:pos0 + j * 128 + TSz, :], [], [cosT], tab_sig)
        for j in range(ntl):
            xi = xin[j % 2]
            r0 = row0 + j * 128
            DMA("sp", xi[0:TSz, :], xs[r0:r0 + TSz, :], [], [xi], xin_sig[j % 2])
            sx = small[0:TSz, 0:1]
            ACT(junk[0:TSz, :], xi[0:TSz, :], AF.Square, [xi], [junk, smallb[0]], accum_out=sx)
            rsqrt_act(sx, sx, 1.0 / D, [smallb[0], cf], [smallb[0]], None)
            ACT(xn[0:TSz, :], xi[0:TSz, :], AF.Copy, [xi, smallb[0]], [xn], scale=sx)
            for kc in range(8):
                TR(ptr[:, kc * 128:kc * 128 + TSz], xn[0:TSz, kc * 128:(kc + 1) * 128], identb[0:TSz, 0:TSz],
                   [xn, cb], [ptr])
            TT(hT[:, :, j * 128:j * 128 + TSz], ptr.ap.rearrange("p (k t) -> p k t", k=8)[:, :, 0:TSz],
               ng_c.unsqueeze(2).to_broadcast([128, 8, TSz]), ALU.mult, [ptr, cf], [hT])

        for cc in range(8):
            u = unit(ucount + cc // 4)
            pc = pb[cc % 2]
            for kc in range(8):
                MM(pc[:, 0:T], u[:, kc, (cc % 4) * 128:(cc % 4 + 1) * 128], hT[:, kc, 0:T], kc == 0, kc == 7,
                   [u, hT], [pc])
            xb_ = xcb[cc % 2]
            uc = ucv[cc % 2]
            CP(xb_[:, 0:3], hist[:, cc, :], [hist], [xb_])
            ACT(xb_[:, 3:3 + T], pc[:, 0:T], AF.Copy, [pc], [xb_])
            CP(hist[:, cc, :], xb_[:, T:T + 3], [xb_], [hist])
            TS(uc[:, 0:T], xb_[:, 0:T], cw_c[:, cc * 4:cc * 4 + 1], cbias_c[:, cc:cc + 1], ALU.mult, ALU.add,
               [xb_, cf], [uc])
            for jj in range(1, 4):
                STT(uc[:, 0:T], xb_[:, jj:jj + T], cw_c[:, cc * 4 + jj:cc * 4 + jj + 1], uc[:, 0:T], ALU.mult, ALU.add,
                    [xb_, uc, cf], [uc])
            e = ta[cc % 2]
            ACT(e[:, 0:T], uc[:, 0:T], AF.Exp, [uc], [e], scale=-1.0)
            ACT(e[:, 0:T], e[:, 0:T], AF.Ln, [e], [e], bias=1.0)
            ACT(e[:, 0:T], e[:, 0:T], AF.Exp, [e], [e], scale=-1.0)
            TT(uhT[:, cc, 0:T], uc[:, 0:T], e[:, 0:T], ALU.mult, [uc, e], [uhT])

        for g in range(2):
            u = unit(ucount + 2 + g)
            for j in range(ntl):
                pv = pb[(g * ntl + j) % 2]
                for kc in range(8):
                    MM(pv[0:TSz, :], hT[:, kc, j * 128:j * 128 + TSz], u[:, kc, :], kc == 0, kc == 7, [hT, u], [pv])
                ACT(v_ext[0:TSz, j, 2 * g:2 * g + 2, 0:256], pv[0:TSz, :].rearrange("p (h e) -> p h e", h=2), AF.Copy,
                    [pv], [v_ext])
        for g in range(2):
            uo = unit(ucount + 4 + 2 * g)
            uz = unit(ucount + 5 + 2 * g, oldest=ucount + 4 + 2 * g)
            for j in range(ntl):
                po_, pz_ = (pb[0], pb[1]) if (g * ntl + j) % 2 == 0 else (pb[2], pb[3])
                for kc in range(8):
                    MM(po_[0:TSz, :], hT[:, kc, j * 128:j * 128 + TSz], uo[:, kc, :], kc == 0, kc == 7, [hT, uo], [po_])
                for kc in range(8):
                    MM(pz_[0:TSz, :], hT[:, kc, j * 128:j * 128 + TSz], uz[:, kc, :], kc == 0, kc == 7, [hT, uz], [pz_])
                eo, ez, g1 = ta[0], ta[1], ta[2]
                ACT(eo[0:TSz, :], po_[0:TSz, :], AF.Exp, [po_], [eo], scale=-1.0)
                ACT(ez[0:TSz, :], pz_[0:TSz, :], AF.Exp, [pz_], [ez], scale=-1.0)
                TT(g1[0:TSz, :], pz_[0:TSz, :], hng_bc[0:TSz, g * 512:(g + 1) * 512], ALU.mult, [pz_, bc], [g1])
                ACT(eo[0:TSz, :], eo[0:TSz, :], AF.Ln, [eo], [eo], bias=1.0)
                ACT(ez[0:TSz, :], ez[0:TSz, :], AF.Ln, [ez], [ez], bias=1.0)
                TT(eo[0:TSz, :], eo[0:TSz, :], ez[0:TSz, :], ALU.add, [eo, ez], [eo])
                ACT(eo[0:TSz, :], eo[0:TSz, :], AF.Exp, [eo], [eo], scale=-1.0)
                TT(gmg[0:TSz, j, g * 512:(g + 1) * 512], g1[0:TSz, :], eo[0:TSz, :], ALU.mult, [g1, eo], [gmg])

        u8 = unit(ucount + 8)
        for j in range(ntl):
            pk = pb[j % 2]
            r0 = row0 + j * 128
            for kc in range(8):
                MM(pk[0:TSz, 0:328], hT[:, kc, j * 128:j * 128 + TSz], u8[:, kc, 0:328], kc == 0, kc == 7, [hT, u8], [pk])
            s1 = small[0:TSz, 1:2]
            ACT(junk[0:TSz, 0:256], pk[0:TSz, 0:256], AF.Square, [pk], [junk, smallb[1]], accum_out=s1)
            rsqrt_act(s1, s1, 1.0 / KVL, [smallb[1], cf], [smallb[1]], None)
            cf_ = ckvf[j % 2]
            STT(cf_[0:TSz, :], pk[0:TSz, 0:256], s1, kvng_bc[0:TSz, :], ALU.mult, ALU.mult, [pk, smallb[1], bc], [cf_])
            DMA("sp", ckv_o[r0:r0 + TSz, :], cf_[0:TSz, :], [cf_], [], ckvo_sig[j % 2])
            ACT(ckvb[0:TSz, :], cf_[0:TSz, :], AF.Copy, [cf_], [ckvb])
            for c2 in range(2):
                TR(ptr[:, c2 * 128:c2 * 128 + TSz], ckvb[0:TSz, c2 * 128:(c2 + 1) * 128], identb[0:TSz, 0:TSz],
                   [ckvb, cb], [ptr])
            CP(ckvnT[:, :, j * 128:j * 128 + TSz], ptr.ap.rearrange("p (k t) -> p k t", k=8)[:, 0:2, 0:TSz],
               [ptr], [ckvnT])
            s2 = small[0:TSz, 2:3]
            ACT(junk[0:TSz, 256:320], pk[0:TSz, 256:320], AF.Square, [pk], [junk, smallb[2]], accum_out=s2)
            rsqrt_act(s2, s2, 1.0 / ROPE, [smallb[2], cf], [smallb[2]], None)
            xr, xsw, tA = krw[0], krw[1], krw[2]
            STT(xr[0:TSz, :], pk[0:TSz, 256:320], s2, gkr_bc[0:TSz, :], ALU.mult, ALU.mult, [pk, smallb[2], bc], [xr])
            CP(xsw[0:TSz, 0:32], xr[0:TSz, 32:64], [xr], [xsw])
            CP(xsw[0:TSz, 32:64], xr[0:TSz, 0:32], [xr], [xsw])
            TT(tA[0:TSz, :], xr[0:TSz, :], cstm[0:TSz, j, :], ALU.mult, [xr, cosT], [tA])
            TT(xsw[0:TSz, :], xsw[0:TSz, :], sntm[0:TSz, j, :], ALU.mult, [xsw, cosT], [xsw])
            kf = krf[j % 2]
            TT(kf[0:TSz, :], tA[0:TSz, :], xsw[0:TSz, :], ALU.add, [tA, xsw], [kf])
            DMA("sp", kr_o[r0:r0 + TSz, :], kf[0:TSz, :], [kf], [], kro_sig[j % 2])
            ACT(krb[0:TSz, :], kf[0:TSz, :], AF.Copy, [kf], [krb])
            TR(ptr[0:64, 256:256 + TSz], krb[0:TSz, :], identb[0:TSz, 0:TSz], [krb, cb], [ptr])
            CP(krT[0:64, j * 128:j * 128 + TSz], ptr[0:64, 256:256 + TSz], [ptr], [krT])
            gb = smallb[3]
            TT(gcol[0:TSz, j, 0:8], pk[0:TSz, 320:328], bif_bc[0:TSz, :], ALU.add, [pk, bc], [gb])
            ACT(gcol[0:TSz, j, 8:12], gcol[0:TSz, j, 4:8], AF.Exp, [gb], [gb], scale=-1.0)
            ACT(gcol[0:TSz, j, 8:12], gcol[0:TSz, j, 8:12], AF.Ln, [gb], [gb], bias=1.0)
            MM(pmisc[0:TSz, 0:4], tri[0:TSz, 0:TSz], gcol[0:TSz, j, 8:12], True, True, [cf, gb], [pmisc])
            CP(gcol[0:TSz, j, 12:16], pmisc[0:TSz, 0:4], [pmisc], [gb])
            TT(gcol[0:TSz, j, 4:8], gcol[0:TSz, j, 0:4], gcol[0:TSz, j, 12:16], ALU.add, [gb], [gb])
            TR(pmisc[0:4, 128:128 + TSz], gcol[0:TSz, j, 4:8], identf[0:TSz, 0:TSz], [gb, cf], [pmisc])
            TR(pmisc[0:4, 256:256 + TSz], gcol[0:TSz, j, 12:16], identf[0:TSz, 0:TSz], [gb, cf], [pmisc])
            CP(grow[:, 0, j * 128:j * 128 + TSz], pmisc[0:4, 128:128 + TSz], [pmisc], [grow])
            CP(grow[:, 1, j * 128:j * 128 + TSz], pmisc[0:4, 256:256 + TSz], [pmisc], [grow])
        av = grow[:, 0, 0:T].rearrange("p (c t) -> p c t", t=64)
        cv = grow[:, 1, 0:T].rearrange("p (c t) -> p c t", t=64)
        mb = P.buf("mallb") if "mallb" not in kscr_b else kscr_b["mallb"]
        kscr_b["mallb"] = mb
        P.op("dve", "tensor_reduce", dict(out=mall[:, 10:10 + NCH], in_=av, axis=AX.X, op=ALU.max), bl([grow]), bl([mb]))
        TSM(mall[:, 20:20 + NCH], cv[:, :, 63], -1.0, [grow], [mb])
        P.op("dve", "tensor_tensor_scan", dict(out=mall[:, 1:1 + NCH], data0=mall[:, 10:10 + NCH], data1=mall[:, 20:20 + NCH],
                                               initial=mall[:, 0:1], op0=ALU.max, op1=ALU.add), bl([mb]), bl([mb]))
        TT(mall[:, 10:10 + NCH], mall[:, 1:1 + NCH], mall[:, 20:20 + NCH], ALU.subtract, [mb], [mb])
        TT(mall[:, 20:20 + NCH], mall[:, 0:NCH], mall[:, 10:10 + NCH], ALU.subtract, [mb], [mb])
        ACT(mall[:, 20:20 + NCH], mall[:, 20:20 + NCH], AF.Exp, [mb], [mb])
        Mb = mall[:, 10:10 + NCH].unsqueeze(2).to_broadcast([4, NCH, 64])
        TT(av, av, Mb, ALU.subtract, [grow, mb], [grow])
        TT(cv, cv, Mb, ALU.subtract, [grow, mb], [grow])
        ACT(grow[:, 0:2, 0:T], grow[:, 0:2, 0:T], AF.Exp, [grow], [grow])
        TT(decbd[:, 0:NCH, :], mall[:, 20:20 + NCH].unsqueeze(2).to_broadcast([4, NCH, 4]),
           eye4.unsqueeze(1).to_broadcast([4, NCH, 4]), ALU.mult, [mb, cf], [decbd])
        MM(pmisc[:, 0:NCH * 4], ones4, decbd[:, 0:NCH, :].rearrange("p c h -> p (c h)"), True, True, [cf, decbd], [pmisc])
        CP(decbc[:, 0:NCH, :].rearrange("p c h -> p (c h)"), pmisc[:, 0:NCH * 4], [pmisc], [decbc])
        for j in range(ntl):
            TR(pmisc[0:TSz, 384:388], grow[:, 0, j * 128:j * 128 + TSz], eye4, [grow, cf], [pmisc])
            TR(pmisc[0:TSz, 392:396], grow[:, 1, j * 128:j * 128 + TSz], eye4, [grow, cf], [pmisc])
            CP(wkc[0:TSz, j, 0:4], pmisc[0:TSz, 384:388], [pmisc], [wkc])
            ACT(wkc[0:TSz, j, 4:8], pmisc[0:TSz, 384:388], AF.Copy, [pmisc], [wkc], scale=1.0 / 16)
            CP(wkc[0:TSz, j, 8:12], pmisc[0:TSz, 392:396], [pmisc], [wkc])
        if last_block:
            DMA("sp", m_o[si], mall[:, NCH:NCH + 1], [mb], [], sigM)
        CP(mall[:, 0:1], mall[:, NCH:NCH + 1], [mb], [mb])

        for h in range(HM):
            for ec in range(2):
                pq, pk_ = pb[0], pb[1]
                for dc in range(2):
                    MM(pq[:, 0:T], wq[:, h, dc, ec * 128:(ec + 1) * 128], uhT[:, 2 * h + dc, 0:T], dc == 0, dc == 1,
                       [wq, uhT], [pq])
                for dc in range(2):
                    MM(pk_[:, 0:T], wk[:, h, dc, ec * 128:(ec + 1) * 128], uhT[:, 2 * h + dc, 0:T], dc == 0, dc == 1,
                       [wk, uhT], [pk_])
                CP(qTm[:, h, ec, 0:T], pq[:, 0:T], [pq], [qTm])
                ACT(kTm[:, h, ec, 0:T], pk_[:, 0:T], AF.Copy, [pk_], [kTm], scale=1.0 / 16)
            for j in range(ntl):
                pkt = pb[2 + j % 2]
                for dc in range(2):
                    MM(pkt[0:TSz, 0:256], uhT[:, 2 * h + dc, j * 128:j * 128 + TSz], wk[:, h, dc, :], dc == 0, dc == 1,
                       [uhT, wk], [pkt])
                ACT(kw[0:TSz, j, h, :], pkt[0:TSz, 0:256], AF.Copy, [pkt, wkc], [kw], scale=wkc[0:TSz, j, 4 + h:5 + h])

        for c in range(NCH):
            j, ph = c // 2, 64 * (c % 2)
            t0 = c * 64
            for h in range(HM):
                reg = pms[(c * HM + h) % 6]
                for dc in range(2):
                    MM(reg[ph:ph + 64, :], kTm[:, h, dc, t0:t0 + 64], qTm[:, h, dc, t0:t0 + 64], dc == 0, dc == 1,
                       [kTm, qTm], [reg])
                STT(swA[ph:ph + 64, j, h, :], reg[ph:ph + 64, :], wkc[ph:ph + 64, j, h:h + 1], cmask[ph:ph + 64, :],
                    ALU.mult, ALU.mult, [reg, wkc, cf], [swA])
        for c in range(NCH):
            j, ph = c // 2, 64 * (c % 2)
            t0 = c * 64
            hr = hraw[j % 2]
            for pair in ((0, 1), (2, 3)):
                for h in pair:
                    dec = decbc[:, c, h:h + 1]
                    for dc in range(2):
                        if dc == 0:
                            ACT(Cbf[h][dc].ap, CstH[h][:, dc, :], AF.Copy, [CstH[h], decbc], [Cbf[h][dc]], scale=dec)
                        else:
                            TSM(Cbf[h][dc].ap, CstH[h][:, dc, :], dec, [CstH[h], decbc], [Cbf[h][dc]])
                for i, h in enumerate(pair):
                    for dc in range(2):
                        pU = pb[2 * i + dc]
                        MM(pU[:, 0:257], kw[ph:ph + 64, j, h, dc * 128:(dc + 1) * 128], v_ext[ph:ph + 64, j, h, :],
                           True, True, [kw, v_ext], [pU])
                for i, h in enumerate(pair):
                    dec = decbc[:, c, h:h + 1]
                    for dc in range(2):
                        pU = pb[2 * i + dc]
                        STT(CstH[h][:, dc, :], CstH[h][:, dc, :], dec, pU[:, 0:257], ALU.mult, ALU.add,
                            [CstH[h], decbc, pU], [CstH[h]])
                for i, h in enumerate(pair):
                    pN = pb[4 + i]
                    for dc in range(2):
                        MM(pN[ph:ph + 64, 0:257], qTm[:, h, dc, t0:t0 + 64], Cbf[h][dc].ap, dc == 0, False,
                           [qTm, Cbf[h][dc]], [pN])
                    MM(pN[ph:ph + 64, 0:257], swA[ph:ph + 64, j, h, :], v_ext[ph:ph + 64, j, h, :], False, True,
                       [swA, v_ext], [pN])
                for i, h in enumerate(pair):
                    pN = pb[4 + i]
                    ACT(hr[ph:ph + 64, h, :], pN[ph:ph + 64, 0:257], AF.Copy, [pN], [hr])
            if c % 2 == 1 or c == NCH - 1:
                hb = smallb[4]
                den4 = hr[0:TSz, :, 256]
                STT(small[0:TSz, 8:12], den4, -1.0, den4, ALU.mult, ALU.max, [hr], [hb])
                TT(small[0:TSz, 8:12], small[0:TSz, 8:12], wkc[0:TSz, j, 8:12], ALU.max, [hb, wkc], [hb])
                RCP(small[0:TSz, 8:12], small[0:TSz, 8:12], [hb], [hb])
                TT(hh[0:TSz, :, :], hr[0:TSz, :, 0:256], small[0:TSz, 8:12].unsqueeze(2).to_broadcast([TSz, 4, 256]),
                   ALU.mult, [hr, hb], [hh])
                for h in range(HM):
                    ACT(junk[0:TSz, 0:256], hh[0:TSz, h, :], AF.Square, [hh], [junk, smallb[5]],
                        accum_out=small[0:TSz, 12 + h:13 + h])
                rsqrt_act(small[0:TSz, 12:16], small[0:TSz, 12:16], 1.0 / DH, [smallb[5], cf], [smallb[5]], None)
                for h in range(HM):
                    STT(hmg[0:TSz, h * 256:(h + 1) * 256], hh[0:TSz, h, :], small[0:TSz, 12 + h:13 + h],
                        gmg[0:TSz, j, h * 256:(h + 1) * 256], ALU.mult, ALU.mult, [hh, smallb[5], gmg], [hmg])
                for kc in range(8):
                    TR(ptr[:, kc * 128:kc * 128 + TSz], hmg[0:TSz, kc * 128:(kc + 1) * 128], identb[0:TSz, 0:TSz],
                       [hmg, cb], [ptr])
                ACT(hmT[:, :, j * 128:j * 128 + TSz], ptr.ap.rearrange("p (k t) -> p k t", k=8)[:, :, 0:TSz], AF.Copy,
                    [ptr], [hmT])

        u9 = unit(ucount + 9)
        for k3 in range(3):
            pq = pb[k3 % 2]
            for kc in range(8):
                MM(pq[:, 0:T], u9[:, kc, k3 * 128:(k3 + 1) * 128], hT[:, kc, 0:T], kc == 0, kc == 7, [u9, hT], [pq])
            ACT(cqf[:, k3, 0:T], pq[:, 0:T], AF.Copy, [pq], [cqf])
            ACT(sqb[k3][:, 0:T], pq[:, 0:T], AF.Square, [pq], [sqb[k3]])
        psq = pb[2]
        for k3 in range(3):
            MM(psq[:, 0:T], onesb, sqb[k3][:, 0:T], k3 == 0, k3 == 2, [cb, sqb[k3]], [psq])
        rsqrt_act(rr[0][:, 0:T], psq[:, 0:T], 1.0 / QL, [psq, cf], [rr[0]], None)
        for k3 in range(3):
            STT(cqn[:, k3, 0:T], cqf[:, k3, 0:T], qng_c[:, k3:k3 + 1], rr[0][:, 0:T], ALU.mult, ALU.mult,
                [cqf, rr[0], cf], [cqn])
        MEMSET(qrT[64:128, :, :], 0.0, [qrT], eng="dve")
        for h0 in range(0, HA, 2):
            pair = (h0, h0 + 1)
            for s_, h in enumerate(pair):
                pn, pr = pb[3 * s_], pb[3 * s_ + 1]
                for k3 in range(3):
                    MM(pn[:, 0:T], wuqn[:, k3, h * 128:(h + 1) * 128], cqn[:, k3, 0:T], k3 == 0, k3 == 2, [wuqn, cqn], [pn])
                for k3 in range(3):
                    MM(pr[0:64, 0:T], wuqr[:, k3, h * 64:(h + 1) * 64], cqn[:, k3, 0:T], k3 == 0, k3 == 2, [wuqr, cqn], [pr])
            for s_, h in enumerate(pair):
                pn, pr = pb[3 * s_], pb[3 * s_ + 1]
                ACT(qsq[2 * s_][:, 0:T], pn[:, 0:T], AF.Square, [pn], [qsq[2 * s_]])
                ACT(qsq[2 * s_ + 1][0:64, 0:T], pr[0:64, 0:T], AF.Square, [pr], [qsq[2 * s_ + 1]])
            for s_, h in enumerate(pair):
                ps1, ps2 = pb[3 * s_ + 2], (pmisc if s_ == 0 else pS3[2])
                MM(ps1[:, 0:T], onesb, qsq[2 * s_][:, 0:T], True, True, [cb, qsq[2 * s_]], [ps1])
                MM(ps2[0:64, 0:T], onesb[0:64, 0:64], qsq[2 * s_ + 1][0:64, 0:T], True, True, [cb, qsq[2 * s_ + 1]], [ps2])
            for s_, h in enumerate(pair):
                ps1, ps2 = pb[3 * s_ + 2], (pmisc if s_ == 0 else pS3[2])
                rsqrt_act(qrr[2 * s_][:, 0:T], ps1[:, 0:T], 1.0 / NOPE, [ps1, cf], [qrr[2 * s_]], None)
                rsqrt_act(qrr[2 * s_ + 1][0:64, 0:T], ps2[0:64, 0:T], 1.0 / ROPE, [ps2, cf], [qrr[2 * s_ + 1]], None)
            for s_, h in enumerate(pair):
                pn, pr = pb[3 * s_], pb[3 * s_ + 1]
                STT(qnT[:, h, 0:T], pn[:, 0:T], gqn_c, qrr[2 * s_][:, 0:T], ALU.mult, ALU.mult, [pn, qrr[2 * s_], cf], [qnT])
                STT(qxr[s_][:, 0:T], pr[0:64, 0:T], gqr_c[0:64, :], qrr[2 * s_ + 1][0:64, 0:T], ALU.mult, ALU.mult,
                    [pr, qrr[2 * s_ + 1], cf], [qxr[s_]])
            for s_, h in enumerate(pair):
                prr = pb[3 * s_ + 2]
                MM(prr[0:64, 0:T], RTb, qxr[s_][:, 0:T], True, True, [cb, qxr[s_]], [prr])
            for s_, h in enumerate(pair):
                prr = pb[3 * s_ + 2]
                t1, t2 = qtc[2 * s_], qtc[2 * s_ + 1]
                TT(t1[0:64, 0:T], qxr[s_][:, 0:T], cosT[:, 0:T], ALU.mult, [qxr[s_], cosT], [t1])
                TT(t2[0:64, 0:T], prr[0:64, 0:T], sinT[:, 0:T], ALU.mult, [prr, cosT], [t2])
                TT(qrT[0:64, h, 0:T], t1[0:64, 0:T], t2[0:64, 0:T], ALU.add, [t1, t2], [qrT])

        kv_block(si, kb_cur, T, write_scratch)
        attention(si, T, nkb_past, ucount)
        if debug and last_block:
            DMA("sp", dbg_d["qn"], qnT.ap, [qnT], [], dbg_sig)
            DMA("sp", dbg_d["kn"], knT.ap, [knT], [], dbg_sig)

        for j in range(ntl):
            r0 = row0 + j * 128
            DMA("sp", xres[j][0:TSz, :], xs[r0:r0 + TSz, :], [], [xres[j]], xres_sig[j])
        for half in range(2):
            ub = ucount + 12 + half * 4
            for which, dst in ((0, sgm), (1, sga)):
                u = unit(ub + which)
                for c in range(4):
                    pg = pb[c % 2]
                    for kc in range(8):
                        MM(pg[:, 0:T], u[:, kc, c * 128:(c + 1) * 128], hT[:, kc, 0:T], kc == 0, kc == 7, [u, hT], [pg])
                    e = td[c % 2]
                    ACT(e[:, 0:T], pg[:, 0:T], AF.Exp, [pg], [e], scale=-1.0)
                    ACT(e[:, 0:T], e[:, 0:T], AF.Ln, [e], [e], bias=1.0)
                    ACT(dst[:, c, 0:T], e[:, 0:T], AF.Exp, [e], [dst], scale=-1.0)
            u = unit(ub + 2)
            for c in range(4):
                pg = pb[2 + c % 2]
                for kc in range(8):
                    MM(pg[:, 0:T], u[:, kc, c * 128:(c + 1) * 128], hmT[:, kc, 0:T], kc == 0, kc == 7, [u, hmT], [pg])
                TT(tmpm[:, c, 0:T], pg[:, 0:T], sgm[:, c, 0:T], ALU.mult, [pg, sgm], [tmpm])
            u = unit(ub + 3)
            for c in range(4):
                pg = pb[c % 2]
                for kc in range(8):
                    MM(pg[:, 0:T], u[:, kc, c * 128:(c + 1) * 128], haT[:, kc, 0:T], kc == 0, kc == 7, [u, haT], [pg])
                e = td[c % 2]
                TT(e[:, 0:T], pg[:, 0:T], sga[:, c, 0:T], ALU.mult, [pg, sga], [e])
                TT(mergedT[:, half * 4 + c, 0:T], e[:, 0:T], tmpm[:, c, 0:T], ALU.add, [e, tmpm], [mergedT])
        uo0 = unit(ucount + 20)
        uo1 = unit(ucount + 21, oldest=ucount + 20)
        for j in range(ntl):
            xi = xres[j]
            r0 = row0 + j * 128
            yo = yout[j % 2]
            for g, u in ((0, uo0), (1, uo1)):
                py = pb[2 + g]
                for kc in range(8):
                    MM(py[0:TSz, :], mergedT[:, kc, j * 128:j * 128 + TSz], u[:, kc, :], kc == 0, kc == 7,
                       [mergedT, u], [py])
                TT(yo[0:TSz, g * 512:(g + 1) * 512], py[0:TSz, :], xi[0:TSz, g * 512:(g + 1) * 512], ALU.add,
                   [py, xi], [yo])
            DMA("sp", y_d[r0:r0 + TSz, :], yo[0:TSz, :], [yo], [], yo_sig[j % 2])

    nblocks = sum((L + 511) // 512 for (_, L, _, _) in seqs)
    ring_state["total"] = nblocks * NUNIT
    ucount = 0
    for si, (kind, L, row0, npast) in enumerate(seqs):
        mb = P.buf("mallb") if "mallb" not in kscr_b else kscr_b["mallb"]
        kscr_b["mallb"] = mb
        if npast == 0:
            MEMSET(Cst.ap, 0.0, CstH, eng="dve")
            MEMSET(hist.ap, 0.0, [hist], eng="dve")
            MEMSET(mall[:, 0:1], 0.0, [mb], eng="dve")
        else:
            DMA("sp", Cst[:, :, :, 0:256], sC_d.rearrange("h (dc p) e -> p h dc e", p=128), [], CstH, sigC)
            for h in range(HM):
                DMA("sp", Cst[:, h, :, 256], sn_d[h].rearrange("(dc p) -> p dc", p=128), [], CstH, sigC,
                    allow_slow_non_contiguous=True)
            for jj in range(3):
                DMA("sp", hist[:, :, jj], sconv_d[jj].rearrange("(c p) -> p c", p=128), [], [hist], sigH,
                    allow_slow_non_contiguous=True)
            DMA("sp", mall[:, 0:1], sm_d, [], [mb], sigM)
            for kb in range(npast // 512):
                for jt in range(4):
                    k0 = kb * 512 + jt * 128
                    DMA("sp", pcb.ap, cckv_bf[k0:k0 + 128, :], [wbfB], [pcb], pcb_sig)
                    DMA("sp", pcr.ap, ckr_bf[k0:k0 + 128, :], [wbfB], [pcr], pcr_sig)
                    for c2 in range(2):
                        TR(ptr[:, c2 * 128:(c2 + 1) * 128], pcb[:, c2 * 128:(c2 + 1) * 128], identb, [pcb, cb], [ptr])
                    TR(ptr[0:64, 256:384], pcr.ap, identb, [pcr, cb], [ptr])
                    CP(ckvnT[:, :, jt * 128:(jt + 1) * 128], ptr.ap.rearrange("p (k t) -> p k t", k=8)[:, 0:2, :],
                       [ptr], [ckvnT])
                    CP(krT[0:64, jt * 128:(jt + 1) * 128], ptr[0:64, 256:384], [ptr], [krT])
                kv_block(si, kb, 512, True)
        nb = (L + 511) // 512
        for b in range(nb):
            T = min(512, L - b * 512)
            block(si, kind, row0 + b * 512, T, npast + b * 512, npast // 512 + b, npast // 512 + b, ucount,
                  b == nb - 1, b < nb - 1)
            ucount += NUNIT
        DMA("sp", C_o[si].rearrange("h (dc p) e -> p h dc e", p=128), Cst[:, :, :, 0:256], CstH, [], sigC)
        for h in range(HM):
            DMA("sp", n_o[si, h].rearrange("(dc p) -> p dc", p=128), Cst[:, h, :, 256], CstH, [], sigC,
                allow_slow_non_contiguous=True)
        for jj in range(3):
            DMA("sp", conv_o[si, jj].rearrange("(c p) -> p c", p=128), hist[:, :, jj], [hist], [], sigH,
                allow_slow_non_contiguous=True)

    if debug:
        for n, lb in (("hm", hmT), ("ha", haT), ("mg", mergedT), ("hT", hT)):
            DMA("sp", dbg_d[n], lb.ap, [lb], [], dbg_sig)
    stats = P.lower()
    es.close()
    return nc, stats


def _rope_tables():
    half = ROPE // 2
    inv = np.power(np.float32(10000.0), -np.arange(half, dtype=np.float32) / np.float32(half)).astype(np.float32)
    pos = np.arange(NPOS, dtype=np.float32)
    ang = (pos[:, None] * inv[None, :]).astype(np.float32)
    cos = np.cos(ang.astype(np.float64)).astype(np.float32)
    sin = np.sin(ang.astype(np.float64)).astype(np.float32)
    cs_tm = np.concatenate([cos, cos], axis=1)
    sn_tm = np.concatenate([-sin, sin], axis=1)
    return (np.ascontiguousarray(cs_tm.T), np.ascontiguousarray(np.concatenate([sin, sin], axis=1).T),
            np.ascontiguousarray(cs_tm), np.ascontiguousarray(sn_tm))


def _prep_shared(norm_g, w_in, b_if, conv_w, conv_b, wq_m, wk_m, hnorm_g, qn_g, w_uq, kvn_g, w_ukv,
                 g_qn, g_qr, g_kn, g_kr, w_pm, w_pa, w_out):
    f = np.float32
    w_in, w_pm, w_pa, w_out = w_in[0], w_pm[0], w_pa[0], w_out[0]
    o = np.cumsum([0, 1024, 1024, 4, 4, 1024, 1024, QL, KVL, ROPE, 1024, 1024, 1024])
    seg = {n: (o[i], o[i + 1]) for i, n in enumerate(
        ["xc", "vm", "ig", "fg", "op", "zm", "cq", "ckv", "kr", "za", "gm", "ga"])}

    def cols(n, a=0, b=None):
        s, e = seg[n]
        return w_in[:, s + a:(e if b is None else s + b)]

    z = lambda n: np.zeros((1024, n), f)
    units = [cols("xc", 0, 512), cols("xc", 512, 1024), cols("vm", 0, 512), cols("vm", 512, 1024),
             cols("op", 0, 512), cols("zm", 0, 512), cols("op", 512, 1024), cols("zm", 512, 1024),
             np.concatenate([cols("ckv"), cols("kr"), cols("ig"), cols("fg"), z(512 - 328)], axis=1),
             np.concatenate([cols("cq"), z(512 - QL)], axis=1),
             cols("za", 0, 512), cols("za", 512, 1024),
             cols("gm", 0, 512), cols("ga", 0, 512), w_pm[:, 0:512], w_pa[:, 0:512],
             cols("gm", 512, 1024), cols("ga", 512, 1024), w_pm[:, 512:1024], w_pa[:, 512:1024],
             w_out[:, 0:512], w_out[:, 512:1024]]
    assert len(units) == NUNIT
    wcat = np.stack([u.reshape(8, 128, 512).transpose(1, 0, 2) for u in units]).astype(f)
    wuq = w_uq[0].reshape(3, 128, HA, NOPE + ROPE).transpose(1, 0, 2, 3)
    wuqn = np.ascontiguousarray(wuq[..., :NOPE].reshape(128, 3, HA * NOPE))
    wuqr = np.ascontiguousarray(wuq[..., NOPE:].reshape(128, 3, HA * ROPE))
    wukv = w_ukv[0].reshape(2, 128, HA, NOPE + VD).transpose(1, 0, 2, 3)
    wukvk = np.ascontiguousarray(wukv[..., :NOPE].reshape(128, 2, HA * NOPE))
    wukvv = np.ascontiguousarray(wukv[..., NOPE:].reshape(128, 2, HA * VD))
    wq = np.ascontiguousarray(wq_m[0].reshape(HM, 2, 128, DH).transpose(2, 0, 1, 3))
    wk = np.ascontiguousarray(wk_m[0].reshape(HM, 2, 128, DH).transpose(2, 0, 1, 3))
    cf = np.zeros((128, 512), f)
    cf[:, 0:128] = np.eye(128, dtype=f)
    s_ = np.arange(128)[:, None]
    t_ = np.arange(128)[None, :]
    cf[:, 128:256] = ((s_ <= t_) & (s_ // 64 == t_ // 64)).astype(f)
    cf[:, 256:320] = ((s_ % 64) <= np.arange(64)[None, :]).astype(f)
    cf[:, 320:328] = norm_g[0].reshape(8, 128).T
    cf[:, 328:360] = conv_w[0].reshape(4, 8, 128).transpose(2, 1, 0).reshape(128, 32)
    cf[:, 360:368] = conv_b[0].reshape(8, 128).T
    cf[:, 368:371] = qn_g[0].reshape(3, 128).T
    cf[:, 371] = g_qn[0]
    cf[:, 372] = g_kn[0]
    cf[0:64, 373] = g_qr[0]
    cf[:, 374] = EPS
    cf[0:4, 384:512] = 1.0
    cbm = np.zeros((128, 448), f)
    cbm[:, 0:128] = np.eye(128, dtype=f)
    cbm[:, 128:256] = 1.0
    RT = np.zeros((64, 64), f)
    for i in range(32):
        RT[i + 32, i] = -1.0
        RT[i, i + 32] = 1.0
    cbm[0:64, 256:320] = RT
    bcm = np.concatenate([hnorm_g[0].reshape(-1), kvn_g[0], g_kr[0], b_if[0]]).astype(f)
    bcm = np.ascontiguousarray(np.broadcast_to(bcm[None, :], (128, bcm.shape[0])))
    cosT, sinT, cstm, sntm = _rope_tables()
    return dict(wcat=wcat, wuqn=wuqn, wuqr=wuqr, wukvk=wukvk, wukvv=wukvv, wq=wq, wk=wk, cf=cf, cb=cbm, bc=bcm,
                cosT=cosT, sinT=sinT, cstm=cstm, sntm=sntm)


_CACHE = {}


def kernel(x_prompt, x_sample, cache_ckv, cache_kr, state_conv, state_C, state_n, state_m,
           norm_g, w_in, b_if, conv_w, conv_b, wq_m, wk_m, hnorm_g,
           qn_g, w_uq, kvn_g, w_ukv, g_qn, g_qr, g_kn, g_kr, w_pm, w_pa, w_out):
    A = lambda a: np.ascontiguousarray(np.asarray(a, dtype=np.float32))
    x_prompt, x_sample = A(x_prompt), A(x_sample)
    B, S, _ = x_prompt.shape
    DB, DS, _ = x_sample.shape
    NCORE = 8
    bpc = B // NCORE
    P_ = np.asarray(cache_ckv).shape[2]
    seqs = [("prompt", S, i * S, 0) for i in range(bpc)] + [("sample", DS, bpc * S, P_)]
    ntok = bpc * S + DS
    shared = _prep_shared(*[A(a) for a in (norm_g, w_in, b_if, conv_w, conv_b, wq_m, wk_m, hnorm_g, qn_g, w_uq, kvn_g,
                                            w_ukv, g_qn, g_qr, g_kn, g_kr, w_pm, w_pa, w_out)])
    key = (tuple(seqs), ntok)
    if key not in _CACHE:
        _CACHE[key] = build(seqs, ntok)[0]
    nc = _CACHE[key]
    in_maps = []
    for c in range(NCORE):
        xs = np.concatenate([x_prompt[c * bpc:(c + 1) * bpc].reshape(bpc * S, D), x_sample[c]], axis=0)
        m = dict(shared)
        m.update(xs=np.ascontiguousarray(xs), cckv=A(cache_ckv)[0, c], ckr=A(cache_kr)[0, c],
                 sconv=A(state_conv)[0, c], sC=A(state_C)[0, c], sn=A(state_n)[0, c],
                 sm=A(state_m)[0, c].reshape(HM, 1))
        in_maps.append(m)
    res = run_bass_kernel_spmd(nc, in_maps, core_ids=list(range(NCORE)))
    R = res.results
    cat = lambda k: [r[k] for r in R]
    yp = np.stack([r["y"][:bpc * S].reshape(bpc, S, D) for r in R]).reshape(B, S, D)
    ys = np.stack([r["y"][bpc * S:] for r in R])
    ckv_p = np.stack([r["ckv_o"][:bpc * S].reshape(bpc, S, KVL) for r in R]).reshape(1, B, S, KVL)
    ckv_s = np.stack([r["ckv_o"][bpc * S:] for r in R])[None]
    kr_p = np.stack([r["kr_o"][:bpc * S].reshape(bpc, S, ROPE) for r in R]).reshape(1, B, S, ROPE)
    kr_s = np.stack([r["kr_o"][bpc * S:] for r in R])[None]
    conv_p = np.stack([r["conv_o"][:bpc] for r in R]).reshape(1, B, 3, D)
    conv_s = np.stack([r["conv_o"][bpc] for r in R])[None]
    C_p = np.stack([r["C_o"][:bpc] for r in R]).reshape(1, B, HM, DH, DH)
    C_s = np.stack([r["C_o"][bpc] for r in R])[None]
    n_p = np.stack([r["n_o"][:bpc] for r in R]).reshape(1, B, HM, DH)
    n_s = np.stack([r["n_o"][bpc] for r in R])[None]
    m_p = np.stack([r["m_o"][:bpc] for r in R]).reshape(1, B, HM)
    m_s = np.stack([r["m_o"][bpc] for r in R]).reshape(1, DB, HM)
    f = np.float32
    return tuple(np.ascontiguousarray(a, dtype=f) for a in
                 (yp, ys, ckv_p, kr_p, conv_p, C_p, n_p, m_p, ckv_s, kr_s, conv_s, C_s, n_s, m_s))
```

```python
import numpy as np
from contextlib import ExitStack
import concourse.bass as bass
import concourse.mybir as mybir
from concourse.bass_utils import run_bass_kernel_spmd

F32 = mybir.dt.float32
BF16 = mybir.dt.bfloat16
AF = mybir.ActivationFunctionType
ALU = mybir.AluOpType
AX = mybir.AxisListType

D = 1024
HM, DH = 4, 256
HA, NOPE, ROPE, VD = 8, 128, 64, 128
QL, KVL = 384, 256
EPS = 1e-6
ATT_SCALE = float((NOPE + ROPE) ** -0.5)
NUNIT = 22
RING = 3
NKVS = 4
NONLEGACY = ("A.", "C.", "D.")
NPOS = 4096 + 1024


class Sig:
    def __init__(self, sem, unit, name):
        self.sem, self.unit, self.name, self.n = sem, unit, name, 0


class Buf:
    def __init__(self, name, rng=None):
        self.name, self.rng = name, rng
        self.w = None
        self.r = {}
        self.over = []
        self.legacy = True
        self.psum = False


class Op:
    __slots__ = ("eng", "meth", "kw", "deps", "sig", "inc", "val", "dma", "w_bufs")

    def __init__(self, eng, meth, kw, sig, dma):
        self.eng, self.meth, self.kw, self.sig, self.dma = eng, meth, kw, sig, dma
        self.deps, self.inc, self.val = [], dma, 0


class Prog:
    def __init__(self, nc, es):
        self.nc, self.es = nc, es
        self.h = {"pe": nc.tensor, "act": nc.scalar, "dve": nc.vector, "pool": nc.gpsimd, "sp": nc.sync}
        self.sig = {k: Sig(es.enter_context(nc.semaphore("s_" + k)), 1, k) for k in ("pe", "act", "dve", "pool")}
        self.ops = []
        self.bufs = []
        self.dsigs = []

    def buf(self, name, rng=None):
        b = Buf(name, rng)
        if rng is not None:
            for o in self.bufs:
                if o.rng is not None and o.rng[0] == rng[0] and o.rng[1] < rng[2] and rng[1] < o.rng[2]:
                    o.over.append(b)
                    b.over.append(o)
        self.bufs.append(b)
        if any(name.startswith(p) for p in NONLEGACY):
            b.legacy = False
        return b

    def dma_sig(self, name):
        s = Sig(self.es.enter_context(self.nc.semaphore("d_" + name)), 16, name)
        self.dsigs.append(s)
        return s

    def _need(self, o, p, raw):
        if p is o:
            return False
        if not o.dma and not p.dma and o.eng == p.eng:
            return raw == "raw" and o.eng != "pe"
        if o.dma and p.dma and o.sig is p.sig and raw == "waw":
            return False
        return True

    def op(self, eng, meth, kw, reads=(), writes=(), sig=None):
        dma = sig is not None
        o = Op(eng, meth, kw, sig if dma else self.sig[eng], dma)
        deps = {}
        for b in reads:
            for x in [b] + b.over:
                if x.w is not None and self._need(o, x.w, "raw"):
                    deps[id(x.w)] = x.w
                if x.psum:
                    for r in x.r.values():
                        if r.eng != o.eng:
                            deps[id(r)] = r
        for b in writes:
            for x in [b] + b.over:
                for r in x.r.values():
                    if self._need(o, r, "war"):
                        deps[id(r)] = r
                if x.w is not None and self._need(o, x.w, "waw"):
                    deps[id(x.w)] = x.w
        for b in reads:
            for x in ([b] + b.over) if b.legacy else [b]:
                x.r[id(o.sig)] = o
        for b in writes:
            for x in ([b] + b.over) if b.legacy else [b]:
                x.w = o
                x.r = {}
        o.deps = list(deps.values())
        self.ops.append(o)
        return o

    def lower(self):
        for o in self.ops:
            for d in o.deps:
                d.inc = True
        for o in self.ops:
            if o.inc:
                o.sig.n += o.sig.unit
                o.val = o.sig.n
        seen = {k: {} for k in self.h}
        nwait = 0
        for o in self.ops:
            hd = self.h[o.eng]
            sn = seen[o.eng]
            best = {}
            for d in o.deps:
                k = id(d.sig)
                if sn.get(k, 0) < d.val and best.get(k, (0, None))[0] < d.val:
                    best[k] = (d.val, d.sig)
            for k, (v, s) in best.items():
                hd.wait_ge(s.sem, v)
                sn[k] = v
                nwait += 1
            ins = getattr(hd, o.meth)(**o.kw)
            if o.inc:
                ins.then_inc(o.sig.sem, o.sig.unit)
        for s in self.dsigs:
            if s.n > 0:
                self.nc.sync.wait_ge(s.sem, s.n)
        return len(self.ops), nwait


class LB:
    def __init__(self, ap, buf):
        self.ap, self.buf = ap, buf

    def __getitem__(self, k):
        return self.ap[k]


def build(seqs, ntok, debug=False):
    nc = bass.Bass("TRN2", target_bir_lowering=False)
    es = ExitStack()
    nseq = len(seqs)
    maxkeys = max(L + npast for (_, L, _, npast) in seqs)
    maxkeys = ((maxkeys + 511) // 512) * 512

    def din(name, shape, dt=F32):
        return nc.dram_tensor(name, list(shape), dt, kind="ExternalInput").ap()

    def dout(name, shape, dt=F32):
        return nc.dram_tensor(name, list(shape), dt, kind="ExternalOutput").ap()

    xs = din("xs", [ntok, D])
    wcat = din("wcat", [NUNIT, 128, 8, 512])
    wuqn_d = din("wuqn", [128, 3, 1024])
    wuqr_d = din("wuqr", [128, 3, 512])
    wukvk_d = din("wukvk", [128, 2, 1024])
    wukvv_d = din("wukvv", [128, 2, 1024])
    wq_d = din("wq", [128, 4, 2, 256])
    wk_d = din("wk", [128, 4, 2, 256])
    cf_d = din("cf", [128, 512])
    cb_d = din("cb", [128, 448])
    bc_d = din("bc", [128, 1024 + 256 + 64 + 8])
    cosT_d = din("cosT", [64, NPOS])
    sinT_d = din("sinT", [64, NPOS])
    cstm_d = din("cstm", [NPOS, 64])
    sntm_d = din("sntm", [NPOS, 64])
    cckv_d = din("cckv", [1024, KVL])
    ckr_d = din("ckr", [1024, ROPE])
    sconv_d = din("sconv", [3, D])
    sC_d = din("sC", [HM, DH, DH])
    sn_d = din("sn", [HM, DH])
    sm_d = din("sm", [HM, 1])

    y_d = dout("y", [ntok, D])
    ckv_o = dout("ckv_o", [ntok, KVL])
    kr_o = dout("kr_o", [ntok, ROPE])
    conv_o = dout("conv_o", [nseq, 3, D])
    C_o = dout("C_o", [nseq, HM, DH, DH])
    n_o = dout("n_o", [nseq, HM, DH])
    m_o = dout("m_o", [nseq, HM, 1])

    if debug:
        dbg_d = {n: nc.dram_tensor("dbg_" + n, [128, 8, 512], BF16, kind="ExternalOutput").ap() for n in ("hm", "ha", "mg", "qn", "kn", "hT")}
    wbf = nc.dram_tensor("wbf", [NUNIT, 128, 8 * 512], BF16, kind="Internal").ap()
    cckv_bf = nc.dram_tensor("cckv_bf", [1024, KVL], BF16, kind="Internal").ap()
    ckr_bf = nc.dram_tensor("ckr_bf", [1024, ROPE], BF16, kind="Internal").ap()
    kscr = nc.dram_tensor("kscr", [HA, 128, maxkeys], BF16, kind="Internal").ap()
    vscr = nc.dram_tensor("vscr", [HA, 128, maxkeys // 128, 128], BF16, kind="Internal").ap()
    rscr = nc.dram_tensor("rscr", [128, maxkeys], BF16, kind="Internal").ap()

    P = Prog(nc, es)
    dbg_sig = P.dma_sig("dbg") if debug else None

    def sbt(name, shape, dt):
        return es.enter_context(nc.sbuf_tensor(name, list(shape), dt))

    def pst(name, shape, dt):
        return es.enter_context(nc.psum_tensor(name, list(shape), dt))

    def pers(name, shape, dt):
        t = sbt(name, shape, dt)
        return LB(t[:], P.buf(name))

    ARENA_BYTES = 68 * 1024
    arena = sbt("arena", [128, ARENA_BYTES // 2], BF16)
    aoff = {}

    aofs = {}

    def ar(phase, name, shape, dt, at=None):
        nb = int(np.prod(shape[1:])) * (4 if dt == F32 else 2)
        nb = (nb + 63) // 64 * 64
        o = aoff.get(phase, 0) if at is None else at
        assert o + nb <= ARENA_BYTES, (phase, name, o + nb)
        if at is None:
            aoff[phase] = o + nb
        aofs[name] = o
        ap = arena[0:shape[0], o // 2:(o + nb) // 2]
        n_el = int(np.prod(shape[1:]))
        if dt == F32:
            ap = ap.bitcast(F32)[:, 0:n_el]
        else:
            ap = ap[:, 0:n_el]
        if len(shape) > 2:
            names = " ".join("d%d" % i for i in range(1, len(shape)))
            ap = ap.rearrange("p (%s) -> p %s" % (names, names), **{"d%d" % i: shape[i] for i in range(2, len(shape))})
        return LB(ap, P.buf(phase + "." + name, ("arena", o, o + nb)))

    pb = []
    for i in range(6):
        t = pst("pb%d" % i, [128, 512], F32)
        pb.append(LB(t[:], P.buf("pb%d" % i)))
        pb[-1].buf.psum = True
    t = pst("ptr", [128, 1024], BF16)
    ptr = LB(t[:], P.buf("ptr"))
    ptr.buf.psum = True
    t = pst("pmisc", [128, 512], F32)
    pmisc = LB(t[:], P.buf("pmisc"))
    pmisc.buf.psum = True
    pms = [LB(pb[k].ap[:, 0:64], pb[k].buf) for k in range(6)]

    ring = [pers("ring%d" % i, [128, 8, 512], BF16) for i in range(RING)]
    ring_sig = [P.dma_sig("ring%d" % i) for i in range(RING)]
    xin = [pers("xin%d" % i, [128, D], F32) for i in range(2)]
    xin_sig = [P.dma_sig("xin%d" % i) for i in range(2)]
    xn = pers("xn", [128, D], BF16)
    junk = pers("junk", [128, D], BF16)
    hT = pers("hT", [128, 8, 512], BF16)
    hmT = pers("hmT", [128, 8, 512], BF16)
    haT = pers("haT", [128, 8, 512], BF16)
    Cst_t = sbt("Cst", [128, HM, 2, 257], F32)
    Cst = LB(Cst_t[:], None)
    CstH = [LB(Cst_t[:, h], P.buf("Cst%d" % h)) for h in range(HM)]
    Cbf = [[pers("Cbf%d_%d" % (h, dc), [128, 257], BF16) for dc in range(2)] for h in range(HM)]
    hist = pers("hist", [128, 8, 3], F32)
    wuqn = pers("wuqn_s", [128, 3, 1024], BF16)
    wuqr = pers("wuqr_s", [128, 3, 512], BF16)
    wukvk = pers("wukvk_s", [128, 2, 1024], BF16)
    wukvv = pers("wukvv_s", [128, 2, 1024], BF16)
    wq = pers("wq_s", [128, 4, 2, 256], BF16)
    wk = pers("wk_s", [128, 4, 2, 256], BF16)
    cf = pers("cf_s", [128, 512], F32)
    cb = pers("cb_s", [128, 448], BF16)
    bc = pers("bc_s", [128, 1024 + 256 + 64 + 8], F32)
    cosT = pers("cosT_s", [64, 512], F32)
    sinT = pers("sinT_s", [64, 512], F32)
    cstm = pers("cstm_s", [128, 4, 64], F32)
    sntm = pers("sntm_s", [128, 4, 64], F32)
    tab_sig = P.dma_sig("tab")
    small = pers("small", [128, 64], F32)
    smallb = [P.buf("small%d" % i) for i in range(8)]
    gcol = pers("gcol", [128, 4, 16], F32)
    grow = pers("grow", [4, 2, 512], F32)
    mall = pers("mall", [4, 40], F32)
    decbd = pers("decbd", [4, 8, 4], F32)
    decbc = pers("decbc", [128, 8, 4], F32)
    wkc = pers("wkc", [128, 4, 12], F32)
    init_sig = P.dma_sig("init")
    sigC = P.dma_sig("stC")
    sigH = P.dma_sig("stH")
    sigM = P.dma_sig("stM")

    identf = cf[:, 0:128]
    tri = cf[:, 128:256]
    cmask = cf[:, 256:320]
    ng_c = cf[:, 320:328]
    cw_c = cf[:, 328:360]
    cbias_c = cf[:, 360:368]
    qng_c = cf[:, 368:371]
    gqn_c = cf[:, 371:372]
    gkn_c = cf[:, 372:373]
    gqr_c = cf[:, 373:374]
    eye4 = cf[0:4, 0:4]
    ones4 = cf[0:4, 384:512]
    identb = cb[:, 0:128]
    onesb = cb[:, 128:256]
    RTb = cb[0:64, 256:320]
    hng_bc = bc[:, 0:1024]
    kvng_bc = bc[:, 1024:1280]
    gkr_bc = bc[:, 1280:1344]
    bif_bc = bc[:, 1344:1352]

    def bl(xs_):
        return [x.buf if isinstance(x, LB) else x for x in xs_]

    def MM(out, lhsT, rhs, start, stop, R, W):
        P.op("pe", "matmul", dict(out=out, lhsT=lhsT, rhs=rhs, start=start, stop=stop), bl(R), bl(W))

    def TR(out, in_, ident, R, W):
        P.op("pe", "transpose", dict(out=out, in_=in_, identity=ident), bl(R), bl(W))

    def ACT(out, in_, func, R, W, **kw):
        P.op("act", "activation", dict(out=out, in_=in_, func=func, **kw), bl(R), bl(W))

    def TT(out, in0, in1, op, R, W, eng="dve"):
        P.op(eng, "tensor_tensor", dict(out=out, in0=in0, in1=in1, op=op), bl(R), bl(W))

    def TS(out, in0, s1, s2, op0, op1, R, W, eng="dve"):
        P.op(eng, "tensor_scalar", dict(out=out, in0=in0, scalar1=s1, scalar2=s2, op0=op0, op1=op1), bl(R), bl(W))

    def TSA(out, in0, s1, R, W, eng="dve"):
        P.op(eng, "tensor_scalar_add", dict(out=out, in0=in0, scalar1=s1), bl(R), bl(W))

    def TSM(out, in0, s1, R, W, eng="dve"):
        P.op(eng, "tensor_scalar_mul", dict(out=out, in0=in0, scalar1=s1), bl(R), bl(W))

    def STT(out, in0, scalar, in1, op0, op1, R, W, eng="dve"):
        P.op(eng, "scalar_tensor_tensor", dict(out=out, in0=in0, scalar=scalar, in1=in1, op0=op0, op1=op1), bl(R), bl(W))

    def CP(out, in_, R, W, eng="dve"):
        P.op(eng, "tensor_copy", dict(out=out, in_=in_), bl(R), bl(W))

    def RCP(out, in_, R, W):
        P.op("dve", "reciprocal", dict(out=out, in_=in_), bl(R), bl(W))

    def MEMSET(ap, v, W, eng="dve"):
        P.op(eng, "memset", dict(ap=ap, constant=v), [], bl(W))

    def DMA(q, out, in_, R, W, sig, **kw):
        P.op(q, "dma_start", dict(out=out, in_=in_, **kw), bl(R), bl(W), sig=sig)

    def rsqrt_act(out, in_, scale, R, W, tmpbuf):
        ACT(out, in_, AF.Ln, R, W, scale=scale, bias=eps_ap(out))
        ACT(out, out, AF.Exp, W, W, scale=-0.5)

    epsb = P.buf("epsb")

    def eps_ap(like):
        np_ = like.shape[0]
        p0 = like.base_partition() if hasattr(like, "base_partition") else 0
        return cf[p0:p0 + np_, 374:375]

    init_sig2 = P.dma_sig("init2")
    DMA("sp", cf.ap, cf_d, [], [cf], init_sig2)
    DMA("sp", bc.ap, bc_d, [], [bc], init_sig2)
    for b_ in (cf, bc):
        b_.buf.w = P.ops[-1]
    DMA("pool", cb.ap, cb_d, [], [cb], init_sig)
    DMA("pool", wuqn.ap, wuqn_d, [], [wuqn], init_sig)
    DMA("pool", wuqr.ap, wuqr_d, [], [wuqr], init_sig)
    DMA("pool", wukvk.ap, wukvk_d, [], [wukvk], init_sig)
    DMA("pool", wukvv.ap, wukvv_d, [], [wukvv], init_sig)
    DMA("pool", wq.ap, wq_d, [], [wq], init_sig)
    DMA("pool", wk.ap, wk_d, [], [wk], init_sig)
    for b_ in (cb, wuqn, wuqr, wukvk, wukvv, wq, wk):
        b_.buf.w = P.ops[-1]

    wbf_sig = P.dma_sig("wbf")
    wbfB = P.buf("wbfB")
    wbfA_sig = P.dma_sig("wbfA")
    wbfA = P.buf("wbfA")
    NEARLY = 4
    for k_ in range(NUNIT):
        DMA("pool", wbf[k_], wcat[k_].rearrange("p k c -> p (k c)"), [], [wbfA if k_ < NEARLY else wbfB],
            wbfA_sig if k_ < NEARLY else wbf_sig)
    DMA("pool", cckv_bf, cckv_d, [], [wbfB], wbf_sig)
    DMA("pool", ckr_bf, ckr_d, [], [wbfB], wbf_sig)
    ring_state = {"issued": 0, "total": 0}

    def ring_prefetch(upto):
        while ring_state["issued"] < min(upto + 1, ring_state["total"]):
            k = ring_state["issued"]
            s = k % RING
            DMA("sp", ring[s].ap, wbf[k % NUNIT].rearrange("p (k c) -> p k c", k=8),
                [wbfA if (k % NUNIT) < NEARLY else wbfB], [ring[s]], ring_sig[s])
            ring_state["issued"] += 1

    def unit(k, oldest=None):
        ring_prefetch((k if oldest is None else oldest) + RING - 1)
        return ring[k % RING]

    uhT = ar("A", "uhT", [128, 8, 512], BF16)
    qTm = ar("A", "qTm", [128, 4, 2, 512], BF16)
    kTm = ar("A", "kTm", [128, 4, 2, 512], BF16)
    kw = ar("A", "kw", [128, 4, 4, 256], BF16)
    gmg = ar("A", "gmg", [128, 4, 1024], BF16)
    xcb = [ar("A", "xcb%d" % i, [128, 515], F32) for i in range(2)]
    ucv = [ar("A", "ucv%d" % i, [128, 512], F32) for i in range(2)]
    ta = [ar("A", "ta%d" % i, [128, 512], F32) for i in range(3)]
    _o = aofs["xcb0"]
    hraw = [ar("A", "hraw%d" % i, [128, 4, 257], F32, at=_o + i * 4160) for i in range(2)]
    hh = ar("A", "hh", [128, 4, 256], F32, at=_o + 8320)
    hmg = ar("A", "hmg", [128, 1024], BF16, at=_o + 8320 + 4096)
    assert _o + 8320 + 4096 + 2048 <= aoff["A"]
    qnT = ar("C", "qnT", [128, 8, 512], BF16)
    qrT = ar("C", "qrT", [128, 8, 512], BF16)
    knT = ar("C", "knT", [128, 8, 512], BF16)
    Vcur = ar("C", "Vcur", [128, 4, 1024], BF16)
    cqf = ar("C", "cqf", [128, 3, 512], F32)
    cqn = ar("C", "cqn", [128, 3, 512], BF16)
    sqb = [ar("C", "sqb%d" % i, [128, 512], BF16) for i in range(3)]
    rr = [ar("C", "rr%d" % i, [128, 512], F32) for i in range(2)]
    tc_ = [ar("C", "tc%d" % i, [128, 512], F32) for i in range(2)]
    xrb = ar("C", "xrb", [64, 512], BF16)
    ckvnT = pers("ckvnT", [128, 2, 512], BF16)
    krT = pers("krT", [128, 512], BF16)
    ckvf = [ar("A", "ckvf%d" % i, [128, 256], F32) for i in range(2)]
    ckvb = ar("A", "ckvb", [128, 256], BF16)
    krf = [ar("A", "krf%d" % i, [128, 64], F32) for i in range(2)]
    krw = [ar("A", "krw%d" % i, [128, 64], F32) for i in range(3)]
    krb = ar("A", "krb", [128, 64], BF16)
    kvs = [dict(K=ar("C", "kvK%d" % i, [128, 512], BF16), R=ar("C", "kvR%d" % i, [128, 512], BF16),
                V=ar("C", "kvV%d" % i, [128, 4, 128], BF16)) for i in range(NKVS)]
    kvs_sig = [P.dma_sig("kvs%d" % i) for i in range(NKVS)]
    _k0 = aofs["kvK0"]
    qsq = [sqb[0], sqb[1], sqb[2], ar("C", "qsq3", [128, 512], BF16, at=_k0)]
    qrr = [rr[0], rr[1], ar("C", "qrr2", [128, 512], F32, at=_k0 + 1024), ar("C", "qrr3", [128, 512], F32, at=_k0 + 3072)]
    qxr = [xrb, ar("C", "qxr1", [64, 512], BF16, at=_k0 + 5120)]
    qtc = [tc_[0], tc_[1], ar("C", "qtc2", [128, 512], F32, at=_k0 + 6144), ar("C", "qtc3", [128, 512], F32, at=_k0 + 8192)]
    assert _k0 + 10240 <= aofs["kvK0"] + NKVS * 3072
    Pt = [pers("Pt%d" % i, [128, 512], BF16) for i in range(3)]
    Pacc = [pers("Pacc%d" % i, [128, 512], F32) for i in range(2)]
    pS3 = [pb[2], pb[3], LB(ptr.ap.bitcast(F32), ptr.buf)]
    swA = ar("C", "swA", [128, 4, 4, 64], BF16)
    pcb = ar("C", "pcb", [128, 256], BF16)
    pcr = ar("C", "pcr", [128, 64], BF16)
    sgm = ar("D", "sgm", [128, 4, 512], F32)
    sga = ar("D", "sga", [128, 4, 512], F32)
    tmpm = ar("D", "tmpm", [128, 4, 512], F32)
    td = [ar("D", "td%d" % i, [128, 512], F32) for i in range(2)]
    mergedT = ar("D", "mergedT", [128, 8, 512], BF16)
    yout = [ar("D", "yout%d" % i, [128, 1024], F32) for i in range(2)]
    xres = [ar("D", "xres%d" % i, [128, D], F32) for i in range(4)]
    xres_sig = [P.dma_sig("xres%d" % i) for i in range(4)]
    v_ext = pers("v_ext", [128, 4, 4, 257], BF16)
    Pd = [pers("Pd%d" % i, [128, 512], BF16) for i in range(4)]

    scr_sig = {k: P.dma_sig("scrw" + k) for k in "KRV"}
    ckvo_sig = [P.dma_sig("ckvo%d" % i) for i in range(2)]
    kro_sig = [P.dma_sig("kro%d" % i) for i in range(2)]
    yo_sig = [P.dma_sig("yo%d" % i) for i in range(2)]
    pcb_sig = P.dma_sig("pcb")
    pcr_sig = P.dma_sig("pcr")
    kscr_b = {}

    def scr_buf(si, kb):
        key = ("scr", kb)
        if key not in kscr_b:
            kscr_b[key] = {k: P.buf("scr%s_%d" % (k, kb)) for k in "KRV"}
        return kscr_b[key]

    MEMSET(v_ext[:, :, :, 256:257], 1.0, [v_ext])
    MEMSET(krT[64:128, :], 0.0, [krT])
    for i in range(4):
        MEMSET(Pd[i].ap, 0.0, [Pd[i]])

    def kv_block(si, kb, NK, to_scratch):
        TSk = min(NK, 128)
        for h in range(HA):
            pk = pb[h % 2]
            for c2 in range(2):
                MM(pk[:, 0:NK], wukvk[:, c2, h * 128:(h + 1) * 128], ckvnT[:, c2, 0:NK], c2 == 0, c2 == 1,
                   [wukvk, ckvnT], [pk])
            sq = sqb[h % 2]
            ACT(sq[:, 0:NK], pk[:, 0:NK], AF.Square, [pk], [sq])
            ps = pb[2 + h % 2]
            MM(ps[:, 0:NK], onesb, sq[:, 0:NK], True, True, [cb, sq], [ps])
            r = rr[h % 2]
            rsqrt_act(r[:, 0:NK], ps[:, 0:NK], 1.0 / NOPE, [ps, cf], [r], None)
            STT(knT[:, h, 0:NK], pk[:, 0:NK], gkn_c, r[:, 0:NK], ALU.mult, ALU.mult, [pk, r, cf], [knT])
        for jt in range((NK + 127) // 128):
            for g in range(2):
                pv = pb[4 + g]
                for c2 in range(2):
                    MM(pv[0:TSk, :], ckvnT[:, c2, jt * 128:jt * 128 + TSk], wukvv[:, c2, g * 512:(g + 1) * 512],
                       c2 == 0, c2 == 1, [ckvnT, wukvv], [pv])
                ACT(Vcur[0:TSk, jt, g * 512:(g + 1) * 512], pv[0:TSk, :], AF.Copy, [pv], [Vcur])
        if to_scratch:
            sb_ = scr_buf(si, kb)
            k0 = kb * 512
            DMA("sp", kscr[:, :, k0:k0 + NK].rearrange("h d k -> d h k"), knT[:, :, 0:NK], [knT], [sb_["K"]], scr_sig["K"])
            DMA("sp", rscr[:, k0:k0 + NK], krT[:, 0:NK], [krT], [sb_["R"]], scr_sig["R"])
            nt = NK // 128
            for t_ in range(nt):
                DMA("sp", vscr[:, :, kb * 4 + t_, :].rearrange("h p d -> p h d"),
                    Vcur[:, t_, :].rearrange("p (h d) -> p h d", h=HA), [Vcur], [sb_["V"]], scr_sig["V"])

    def attention(si, T, nkb_past, ucount):
        TSq = min(T, 128)
        ntl = (T + 127) // 128
        loads = [(h_, kb_) for h_ in range(HA) for kb_ in range(nkb_past)]
        nload = [0]

        def issue_load():
            li = nload[0]
            if li >= len(loads):
                return
            nload[0] += 1
            h_, kb_ = loads[li]
            s_ = li % NKVS
            sl = kvs[s_]
            slb = [sl["K"], sl["R"], sl["V"]]
            sb_ = scr_buf(si, kb_)
            k0 = kb_ * 512
            DMA("sp", sl["K"].ap, kscr[h_, :, k0:k0 + 512], list(sb_.values()), slb, kvs_sig[s_])
            DMA("sp", sl["R"].ap, rscr[:, k0:k0 + 512], list(sb_.values()), slb, kvs_sig[s_])
            DMA("sp", sl["V"].ap, vscr[h_, :, kb_ * 4:kb_ * 4 + 4, :], list(sb_.values()), slb, kvs_sig[s_])

        for _ in range(NKVS - 1):
            issue_load()
        pending = [None]
        for h in range(HA):
            po, pden = (pb[4], pb[5]) if h % 2 == 0 else (pb[0], pb[1])
            tasks = []
            for kb in range(nkb_past):
                li = h * nkb_past + kb
                sl = kvs[li % NKVS]
                slb = [sl["K"], sl["R"], sl["V"]]
                for jt in range(4):
                    tasks.append(("past", sl, slb, jt))
            for jt in range(ntl):
                tasks.append(("diag", None, None, jt))
            nt_ = len(tasks)

            pacc = Pacc[h % 2]

            def emit_S(i):
                kind_, sl, slb, jt = tasks[i]
                pS = pS3[i % 3]
                if kind_ == "past":
                    MM(pS[:, 0:T], sl["K"][:, jt * 128:(jt + 1) * 128], qnT[:, h, 0:T], True, False, slb + [qnT], [pS])
                    MM(pS[:, 0:T], sl["R"][:, jt * 128:(jt + 1) * 128], qrT[:, h, 0:T], False, True,
                       slb + [qrT], [pS])
                else:
                    c0 = jt * 128
                    MM(pS[0:TSq, c0:T], knT[:, h, c0:c0 + TSq], qnT[:, h, c0:T], True, False, [knT, qnT], [pS])
                    MM(pS[0:TSq, c0:T], krT[:, c0:c0 + TSq], qrT[:, h, c0:T], False, True, [krT, qrT], [pS])

            def emit_rest(i):
                kind_, sl, slb, jt = tasks[i]
                pS = pS3[i % 3]
                first, last = i == 0, i == nt_ - 1
                if kind_ == "past":
                    pt_ = Pt[i % 3]
                    ACT(pt_[:, 0:T], pS[:, 0:T], AF.Exp, [pS], [pt_], scale=ATT_SCALE)
                    MM(po[:, 0:T], sl["V"][:, jt, :], pt_[:, 0:T], first, last, slb + [pt_], [po])
                    if first:
                        CP(pacc[:, 0:T], pt_[:, 0:T], [pt_], [pacc])
                    else:
                        TT(pacc[:, 0:T], pacc[:, 0:T], pt_[:, 0:T], ALU.add, [pacc, pt_], [pacc])
                else:
                    c0 = jt * 128
                    pd_ = Pd[jt]
                    ACT(pd_[0:64, c0:T], pS[0:64, c0:T], AF.Exp, [pS], [pd_], scale=ATT_SCALE)
                    if TSq == 128:
                        ACT(pd_[64:128, c0 + 64:T], pS[64:128, c0 + 64:T], AF.Exp, [pS], [pd_], scale=ATT_SCALE)
                    MM(po[:, c0:T], Vcur[0:TSq, jt, h * 128:(h + 1) * 128], pd_[0:TSq, c0:T], first, last,
                       [Vcur, pd_], [po])
                    if first:
                        if TSq < 128:
                            MEMSET(pacc[:, 0:T], 0.0, [pacc], eng="dve")
                        CP(pacc[0:TSq, 0:T], pd_[0:TSq, 0:T], [pd_], [pacc])
                    else:
                        TT(pacc[0:TSq, c0:T], pacc[0:TSq, c0:T], pd_[0:TSq, c0:T], ALU.add, [pacc, pd_], [pacc])

            emit_S(0)
            if nt_ > 1:
                emit_S(1)
            if pending[0] is not None:
                pending[0]()
                pending[0] = None
            for i in range(nt_):
                if i + 2 < nt_:
                    emit_S(i + 2)
                emit_rest(i)
                if tasks[i][0] == "past" and tasks[i][3] == 1:
                    issue_load()
            def fin(h=h, po=po, pden=pden, pacc=pacc):
                hi_, lo_ = sqb[0], sqb[1]
                CP(hi_[:, 0:T], pacc[:, 0:T], [pacc], [hi_])
                TT(lo_[:, 0:T], pacc[:, 0:T], hi_[:, 0:T], ALU.subtract, [pacc, hi_], [lo_])
                MM(pden[:, 0:T], onesb, hi_[:, 0:T], True, False, [cb, hi_], [pden])
                MM(pden[:, 0:T], onesb, lo_[:, 0:T], False, True, [cb, lo_], [pden])
                u = unit(ucount + 10 + h // 4)
                pz = pmisc
                for kc in range(8):
                    MM(pz[:, 0:T], u[:, kc, (h % 4) * 128:(h % 4 + 1) * 128], hT[:, kc, 0:T], kc == 0, kc == 7,
                       [u, hT], [pz])
                e, zf, d1, n1 = tc_[0], tc_[1], rr[0], rr[1]
                ACT(e[:, 0:T], pz[:, 0:T], AF.Exp, [pz], [e], scale=-1.0)
                ACT(zf[:, 0:T], pz[:, 0:T], AF.Copy, [pz], [zf])
                STT(d1[:, 0:T], e[:, 0:T], 1.0, pden[:, 0:T], ALU.add, ALU.mult, [e, pden], [d1])
                ACT(d1[:, 0:T], d1[:, 0:T], AF.Ln, [d1], [d1])
                ACT(d1[:, 0:T], d1[:, 0:T], AF.Exp, [d1], [d1], scale=-1.0)
                TT(n1[:, 0:T], po[:, 0:T], zf[:, 0:T], ALU.mult, [po, zf], [n1])
                TT(haT[:, h, 0:T], n1[:, 0:T], d1[:, 0:T], ALU.mult, [n1, d1], [haT])
            pending[0] = fin
        if pending[0] is not None:
            pending[0]()
            pending[0] = None

    def block(si, kind, row0, T, pos0, nkb_past, kb_cur, ucount, last_block, write_scratch):
        TSz = min(T, 128)
        ntl = (T + 127) // 128
        NCH = T // 64
        DMA("sp", cosT[:, 0:T], cosT_d[:, pos0:pos0 + T], [], [cosT], tab_sig)
        DMA("sp", sinT[:, 0:T], sinT_d[:, pos0:pos0 + T], [], [cosT], tab_sig)
        for j in range(ntl):
            DMA("sp", cstm[0:TSz, j, :], cstm_d[pos0 + j * 128:pos0 + j * 128 + TSz, :], [], [cosT], tab_sig)
            DMA("sp", sntm[0:TSz, j, :], sntm_d[pos0 + j * 128:pos0 + j * 128 + TSz, :], [], [cosT], tab_sig)
        for j in range(ntl):
            xi = xin[j % 2]
            r0 = row0 + j * 128
            DMA("sp", xi[0:TSz, :], xs[r0:r0 + TSz, :], [], [xi], xin_sig[j % 2])
            sx = small[0:TSz, 0:1]
            ACT(junk[0:TSz, :], xi[0:TSz, :], AF.Square, [xi], [junk, smallb[0]], accum_out=sx)
            rsqrt_act(sx, sx, 1.0 / D, [smallb[0], cf], [smallb[0]], None)
            ACT(xn[0:TSz, :], xi[0:TSz, :], AF.Copy, [xi, smallb[0]], [xn], scale=sx)
            for kc in range(8):
                TR(ptr[:, kc * 128:kc * 128 + TSz], xn[0:TSz, kc * 128:(kc + 1) * 128], identb[0:TSz, 0:TSz],
                   [xn, cb], [ptr])
            TT(hT[:, :, j * 128:j * 128 + TSz], ptr.ap.rearrange("p (k t) -> p k t", k=8)[:, :, 0:TSz],
               ng_c.unsqueeze(2).to_broadcast([128, 8, TSz]), ALU.mult, [ptr, cf], [hT])

        def conv_front(cc):
            u = unit(ucount + cc // 4)
            pc = pb[cc % 2]
            for kc in range(8):
                MM(pc[:, 0:T], u[:, kc, (cc % 4) * 128:(cc % 4 + 1) * 128], hT[:, kc, 0:T], kc == 0, kc == 7,
                   [u, hT], [pc])
            xb_ = xcb[cc % 2]
            CP(xb_[:, 0:3], hist[:, cc, :], [hist], [xb_])
            ACT(xb_[:, 3:3 + T], pc[:, 0:T], AF.Copy, [pc], [xb_])

        def conv_silu(cc):
            uc, e = ucv[cc % 2], ta[cc % 2]
            ACT(e[:, 0:T], uc[:, 0:T], AF.Exp, [uc], [e], scale=-1.0)
            ACT(e[:, 0:T], e[:, 0:T], AF.Ln, [e], [e], bias=1.0)
            ACT(e[:, 0:T], e[:, 0:T], AF.Exp, [e], [e], scale=-1.0)

        def conv_taps(cc):
            xb_, uc = xcb[cc % 2], ucv[cc % 2]
            CP(hist[:, cc, :], xb_[:, T:T + 3], [xb_], [hist])
            TS(uc[:, 0:T], xb_[:, 0:T], cw_c[:, cc * 4:cc * 4 + 1], cbias_c[:, cc:cc + 1], ALU.mult, ALU.add,
               [xb_, cf], [uc])
            for jj in range(1, 4):
                STT(uc[:, 0:T], xb_[:, jj:jj + T], cw_c[:, cc * 4 + jj:cc * 4 + jj + 1], uc[:, 0:T], ALU.mult, ALU.add,
                    [xb_, uc, cf], [uc])

        def conv_mult(cc):
            uc, e = ucv[cc % 2], ta[cc % 2]
            TT(uhT[:, cc, 0:T], uc[:, 0:T], e[:, 0:T], ALU.mult, [uc, e], [uhT])

        for k in range(9):
            if k < 8:
                conv_front(k)
            if k >= 1:
                conv_silu(k - 1)
            if k < 8:
                conv_taps(k)
            if k >= 1:
                conv_mult(k - 1)

        for g in range(2):
            u = unit(ucount + 2 + g)
            for j in range(ntl):
                pv = pb[(g * ntl + j) % 2]
                for kc in range(8):
                    MM(pv[0:TSz, :], hT[:, kc, j * 128:j * 128 + TSz], u[:, kc, :], kc == 0, kc == 7, [hT, u], [pv])
                ACT(v_ext[0:TSz, j, 2 * g:2 * g + 2, 0:256], pv[0:TSz, :].rearrange("p (h e) -> p h e", h=2), AF.Copy,
                    [pv], [v_ext])
        for g in range(2):
            uo = unit(ucount + 4 + 2 * g)
            uz = unit(ucount + 5 + 2 * g, oldest=ucount + 4 + 2 * g)
            for j in range(ntl):
                po_, pz_ = (pb[0], pb[1]) if (g * ntl + j) % 2 == 0 else (pb[2], pb[3])
                for kc in range(8):
                    MM(po_[0:TSz, :], hT[:, kc, j * 128:j * 128 + TSz], uo[:, kc, :], kc == 0, kc == 7, [hT, uo], [po_])
                for kc in range(8):
                    MM(pz_[0:TSz, :], hT[:, kc, j * 128:j * 128 + TSz], uz[:, kc, :], kc == 0, kc == 7, [hT, uz], [pz_])
                eo, ez, g1 = ta[0], ta[1], ta[2]
                ACT(eo[0:TSz, :], po_[0:TSz, :], AF.Exp, [po_], [eo], scale=-1.0)
                ACT(ez[0:TSz, :], pz_[0:TSz, :], AF.Exp, [pz_], [ez], scale=-1.0)
                TT(g1[0:TSz, :], pz_[0:TSz, :], hng_bc[0:TSz, g * 512:(g + 1) * 512], ALU.mult, [pz_, bc], [g1])
                ACT(eo[0:TSz, :], eo[0:TSz, :], AF.Ln, [eo], [eo], bias=1.0)
                ACT(ez[0:TSz, :], ez[0:TSz, :], AF.Ln, [ez], [ez], bias=1.0)
                TT(eo[0:TSz, :], eo[0:TSz, :], ez[0:TSz, :], ALU.add, [eo, ez], [eo])
                ACT(eo[0:TSz, :], eo[0:TSz, :], AF.Exp, [eo], [eo], scale=-1.0)
                TT(gmg[0:TSz, j, g * 512:(g + 1) * 512], g1[0:TSz, :], eo[0:TSz, :], ALU.mult, [g1, eo], [gmg])

        u8 = unit(ucount + 8)
        for j in range(ntl):
            pk = pb[j % 2]
            r0 = row0 + j * 128
            for kc in range(8):
                MM(pk[0:TSz, 0:328], hT[:, kc, j * 128:j * 128 + TSz], u8[:, kc, 0:328], kc == 0, kc == 7, [hT, u8], [pk])
            s1 = small[0:TSz, 1:2]
            ACT(junk[0:TSz, 0:256], pk[0:TSz, 0:256], AF.Square, [pk], [junk, smallb[1]], accum_out=s1)
            rsqrt_act(s1, s1, 1.0 / KVL, [smallb[1], cf], [smallb[1]], None)
            cf_ = ckvf[j % 2]
            STT(cf_[0:TSz, :], pk[0:TSz, 0:256], s1, kvng_bc[0:TSz, :], ALU.mult, ALU.mult, [pk, smallb[1], bc], [cf_])
            DMA("sp", ckv_o[r0:r0 + TSz, :], cf_[0:TSz, :], [cf_], [], ckvo_sig[j % 2])
            ACT(ckvb[0:TSz, :], cf_[0:TSz, :], AF.Copy, [cf_], [ckvb])
            for c2 in range(2):
                TR(ptr[:, c2 * 128:c2 * 128 + TSz], ckvb[0:TSz, c2 * 128:(c2 + 1) * 128], identb[0:TSz, 0:TSz],
                   [ckvb, cb], [ptr])
            CP(ckvnT[:, :, j * 128:j * 128 + TSz], ptr.ap.rearrange("p (k t) -> p k t", k=8)[:, 0:2, 0:TSz],
               [ptr], [ckvnT])
            s2 = small[0:TSz, 2:3]
            ACT(junk[0:TSz, 256:320], pk[0:TSz, 256:320], AF.Square, [pk], [junk, smallb[2]], accum_out=s2)
            rsqrt_act(s2, s2, 1.0 / ROPE, [smallb[2], cf], [smallb[2]], None)
            xr, xsw, tA = krw[0], krw[1], krw[2]
            STT(xr[0:TSz, :], pk[0:TSz, 256:320], s2, gkr_bc[0:TSz, :], ALU.mult, ALU.mult, [pk, smallb[2], bc], [xr])
            CP(xsw[0:TSz, 0:32], xr[0:TSz, 32:64], [xr], [xsw])
            CP(xsw[0:TSz, 32:64], xr[0:TSz, 0:32], [xr], [xsw])
            TT(tA[0:TSz, :], xr[0:TSz, :], cstm[0:TSz, j, :], ALU.mult, [xr, cosT], [tA])
            TT(xsw[0:TSz, :], xsw[0:TSz, :], sntm[0:TSz, j, :], ALU.mult, [xsw, cosT], [xsw])
            kf = krf[j % 2]
            TT(kf[0:TSz, :], tA[0:TSz, :], xsw[0:TSz, :], ALU.add, [tA, xsw], [kf])
            DMA("sp", kr_o[r0:r0 + TSz, :], kf[0:TSz, :], [kf], [], kro_sig[j % 2])
            ACT(krb[0:TSz, :], kf[0:TSz, :], AF.Copy, [kf], [krb])
            TR(ptr[0:64, 256:256 + TSz], krb[0:TSz, :], identb[0:TSz, 0:TSz], [krb, cb], [ptr])
            CP(krT[0:64, j * 128:j * 128 + TSz], ptr[0:64, 256:256 + TSz], [ptr], [krT])
            gb = smallb[3]
            TT(gcol[0:TSz, j, 0:8], pk[0:TSz, 320:328], bif_bc[0:TSz, :], ALU.add, [pk, bc], [gb])
            ACT(gcol[0:TSz, j, 8:12], gcol[0:TSz, j, 4:8], AF.Exp, [gb], [gb], scale=-1.0)
            ACT(gcol[0:TSz, j, 8:12], gcol[0:TSz, j, 8:12], AF.Ln, [gb], [gb], bias=1.0)
            MM(pmisc[0:TSz, 0:4], tri[0:TSz, 0:TSz], gcol[0:TSz, j, 8:12], True, True, [cf, gb], [pmisc])
            CP(gcol[0:TSz, j, 12:16], pmisc[0:TSz, 0:4], [pmisc], [gb])
            TT(gcol[0:TSz, j, 4:8], gcol[0:TSz, j, 0:4], gcol[0:TSz, j, 12:16], ALU.add, [gb], [gb])
            TR(pmisc[0:4, 128:128 + TSz], gcol[0:TSz, j, 4:8], identf[0:TSz, 0:TSz], [gb, cf], [pmisc])
            TR(pmisc[0:4, 256:256 + TSz], gcol[0:TSz, j, 12:16], identf[0:TSz, 0:TSz], [gb, cf], [pmisc])
            CP(grow[:, 0, j * 128:j * 128 + TSz], pmisc[0:4, 128:128 + TSz], [pmisc], [grow])
            CP(grow[:, 1, j * 128:j * 128 + TSz], pmisc[0:4, 256:256 + TSz], [pmisc], [grow])
        av = grow[:, 0, 0:T].rearrange("p (c t) -> p c t", t=64)
        cv = grow[:, 1, 0:T].rearrange("p (c t) -> p c t", t=64)
        mb = P.buf("mallb") if "mallb" not in kscr_b else kscr_b["mallb"]
        kscr_b["mallb"] = mb
        P.op("dve", "tensor_reduce", dict(out=mall[:, 10:10 + NCH], in_=av, axis=AX.X, op=ALU.max), bl([grow]), bl([mb]))
        TSM(mall[:, 20:20 + NCH], cv[:, :, 63], -1.0, [grow], [mb])
        P.op("dve", "tensor_tensor_scan", dict(out=mall[:, 1:1 + NCH], data0=mall[:, 10:10 + NCH], data1=mall[:, 20:20 + NCH],
                                               initial=mall[:, 0:1], op0=ALU.max, op1=ALU.add), bl([mb]), bl([mb]))
        TT(mall[:, 10:10 + NCH], mall[:, 1:1 + NCH], mall[:, 20:20 + NCH], ALU.subtract, [mb], [mb])
        TT(mall[:, 20:20 + NCH], mall[:, 0:NCH], mall[:, 10:10 + NCH], ALU.subtract, [mb], [mb])
        ACT(mall[:, 20:20 + NCH], mall[:, 20:20 + NCH], AF.Exp, [mb], [mb])
        Mb = mall[:, 10:10 + NCH].unsqueeze(2).to_broadcast([4, NCH, 64])
        TT(av, av, Mb, ALU.subtract, [grow, mb], [grow])
        TT(cv, cv, Mb, ALU.subtract, [grow, mb], [grow])
        ACT(grow[:, 0:2, 0:T], grow[:, 0:2, 0:T], AF.Exp, [grow], [grow])
        TT(decbd[:, 0:NCH, :], mall[:, 20:20 + NCH].unsqueeze(2).to_broadcast([4, NCH, 4]),
           eye4.unsqueeze(1).to_broadcast([4, NCH, 4]), ALU.mult, [mb, cf], [decbd])
        MM(pmisc[:, 0:NCH * 4], ones4, decbd[:, 0:NCH, :].rearrange("p c h -> p (c h)"), True, True, [cf, decbd], [pmisc])
        CP(decbc[:, 0:NCH, :].rearrange("p c h -> p (c h)"), pmisc[:, 0:NCH * 4], [pmisc], [decbc])
        for j in range(ntl):
            TR(pmisc[0:TSz, 384:388], grow[:, 0, j * 128:j * 128 + TSz], eye4, [grow, cf], [pmisc])
            TR(pmisc[0:TSz, 392:396], grow[:, 1, j * 128:j * 128 + TSz], eye4, [grow, cf], [pmisc])
            CP(wkc[0:TSz, j, 0:4], pmisc[0:TSz, 384:388], [pmisc], [wkc])
            ACT(wkc[0:TSz, j, 4:8], pmisc[0:TSz, 384:388], AF.Copy, [pmisc], [wkc], scale=1.0 / 16)
            CP(wkc[0:TSz, j, 8:12], pmisc[0:TSz, 392:396], [pmisc], [wkc])
        if last_block:
            DMA("sp", m_o[si], mall[:, NCH:NCH + 1], [mb], [], sigM)
        CP(mall[:, 0:1], mall[:, NCH:NCH + 1], [mb], [mb])

        for h in range(HM):
            for ec in range(2):
                pq, pk_ = pb[0], pb[1]
                for dc in range(2):
                    MM(pq[:, 0:T], wq[:, h, dc, ec * 128:(ec + 1) * 128], uhT[:, 2 * h + dc, 0:T], dc == 0, dc == 1,
                       [wq, uhT], [pq])
                for dc in range(2):
                    MM(pk_[:, 0:T], wk[:, h, dc, ec * 128:(ec + 1) * 128], uhT[:, 2 * h + dc, 0:T], dc == 0, dc == 1,
                       [wk, uhT], [pk_])
                CP(qTm[:, h, ec, 0:T], pq[:, 0:T], [pq], [qTm])
                ACT(kTm[:, h, ec, 0:T], pk_[:, 0:T], AF.Copy, [pk_], [kTm], scale=1.0 / 16)
            for j in range(ntl):
                pkt = pb[2 + j % 2]
                for dc in range(2):
                    MM(pkt[0:TSz, 0:256], uhT[:, 2 * h + dc, j * 128:j * 128 + TSz], wk[:, h, dc, :], dc == 0, dc == 1,
                       [uhT, wk], [pkt])
                ACT(kw[0:TSz, j, h, :], pkt[0:TSz, 0:256], AF.Copy, [pkt, wkc], [kw], scale=wkc[0:TSz, j, 4 + h:5 + h])

        for c in range(NCH):
            j, ph = c // 2, 64 * (c % 2)
            t0 = c * 64
            for h in range(HM):
                reg = pms[(c * HM + h) % 6]
                for dc in range(2):
                    MM(reg[ph:ph + 64, :], kTm[:, h, dc, t0:t0 + 64], qTm[:, h, dc, t0:t0 + 64], dc == 0, dc == 1,
                       [kTm, qTm], [reg])
                STT(swA[ph:ph + 64, j, h, :], reg[ph:ph + 64, :], wkc[ph:ph + 64, j, h:h + 1], cmask[ph:ph + 64, :],
                    ALU.mult, ALU.mult, [reg, wkc, cf], [swA])
        for c in range(NCH):
            j, ph = c // 2, 64 * (c % 2)
            t0 = c * 64
            hr = hraw[j % 2]
            for pair in ((0, 1), (2, 3)):
                for h in pair:
                    dec = decbc[:, c, h:h + 1]
                    for dc in range(2):
                        if dc == 0:
                            ACT(Cbf[h][dc].ap, CstH[h][:, dc, :], AF.Copy, [CstH[h], decbc], [Cbf[h][dc]], scale=dec)
                        else:
                            TSM(Cbf[h][dc].ap, CstH[h][:, dc, :], dec, [CstH[h], decbc], [Cbf[h][dc]])
                for i, h in enumerate(pair):
                    for dc in range(2):
                        pU = pb[2 * i + dc]
                        MM(pU[:, 0:257], kw[ph:ph + 64, j, h, dc * 128:(dc + 1) * 128], v_ext[ph:ph + 64, j, h, :],
                           True, True, [kw, v_ext], [pU])
                for i, h in enumerate(pair):
                    dec = decbc[:, c, h:h + 1]
                    for dc in range(2):
                        pU = pb[2 * i + dc]
                        STT(CstH[h][:, dc, :], CstH[h][:, dc, :], dec, pU[:, 0:257], ALU.mult, ALU.add,
                            [CstH[h], decbc, pU], [CstH[h]])
                for i, h in enumerate(pair):
                    pN = pb[4 + i]
                    for dc in range(2):
                        MM(pN[ph:ph + 64, 0:257], qTm[:, h, dc, t0:t0 + 64], Cbf[h][dc].ap, dc == 0, False,
                           [qTm, Cbf[h][dc]], [pN])
                    MM(pN[ph:ph + 64, 0:257], swA[ph:ph + 64, j, h, :], v_ext[ph:ph + 64, j, h, :], False, True,
                       [swA, v_ext], [pN])
                for i, h in enumerate(pair):
                    pN = pb[4 + i]
                    ACT(hr[ph:ph + 64, h, :], pN[ph:ph + 64, 0:257], AF.Copy, [pN], [hr])
            if c % 2 == 1 or c == NCH - 1:
                hb = smallb[4]
                den4 = hr[0:TSz, :, 256]
                STT(small[0:TSz, 8:12], den4, -1.0, den4, ALU.mult, ALU.max, [hr], [hb])
                TT(small[0:TSz, 8:12], small[0:TSz, 8:12], wkc[0:TSz, j, 8:12], ALU.max, [hb, wkc], [hb])
                RCP(small[0:TSz, 8:12], small[0:TSz, 8:12], [hb], [hb])
                TT(hh[0:TSz, :, :], hr[0:TSz, :, 0:256], small[0:TSz, 8:12].unsqueeze(2).to_broadcast([TSz, 4, 256]),
                   ALU.mult, [hr, hb], [hh])
                for h in range(HM):
                    ACT(junk[0:TSz, 0:256], hh[0:TSz, h, :], AF.Square, [hh], [junk, smallb[5]],
                        accum_out=small[0:TSz, 12 + h:13 + h])
                rsqrt_act(small[0:TSz, 12:16], small[0:TSz, 12:16], 1.0 / DH, [smallb[5], cf], [smallb[5]], None)
                for h in range(HM):
                    STT(hmg[0:TSz, h * 256:(h + 1) * 256], hh[0:TSz, h, :], small[0:TSz, 12 + h:13 + h],
                        gmg[0:TSz, j, h * 256:(h + 1) * 256], ALU.mult, ALU.mult, [hh, smallb[5], gmg], [hmg])
                for kc in range(8):
                    TR(ptr[:, kc * 128:kc * 128 + TSz], hmg[0:TSz, kc * 128:(kc + 1) * 128], identb[0:TSz, 0:TSz],
                       [hmg, cb], [ptr])
                ACT(hmT[:, :, j * 128:j * 128 + TSz], ptr.ap.rearrange("p (k t) -> p k t", k=8)[:, :, 0:TSz], AF.Copy,
                    [ptr], [hmT])

        u9 = unit(ucount + 9)
        for k3 in range(3):
            pq = pb[k3 % 2]
            for kc in range(8):
                MM(pq[:, 0:T], u9[:, kc, k3 * 128:(k3 + 1) * 128], hT[:, kc, 0:T], kc == 0, kc == 7, [u9, hT], [pq])
            ACT(cqf[:, k3, 0:T], pq[:, 0:T], AF.Copy, [pq], [cqf])
            ACT(sqb[k3][:, 0:T], pq[:, 0:T], AF.Square, [pq], [sqb[k3]])
        psq = pb[2]
        for k3 in range(3):
            MM(psq[:, 0:T], onesb, sqb[k3][:, 0:T], k3 == 0, k3 == 2, [cb, sqb[k3]], [psq])
        rsqrt_act(rr[0][:, 0:T], psq[:, 0:T], 1.0 / QL, [psq, cf], [rr[0]], None)
        for k3 in range(3):
            STT(cqn[:, k3, 0:T], cqf[:, k3, 0:T], qng_c[:, k3:k3 + 1], rr[0][:, 0:T], ALU.mult, ALU.mult,
                [cqf, rr[0], cf], [cqn])
        MEMSET(qrT[64:128, :, :], 0.0, [qrT], eng="dve")
        for h0 in range(0, HA, 2):
            pair = (h0, h0 + 1)
            for s_, h in enumerate(pair):
                pn, pr = pb[3 * s_], pb[3 * s_ + 1]
                for k3 in range(3):
                    MM(pn[:, 0:T], wuqn[:, k3, h * 128:(h + 1) * 128], cqn[:, k3, 0:T], k3 == 0, k3 == 2, [wuqn, cqn], [pn])
                for k3 in range(3):
                    MM(pr[0:64, 0:T], wuqr[:, k3, h * 64:(h + 1) * 64], cqn[:, k3, 0:T], k3 == 0, k3 == 2, [wuqr, cqn], [pr])
            for s_, h in enumerate(pair):
                pn, pr = pb[3 * s_], pb[3 * s_ + 1]
                ACT(qsq[2 * s_][:, 0:T], pn[:, 0:T], AF.Square, [pn], [qsq[2 * s_]])
                ACT(qsq[2 * s_ + 1][0:64, 0:T], pr[0:64, 0:T], AF.Square, [pr], [qsq[2 * s_ + 1]])
            for s_, h in enumerate(pair):
                ps1, ps2 = pb[3 * s_ + 2], (pmisc if s_ == 0 else pS3[2])
                MM(ps1[:, 0:T], onesb, qsq[2 * s_][:, 0:T], True, True, [cb, qsq[2 * s_]], [ps1])
                MM(ps2[0:64, 0:T], onesb[0:64, 0:64], qsq[2 * s_ + 1][0:64, 0:T], True, True, [cb, qsq[2 * s_ + 1]], [ps2])
            for s_, h in enumerate(pair):
                ps1, ps2 = pb[3 * s_ + 2], (pmisc if s_ == 0 else pS3[2])
                rsqrt_act(qrr[2 * s_][:, 0:T], ps1[:, 0:T], 1.0 / NOPE, [ps1, cf], [qrr[2 * s_]], None)
                rsqrt_act(qrr[2 * s_ + 1][0:64, 0:T], ps2[0:64, 0:T], 1.0 / ROPE, [ps2, cf], [qrr[2 * s_ + 1]], None)
            for s_, h in enumerate(pair):
                pn, pr = pb[3 * s_], pb[3 * s_ + 1]
                STT(qnT[:, h, 0:T], pn[:, 0:T], gqn_c, qrr[2 * s_][:, 0:T], ALU.mult, ALU.mult, [pn, qrr[2 * s_], cf], [qnT])
                STT(qxr[s_][:, 0:T], pr[0:64, 0:T], gqr_c[0:64, :], qrr[2 * s_ + 1][0:64, 0:T], ALU.mult, ALU.mult,
                    [pr, qrr[2 * s_ + 1], cf], [qxr[s_]])
            for s_, h in enumerate(pair):
                prr = pb[3 * s_ + 2]
                MM(prr[0:64, 0:T], RTb, qxr[s_][:, 0:T], True, True, [cb, qxr[s_]], [prr])
            for s_, h in enumerate(pair):
                prr = pb[3 * s_ + 2]
                t1, t2 = qtc[2 * s_], qtc[2 * s_ + 1]
                TT(t1[0:64, 0:T], qxr[s_][:, 0:T], cosT[:, 0:T], ALU.mult, [qxr[s_], cosT], [t1])
                TT(t2[0:64, 0:T], prr[0:64, 0:T], sinT[:, 0:T], ALU.mult, [prr, cosT], [t2])
                TT(qrT[0:64, h, 0:T], t1[0:64, 0:T], t2[0:64, 0:T], ALU.add, [t1, t2], [qrT])

        kv_block(si, kb_cur, T, write_scratch)
        attention(si, T, nkb_past, ucount)
        if debug and last_block:
            DMA("sp", dbg_d["qn"], qnT.ap, [qnT], [], dbg_sig)
            DMA("sp", dbg_d["kn"], knT.ap, [knT], [], dbg_sig)

        for j in range(ntl):
            r0 = row0 + j * 128
            DMA("sp", xres[j][0:TSz, :], xs[r0:r0 + TSz, :], [], [xres[j]], xres_sig[j])
        for half in range(2):
            ub = ucount + 12 + half * 4
            for which, dst in ((0, sgm), (1, sga)):
                u = unit(ub + which)
                for c in range(4):
                    pg = pb[c % 2]
                    for kc in range(8):
                        MM(pg[:, 0:T], u[:, kc, c * 128:(c + 1) * 128], hT[:, kc, 0:T], kc == 0, kc == 7, [u, hT], [pg])
                    e = td[c % 2]
                    ACT(e[:, 0:T], pg[:, 0:T], AF.Exp, [pg], [e], scale=-1.0)
                    ACT(e[:, 0:T], e[:, 0:T], AF.Ln, [e], [e], bias=1.0)
                    ACT(dst[:, c, 0:T], e[:, 0:T], AF.Exp, [e], [dst], scale=-1.0)
            u = unit(ub + 2)
            for c in range(4):
                pg = pb[2 + c % 2]
                for kc in range(8):
                    MM(pg[:, 0:T], u[:, kc, c * 128:(c + 1) * 128], hmT[:, kc, 0:T], kc == 0, kc == 7, [u, hmT], [pg])
                TT(tmpm[:, c, 0:T], pg[:, 0:T], sgm[:, c, 0:T], ALU.mult, [pg, sgm], [tmpm])
            u = unit(ub + 3)
            for c in range(4):
                pg = pb[c % 2]
                for kc in range(8):
                    MM(pg[:, 0:T], u[:, kc, c * 128:(c + 1) * 128], haT[:, kc, 0:T], kc == 0, kc == 7, [u, haT], [pg])
                e = td[c % 2]
                TT(e[:, 0:T], pg[:, 0:T], sga[:, c, 0:T], ALU.mult, [pg, sga], [e])
                TT(mergedT[:, half * 4 + c, 0:T], e[:, 0:T], tmpm[:, c, 0:T], ALU.add, [e, tmpm], [mergedT])
        uo0 = unit(ucount + 20)
        uo1 = unit(ucount + 21, oldest=ucount + 20)
        for j in range(ntl):
            xi = xres[j]
            r0 = row0 + j * 128
            yo = yout[j % 2]
            for g, u in ((0, uo0), (1, uo1)):
                py = pb[2 + g]
                for kc in range(8):
                    MM(py[0:TSz, :], mergedT[:, kc, j * 128:j * 128 + TSz], u[:, kc, :], kc == 0, kc == 7,
                       [mergedT, u], [py])
                TT(yo[0:TSz, g * 512:(g + 1) * 512], py[0:TSz, :], xi[0:TSz, g * 512:(g + 1) * 512], ALU.add,
                   [py, xi], [yo])
            DMA("sp", y_d[r0:r0 + TSz, :], yo[0:TSz, :], [yo], [], yo_sig[j % 2])

    nblocks = sum((L + 511) // 512 for (_, L, _, _) in seqs)
    ring_state["total"] = nblocks * NUNIT
    ucount = 0
    for si, (kind, L, row0, npast) in enumerate(seqs):
        mb = P.buf("mallb") if "mallb" not in kscr_b else kscr_b["mallb"]
        kscr_b["mallb"] = mb
        if npast == 0:
            MEMSET(Cst.ap, 0.0, CstH, eng="dve")
            MEMSET(hist.ap, 0.0, [hist], eng="dve")
            MEMSET(mall[:, 0:1], 0.0, [mb], eng="dve")
        else:
            DMA("sp", Cst[:, :, :, 0:256], sC_d.rearrange("h (dc p) e -> p h dc e", p=128), [], CstH, sigC)
            for h in range(HM):
                DMA("sp", Cst[:, h, :, 256], sn_d[h].rearrange("(dc p) -> p dc", p=128), [], CstH, sigC,
                    allow_slow_non_contiguous=True)
            for jj in range(3):
                DMA("sp", hist[:, :, jj], sconv_d[jj].rearrange("(c p) -> p c", p=128), [], [hist], sigH,
                    allow_slow_non_contiguous=True)
            DMA("sp", mall[:, 0:1], sm_d, [], [mb], sigM)
            for kb in range(npast // 512):
                for jt in range(4):
                    k0 = kb * 512 + jt * 128
                    DMA("sp", pcb.ap, cckv_bf[k0:k0 + 128, :], [wbfB], [pcb], pcb_sig)
                    DMA("sp", pcr.ap, ckr_bf[k0:k0 + 128, :], [wbfB], [pcr], pcr_sig)
                    for c2 in range(2):
                        TR(ptr[:, c2 * 128:(c2 + 1) * 128], pcb[:, c2 * 128:(c2 + 1) * 128], identb, [pcb, cb], [ptr])
                    TR(ptr[0:64, 256:384], pcr.ap, identb, [pcr, cb], [ptr])
                    CP(ckvnT[:, :, jt * 128:(jt + 1) * 128], ptr.ap.rearrange("p (k t) -> p k t", k=8)[:, 0:2, :],
                       [ptr], [ckvnT])
                    CP(krT[0:64, jt * 128:(jt + 1) * 128], ptr[0:64, 256:384], [ptr], [krT])
                kv_block(si, kb, 512, True)
        nb = (L + 511) // 512
        for b in range(nb):
            T = min(512, L - b * 512)
            block(si, kind, row0 + b * 512, T, npast + b * 512, npast // 512 + b, npast // 512 + b, ucount,
                  b == nb - 1, b < nb - 1)
            ucount += NUNIT
        DMA("sp", C_o[si].rearrange("h (dc p) e -> p h dc e", p=128), Cst[:, :, :, 0:256], CstH, [], sigC)
        for h in range(HM):
            DMA("sp", n_o[si, h].rearrange("(dc p) -> p dc", p=128), Cst[:, h, :, 256], CstH, [], sigC,
                allow_slow_non_contiguous=True)
        for jj in range(3):
            DMA("sp", conv_o[si, jj].rearrange("(c p) -> p c", p=128), hist[:, :, jj], [hist], [], sigH,
                allow_slow_non_contiguous=True)

    if debug:
        for n, lb in (("hm", hmT), ("ha", haT), ("mg", mergedT), ("hT", hT)):
            DMA("sp", dbg_d[n], lb.ap, [lb], [], dbg_sig)
    stats = P.lower()
    es.close()
    return nc, stats


def _rope_tables():
    half = ROPE // 2
    inv = np.power(np.float32(10000.0), -np.arange(half, dtype=np.float32) / np.float32(half)).astype(np.float32)
    pos = np.arange(NPOS, dtype=np.float32)
    ang = (pos[:, None] * inv[None, :]).astype(np.float32)
    cos = np.cos(ang.astype(np.float64)).astype(np.float32)
    sin = np.sin(ang.astype(np.float64)).astype(np.float32)
    cs_tm = np.concatenate([cos, cos], axis=1)
    sn_tm = np.concatenate([-sin, sin], axis=1)
    return (np.ascontiguousarray(cs_tm.T), np.ascontiguousarray(np.concatenate([sin, sin], axis=1).T),
            np.ascontiguousarray(cs_tm), np.ascontiguousarray(sn_tm))


def _prep_shared(norm_g, w_in, b_if, conv_w, conv_b, wq_m, wk_m, hnorm_g, qn_g, w_uq, kvn_g, w_ukv,
                 g_qn, g_qr, g_kn, g_kr, w_pm, w_pa, w_out):
    f = np.float32
    w_in, w_pm, w_pa, w_out = w_in[0], w_pm[0], w_pa[0], w_out[0]
    o = np.cumsum([0, 1024, 1024, 4, 4, 1024, 1024, QL, KVL, ROPE, 1024, 1024, 1024])
    seg = {n: (o[i], o[i + 1]) for i, n in enumerate(
        ["xc", "vm", "ig", "fg", "op", "zm", "cq", "ckv", "kr", "za", "gm", "ga"])}

    def cols(n, a=0, b=None):
        s, e = seg[n]
        return w_in[:, s + a:(e if b is None else s + b)]

    z = lambda n: np.zeros((1024, n), f)
    units = [cols("xc", 0, 512), cols("xc", 512, 1024), cols("vm", 0, 512), cols("vm", 512, 1024),
             cols("op", 0, 512), cols("zm", 0, 512), cols("op", 512, 1024), cols("zm", 512, 1024),
             np.concatenate([cols("ckv"), cols("kr"), cols("ig"), cols("fg"), z(512 - 328)], axis=1),
             np.concatenate([cols("cq"), z(512 - QL)], axis=1),
             cols("za", 0, 512), cols("za", 512, 1024),
             cols("gm", 0, 512), cols("ga", 0, 512), w_pm[:, 0:512], w_pa[:, 0:512],
             cols("gm", 512, 1024), cols("ga", 512, 1024), w_pm[:, 512:1024], w_pa[:, 512:1024],
             w_out[:, 0:512], w_out[:, 512:1024]]
    assert len(units) == NUNIT
    wcat = np.stack([u.reshape(8, 128, 512).transpose(1, 0, 2) for u in units]).astype(f)
    wuq = w_uq[0].reshape(3, 128, HA, NOPE + ROPE).transpose(1, 0, 2, 3)
    wuqn = np.ascontiguousarray(wuq[..., :NOPE].reshape(128, 3, HA * NOPE))
    wuqr = np.ascontiguousarray(wuq[..., NOPE:].reshape(128, 3, HA * ROPE))
    wukv = w_ukv[0].reshape(2, 128, HA, NOPE + VD).transpose(1, 0, 2, 3)
    wukvk = np.ascontiguousarray(wukv[..., :NOPE].reshape(128, 2, HA * NOPE))
    wukvv = np.ascontiguousarray(wukv[..., NOPE:].reshape(128, 2, HA * VD))
    wq = np.ascontiguousarray(wq_m[0].reshape(HM, 2, 128, DH).transpose(2, 0, 1, 3))
    wk = np.ascontiguousarray(wk_m[0].reshape(HM, 2, 128, DH).transpose(2, 0, 1, 3))
    cf = np.zeros((128, 512), f)
    cf[:, 0:128] = np.eye(128, dtype=f)
    s_ = np.arange(128)[:, None]
    t_ = np.arange(128)[None, :]
    cf[:, 128:256] = ((s_ <= t_) & (s_ // 64 == t_ // 64)).astype(f)
    cf[:, 256:320] = ((s_ % 64) <= np.arange(64)[None, :]).astype(f)
    cf[:, 320:328] = norm_g[0].reshape(8, 128).T
    cf[:, 328:360] = conv_w[0].reshape(4, 8, 128).transpose(2, 1, 0).reshape(128, 32)
    cf[:, 360:368] = conv_b[0].reshape(8, 128).T
    cf[:, 368:371] = qn_g[0].reshape(3, 128).T
    cf[:, 371] = g_qn[0]
    cf[:, 372] = g_kn[0]
    cf[0:64, 373] = g_qr[0]
    cf[:, 374] = EPS
    cf[0:4, 384:512] = 1.0
    cbm = np.zeros((128, 448), f)
    cbm[:, 0:128] = np.eye(128, dtype=f)
    cbm[:, 128:256] = 1.0
    RT = np.zeros((64, 64), f)
    for i in range(32):
        RT[i + 32, i] = -1.0
        RT[i, i + 32] = 1.0
    cbm[0:64, 256:320] = RT
    bcm = np.concatenate([hnorm_g[0].reshape(-1), kvn_g[0], g_kr[0], b_if[0]]).astype(f)
    bcm = np.ascontiguousarray(np.broadcast_to(bcm[None, :], (128, bcm.shape[0])))
    cosT, sinT, cstm, sntm = _rope_tables()
    return dict(wcat=wcat, wuqn=wuqn, wuqr=wuqr, wukvk=wukvk, wukvv=wukvv, wq=wq, wk=wk, cf=cf, cb=cbm, bc=bcm,
                cosT=cosT, sinT=sinT, cstm=cstm, sntm=sntm)


_CACHE = {}


def kernel(x_prompt, x_sample, cache_ckv, cache_kr, state_conv, state_C, state_n, state_m,
           norm_g, w_in, b_if, conv_w, conv_b, wq_m, wk_m, hnorm_g,
           qn_g, w_uq, kvn_g, w_ukv, g_qn, g_qr, g_kn, g_kr, w_pm, w_pa, w_out):
    A = lambda a: np.ascontiguousarray(np.asarray(a, dtype=np.float32))
    x_prompt, x_sample = A(x_prompt), A(x_sample)
    B, S, _ = x_prompt.shape
    DB, DS, _ = x_sample.shape
    NCORE = 8
    bpc = B // NCORE
    P_ = np.asarray(cache_ckv).shape[2]
    seqs = [("prompt", S, i * S, 0) for i in range(bpc)] + [("sample", DS, bpc * S, P_)]
    ntok = bpc * S + DS
    shared = _prep_shared(*[A(a) for a in (norm_g, w_in, b_if, conv_w, conv_b, wq_m, wk_m, hnorm_g, qn_g, w_uq, kvn_g,
                                            w_ukv, g_qn, g_qr, g_kn, g_kr, w_pm, w_pa, w_out)])
    key = (tuple(seqs), ntok)
    if key not in _CACHE:
        _CACHE[key] = build(seqs, ntok)[0]
    nc = _CACHE[key]
    in_maps = []
    for c in range(NCORE):
        xs = np.concatenate([x_prompt[c * bpc:(c + 1) * bpc].reshape(bpc * S, D), x_sample[c]], axis=0)
        m = dict(shared)
        m.update(xs=np.ascontiguousarray(xs), cckv=A(cache_ckv)[0, c], ckr=A(cache_kr)[0, c],
                 sconv=A(state_conv)[0, c], sC=A(state_C)[0, c], sn=A(state_n)[0, c],
                 sm=A(state_m)[0, c].reshape(HM, 1))
        in_maps.append(m)
    res = run_bass_kernel_spmd(nc, in_maps, core_ids=list(range(NCORE)))
    R = res.results
    cat = lambda k: [r[k] for r in R]
    yp = np.stack([r["y"][:bpc * S].reshape(bpc, S, D) for r in R]).reshape(B, S, D)
    ys = np.stack([r["y"][bpc * S:] for r in R])
    ckv_p = np.stack([r["ckv_o"][:bpc * S].reshape(bpc, S, KVL) for r in R]).reshape(1, B, S, KVL)
    ckv_s = np.stack([r["ckv_o"][bpc * S:] for r in R])[None]
    kr_p = np.stack([r["kr_o"][:bpc * S].reshape(bpc, S, ROPE) for r in R]).reshape(1, B, S, ROPE)
    kr_s = np.stack([r["kr_o"][bpc * S:] for r in R])[None]
    conv_p = np.stack([r["conv_o"][:bpc] for r in R]).reshape(1, B, 3, D)
    conv_s = np.stack([r["conv_o"][bpc] for r in R])[None]
    C_p = np.stack([r["C_o"][:bpc] for r in R]).reshape(1, B, HM, DH, DH)
    C_s = np.stack([r["C_o"][bpc] for r in R])[None]
    n_p = np.stack([r["n_o"][:bpc] for r in R]).reshape(1, B, HM, DH)
    n_s = np.stack([r["n_o"][bpc] for r in R])[None]
    m_p = np.stack([r["m_o"][:bpc] for r in R]).reshape(1, B, HM)
    m_s = np.stack([r["m_o"][bpc] for r in R]).reshape(1, DB, HM)
    f = np.float32
    return tuple(np.ascontiguousarray(a, dtype=f) for a in
                 (yp, ys, ckv_p, kr_p, conv_p, C_p, n_p, m_p, ckv_s, kr_s, conv_s, C_s, n_s, m_s))
```

```python
import numpy as np
from contextlib import ExitStack
import concourse.bass as bass
import concourse.mybir as mybir
from concourse.bass_utils import run_bass_kernel_spmd

F32 = mybir.dt.float32
BF16 = mybir.dt.bfloat16
AF = mybir.ActivationFunctionType
ALU = mybir.AluOpType
AX = mybir.AxisListType

D = 1024
HM, DH = 4, 256
HA, NOPE, ROPE, VD = 8, 128, 64, 128
QL, KVL = 384, 256
EPS = 1e-6
ATT_SCALE = float((NOPE + ROPE) ** -0.5)
NUNIT = 22
RING = 3
NKVS = 4
NONLEGACY = ("A.", "C.", "D.")
NPOS = 4096 + 1024


class Sig:
    def __init__(self, sem, unit, name):
        self.sem, self.unit, self.name, self.n = sem, unit, name, 0


class Buf:
    def __init__(self, name, rng=None):
        self.name, self.rng = name, rng
        self.w = None
        self.r = {}
        self.over = []
        self.legacy = True
        self.psum = False


class Op:
    __slots__ = ("eng", "meth", "kw", "deps", "sig", "inc", "val", "dma", "w_bufs")

    def __init__(self, eng, meth, kw, sig, dma):
        self.eng, self.meth, self.kw, self.sig, self.dma = eng, meth, kw, sig, dma
        self.deps, self.inc, self.val = [], dma, 0


class Prog:
    def __init__(self, nc, es):
        self.nc, self.es = nc, es
        self.h = {"pe": nc.tensor, "act": nc.scalar, "dve": nc.vector, "pool": nc.gpsimd, "sp": nc.sync}
        self.sig = {k: Sig(es.enter_context(nc.semaphore("s_" + k)), 1, k) for k in ("pe", "act", "dve", "pool")}
        self.ops = []
        self.bufs = []
        self.dsigs = []

    def buf(self, name, rng=None):
        b = Buf(name, rng)
        if rng is not None:
            for o in self.bufs:
                if o.rng is not None and o.rng[0] == rng[0] and o.rng[1] < rng[2] and rng[1] < o.rng[2]:
                    o.over.append(b)
                    b.over.append(o)
        self.bufs.append(b)
        if any(name.startswith(p) for p in NONLEGACY):
            b.legacy = False
        return b

    def dma_sig(self, name):
        s = Sig(self.es.enter_context(self.nc.semaphore("d_" + name)), 16, name)
        self.dsigs.append(s)
        return s

    def _need(self, o, p, raw):
        if p is o:
            return False
        if not o.dma and not p.dma and o.eng == p.eng:
            return raw == "raw" and o.eng != "pe"
        if o.dma and p.dma and o.sig is p.sig and raw == "waw":
            return False
        return True

    def op(self, eng, meth, kw, reads=(), writes=(), sig=None):
        dma = sig is not None
        o = Op(eng, meth, kw, sig if dma else self.sig[eng], dma)
        deps = {}
        for b in reads:
            for x in [b] + b.over:
                if x.w is not None and self._need(o, x.w, "raw"):
                    deps[id(x.w)] = x.w
                if x.psum:
                    for r in x.r.values():
                        if r.eng != o.eng:
                            deps[id(r)] = r
        for b in writes:
            for x in [b] + b.over:
                for r in x.r.values():
                    if self._need(o, r, "war"):
                        deps[id(r)] = r
                if x.w is not None and self._need(o, x.w, "waw"):
                    deps[id(x.w)] = x.w
        for b in reads:
            for x in ([b] + b.over) if b.legacy else [b]:
                x.r[id(o.sig)] = o
        for b in writes:
            for x in ([b] + b.over) if b.legacy else [b]:
                x.w = o
                x.r = {}
        o.deps = list(deps.values())
        self.ops.append(o)
        return o

    def lower(self):
        for o in self.ops:
            for d in o.deps:
                d.inc = True
        for o in self.ops:
            if o.inc:
                o.sig.n += o.sig.unit
                o.val = o.sig.n
        seen = {k: {} for k in self.h}
        nwait = 0
        for o in self.ops:
            hd = self.h[o.eng]
            sn = seen[o.eng]
            best = {}
            for d in o.deps:
                k = id(d.sig)
                if sn.get(k, 0) < d.val and best.get(k, (0, None))[0] < d.val:
                    best[k] = (d.val, d.sig)
            for k, (v, s) in best.items():
                hd.wait_ge(s.sem, v)
                sn[k] = v
                nwait += 1
            ins = getattr(hd, o.meth)(**o.kw)
            if o.inc:
                ins.then_inc(o.sig.sem, o.sig.unit)
        for s in self.dsigs:
            if s.n > 0:
                self.nc.sync.wait_ge(s.sem, s.n)
        return len(self.ops), nwait


class LB:
    def __init__(self, ap, buf):
        self.ap, self.buf = ap, buf

    def __getitem__(self, k):
        return self.ap[k]


def build(seqs, ntok, debug=False):
    nc = bass.Bass("TRN2", target_bir_lowering=False)
    es = ExitStack()
    nseq = len(seqs)
    maxkeys = max(L + npast for (_, L, _, npast) in seqs)
    maxkeys = ((maxkeys + 511) // 512) * 512

    def din(name, shape, dt=F32):
        return nc.dram_tensor(name, list(shape), dt, kind="ExternalInput").ap()

    def dout(name, shape, dt=F32):
        return nc.dram_tensor(name, list(shape), dt, kind="ExternalOutput").ap()

    xs = din("xs", [ntok, D])
    wcat = din("wcat", [NUNIT, 128, 8, 512])
    wuqn_d = din("wuqn", [128, 3, 1024])
    wuqr_d = din("wuqr", [128, 3, 512])
    wukvk_d = din("wukvk", [128, 2, 1024])
    wukvv_d = din("wukvv", [128, 2, 1024])
    wq_d = din("wq", [128, 4, 2, 256])
    wk_d = din("wk", [128, 4, 2, 256])
    cf_d = din("cf", [128, 512])
    cb_d = din("cb", [128, 448])
    bc_d = din("bc", [128, 1024 + 256 + 64 + 8])
    cosT_d = din("cosT", [64, NPOS])
    sinT_d = din("sinT", [64, NPOS])
    cstm_d = din("cstm", [NPOS, 64])
    sntm_d = din("sntm", [NPOS, 64])
    cckv_d = din("cckv", [1024, KVL])
    ckr_d = din("ckr", [1024, ROPE])
    sconv_d = din("sconv", [3, D])
    sC_d = din("sC", [HM, DH, DH])
    sn_d = din("sn", [HM, DH])
    sm_d = din("sm", [HM, 1])

    y_d = dout("y", [ntok, D])
    ckv_o = dout("ckv_o", [ntok, KVL])
    kr_o = dout("kr_o", [ntok, ROPE])
    conv_o = dout("conv_o", [nseq, 3, D])
    C_o = dout("C_o", [nseq, HM, DH, DH])
    n_o = dout("n_o", [nseq, HM, DH])
    m_o = dout("m_o", [nseq, HM, 1])

    if debug:
        dbg_d = {n: nc.dram_tensor("dbg_" + n, [128, 8, 512], BF16, kind="ExternalOutput").ap() for n in ("hm", "ha", "mg", "qn", "kn", "hT")}
    wbf = nc.dram_tensor("wbf", [NUNIT, 128, 8 * 512], BF16, kind="Internal").ap()
    cckv_bf = nc.dram_tensor("cckv_bf", [1024, KVL], BF16, kind="Internal").ap()
    ckr_bf = nc.dram_tensor("ckr_bf", [1024, ROPE], BF16, kind="Internal").ap()
    kscr = nc.dram_tensor("kscr", [HA, 128, maxkeys], BF16, kind="Internal").ap()
    vscr = nc.dram_tensor("vscr", [HA, 128, maxkeys // 128, 128], BF16, kind="Internal").ap()
    rscr = nc.dram_tensor("rscr", [128, maxkeys], BF16, kind="Internal").ap()

    P = Prog(nc, es)
    dbg_sig = P.dma_sig("dbg") if debug else None

    def sbt(name, shape, dt):
        return es.enter_context(nc.sbuf_tensor(name, list(shape), dt))

    def pst(name, shape, dt):
        return es.enter_context(nc.psum_tensor(name, list(shape), dt))

    def pers(name, shape, dt):
        t = sbt(name, shape, dt)
        return LB(t[:], P.buf(name))

    ARENA_BYTES = 68 * 1024
    arena = sbt("arena", [128, ARENA_BYTES // 2], BF16)
    aoff = {}

    aofs = {}

    def ar(phase, name, shape, dt, at=None):
        nb = int(np.prod(shape[1:])) * (4 if dt == F32 else 2)
        nb = (nb + 63) // 64 * 64
        o = aoff.get(phase, 0) if at is None else at
        assert o + nb <= ARENA_BYTES, (phase, name, o + nb)
        if at is None:
            aoff[phase] = o + nb
        aofs[name] = o
        ap = arena[0:shape[0], o // 2:(o + nb) // 2]
        n_el = int(np.prod(shape[1:]))
        if dt == F32:
            ap = ap.bitcast(F32)[:, 0:n_el]
        else:
            ap = ap[:, 0:n_el]
        if len(shape) > 2:
            names = " ".join("d%d" % i for i in range(1, len(shape)))
            ap = ap.rearrange("p (%s) -> p %s" % (names, names), **{"d%d" % i: shape[i] for i in range(2, len(shape))})
        return LB(ap, P.buf(phase + "." + name, ("arena", o, o + nb)))

    pb = []
    for i in range(6):
        t = pst("pb%d" % i, [128, 512], F32)
        pb.append(LB(t[:], P.buf("pb%d" % i)))
        pb[-1].buf.psum = True
    t = pst("ptr", [128, 1024], BF16)
    ptr = LB(t[:], P.buf("ptr"))
    ptr.buf.psum = True
    t = pst("pmisc", [128, 512], F32)
    pmisc = LB(t[:], P.buf("pmisc"))
    pmisc.buf.psum = True
    pms = [LB(pb[k].ap[:, 0:64], pb[k].buf) for k in range(6)]

    ring = [pers("ring%d" % i, [128, 8, 512], BF16) for i in range(RING)]
    ring_sig = [P.dma_sig("ring%d" % i) for i in range(RING)]
    xin = [pers("xin%d" % i, [128, D], F32) for i in range(2)]
    xin_sig = [P.dma_sig("xin%d" % i) for i in range(2)]
    xn = pers("xn", [128, D], BF16)
    junk = pers("junk", [128, D], BF16)
    hT = pers("hT", [128, 8, 512], BF16)
    hmT = pers("hmT", [128, 8, 512], BF16)
    haT = pers("haT", [128, 8, 512], BF16)
    Cst_t = sbt("Cst", [128, HM, 2, 257], F32)
    Cst = LB(Cst_t[:], None)
    CstH = [LB(Cst_t[:, h], P.buf("Cst%d" % h)) for h in range(HM)]
    Cbf = [[pers("Cbf%d_%d" % (h, dc), [128, 257], BF16) for dc in range(2)] for h in range(HM)]
    hist = pers("hist", [128, 8, 3], F32)
    wuqn = pers("wuqn_s", [128, 3, 1024], BF16)
    wuqr = pers("wuqr_s", [128, 3, 512], BF16)
    wukvk = pers("wukvk_s", [128, 2, 1024], BF16)
    wukvv = pers("wukvv_s", [128, 2, 1024], BF16)
    wq = pers("wq_s", [128, 4, 2, 256], BF16)
    wk = pers("wk_s", [128, 4, 2, 256], BF16)
    cf = pers("cf_s", [128, 512], F32)
    cb = pers("cb_s", [128, 448], BF16)
    bc = pers("bc_s", [128, 1024 + 256 + 64 + 8], F32)
    cosT = pers("cosT_s", [64, 512], F32)
    sinT = pers("sinT_s", [64, 512], F32)
    cstm = pers("cstm_s", [128, 4, 64], F32)
    sntm = pers("sntm_s", [128, 4, 64], F32)
    tab_sig = P.dma_sig("tab")
    small = pers("small", [128, 64], F32)
    smallb = [P.buf("small%d" % i) for i in range(8)]
    gcol = pers("gcol", [128, 4, 16], F32)
    grow = pers("grow", [4, 2, 512], F32)
    mall = pers("mall", [4, 40], F32)
    decbd = pers("decbd", [4, 8, 4], F32)
    decbc = pers("decbc", [128, 8, 4], F32)
    wkc = pers("wkc", [128, 4, 12], F32)
    init_sig = P.dma_sig("init")
    sigC = P.dma_sig("stC")
    sigH = P.dma_sig("stH")
    sigM = P.dma_sig("stM")

    identf = cf[:, 0:128]
    tri = cf[:, 128:256]
    cmask = cf[:, 256:320]
    ng_c = cf[:, 320:328]
    cw_c = cf[:, 328:360]
    cbias_c = cf[:, 360:368]
    qng_c = cf[:, 368:371]
    gqn_c = cf[:, 371:372]
    gkn_c = cf[:, 372:373]
    gqr_c = cf[:, 373:374]
    eye4 = cf[0:4, 0:4]
    ones4 = cf[0:4, 384:512]
    identb = cb[:, 0:128]
    onesb = cb[:, 128:256]
    RTb = cb[0:64, 256:320]
    hng_bc = bc[:, 0:1024]
    kvng_bc = bc[:, 1024:1280]
    gkr_bc = bc[:, 1280:1344]
    bif_bc = bc[:, 1344:1352]

    def bl(xs_):
        return [x.buf if isinstance(x, LB) else x for x in xs_]

    def MM(out, lhsT, rhs, start, stop, R, W):
        P.op("pe", "matmul", dict(out=out, lhsT=lhsT, rhs=rhs, start=start, stop=stop), bl(R), bl(W))

    def TR(out, in_, ident, R, W):
        P.op("pe", "transpose", dict(out=out, in_=in_, identity=ident), bl(R), bl(W))

    def ACT(out, in_, func, R, W, **kw):
        P.op("act", "activation", dict(out=out, in_=in_, func=func, **kw), bl(R), bl(W))

    def TT(out, in0, in1, op, R, W, eng="dve"):
        P.op(eng, "tensor_tensor", dict(out=out, in0=in0, in1=in1, op=op), bl(R), bl(W))

    def TS(out, in0, s1, s2, op0, op1, R, W, eng="dve"):
        P.op(eng, "tensor_scalar", dict(out=out, in0=in0, scalar1=s1, scalar2=s2, op0=op0, op1=op1), bl(R), bl(W))

    def TSA(out, in0, s1, R, W, eng="dve"):
        P.op(eng, "tensor_scalar_add", dict(out=out, in0=in0, scalar1=s1), bl(R), bl(W))

    def TSM(out, in0, s1, R, W, eng="dve"):
        P.op(eng, "tensor_scalar_mul", dict(out=out, in0=in0, scalar1=s1), bl(R), bl(W))

    def STT(out, in0, scalar, in1, op0, op1, R, W, eng="dve"):
        P.op(eng, "scalar_tensor_tensor", dict(out=out, in0=in0, scalar=scalar, in1=in1, op0=op0, op1=op1), bl(R), bl(W))

    def CP(out, in_, R, W, eng="dve"):
        P.op(eng, "tensor_copy", dict(out=out, in_=in_), bl(R), bl(W))

    def RCP(out, in_, R, W):
        P.op("dve", "reciprocal", dict(out=out, in_=in_), bl(R), bl(W))

    def MEMSET(ap, v, W, eng="dve"):
        P.op(eng, "memset", dict(ap=ap, constant=v), [], bl(W))

    def DMA(q, out, in_, R, W, sig, **kw):
        P.op(q, "dma_start", dict(out=out, in_=in_, **kw), bl(R), bl(W), sig=sig)

    def rsqrt_act(out, in_, scale, R, W, tmpbuf):
        ACT(out, in_, AF.Ln, R, W, scale=scale, bias=eps_ap(out))
        ACT(out, out, AF.Exp, W, W, scale=-0.5)

    epsb = P.buf("epsb")

    def eps_ap(like):
        np_ = like.shape[0]
        p0 = like.base_partition() if hasattr(like, "base_partition") else 0
        return cf[p0:p0 + np_, 374:375]

    init_sig2 = P.dma_sig("init2")
    DMA("sp", cf.ap, cf_d, [], [cf], init_sig2)
    DMA("sp", bc.ap, bc_d, [], [bc], init_sig2)
    for b_ in (cf, bc):
        b_.buf.w = P.ops[-1]
    DMA("pool", cb.ap, cb_d, [], [cb], init_sig)
    DMA("pool", wuqn.ap, wuqn_d, [], [wuqn], init_sig)
    DMA("pool", wuqr.ap, wuqr_d, [], [wuqr], init_sig)
    DMA("pool", wukvk.ap, wukvk_d, [], [wukvk], init_sig)
    DMA("pool", wukvv.ap, wukvv_d, [], [wukvv], init_sig)
    DMA("pool", wq.ap, wq_d, [], [wq], init_sig)
    DMA("pool", wk.ap, wk_d, [], [wk], init_sig)
    for b_ in (cb, wuqn, wuqr, wukvk, wukvv, wq, wk):
        b_.buf.w = P.ops[-1]

    wbf_sig = P.dma_sig("wbf")
    wbfB = P.buf("wbfB")
    wbfA_sig = P.dma_sig("wbfA")
    wbfA = P.buf("wbfA")
    NEARLY = 4
    for k_ in range(NUNIT):
        DMA("pool", wbf[k_], wcat[k_].rearrange("p k c -> p (k c)"), [], [wbfA if k_ < NEARLY else wbfB],
            wbfA_sig if k_ < NEARLY else wbf_sig)
    DMA("pool", cckv_bf, cckv_d, [], [wbfB], wbf_sig)
    DMA("pool", ckr_bf, ckr_d, [], [wbfB], wbf_sig)
    ring_state = {"issued": 0, "total": 0}

    def ring_prefetch(upto):
        while ring_state["issued"] < min(upto + 1, ring_state["total"]):
            k = ring_state["issued"]
            s = k % RING
            DMA("sp", ring[s].ap, wbf[k % NUNIT].rearrange("p (k c) -> p k c", k=8),
                [wbfA if (k % NUNIT) < NEARLY else wbfB], [ring[s]], ring_sig[s])
            ring_state["issued"] += 1

    def unit(k, oldest=None):
        ring_prefetch((k if oldest is None else oldest) + RING - 1)
        return ring[k % RING]

    uhT = ar("A", "uhT", [128, 8, 512], BF16)
    qTm = ar("A", "qTm", [128, 4, 2, 512], BF16)
    kTm = ar("A", "kTm", [128, 4, 2, 512], BF16)
    kw = ar("A", "kw", [128, 4, 4, 256], BF16)
    gmg = ar("A", "gmg", [128, 4, 1024], BF16)
    xcb = [ar("A", "xcb%d" % i, [128, 515], F32) for i in range(2)]
    ucv = [ar("A", "ucv%d" % i, [128, 512], F32) for i in range(2)]
    ta = [ar("A", "ta%d" % i, [128, 512], F32) for i in range(3)]
    _o = aofs["xcb0"]
    hraw = [ar("A", "hraw%d" % i, [128, 4, 257], F32, at=_o + i * 4160) for i in range(2)]
    hh = ar("A", "hh", [128, 4, 256], F32, at=_o + 8320)
    hmg = ar("A", "hmg", [128, 1024], BF16, at=_o + 8320 + 4096)
    assert _o + 8320 + 4096 + 2048 <= aoff["A"]
    qnT = ar("C", "qnT", [128, 8, 512], BF16)
    qrT = ar("C", "qrT", [128, 8, 512], BF16)
    knT = ar("C", "knT", [128, 8, 512], BF16)
    Vcur = ar("C", "Vcur", [128, 4, 1024], BF16)
    cqf = ar("C", "cqf", [128, 3, 512], F32)
    cqn = ar("C", "cqn", [128, 3, 512], BF16)
    sqb = [ar("C", "sqb%d" % i, [128, 512], BF16) for i in range(3)]
    rr = [ar("C", "rr%d" % i, [128, 512], F32) for i in range(2)]
    tc_ = [ar("C", "tc%d" % i, [128, 512], F32) for i in range(2)]
    xrb = ar("C", "xrb", [64, 512], BF16)
    ckvnT = pers("ckvnT", [128, 2, 512], BF16)
    krT = pers("krT", [128, 512], BF16)
    ckvf = [ar("A", "ckvf%d" % i, [128, 256], F32) for i in range(2)]
    ckvb = ar("A", "ckvb", [128, 256], BF16)
    krf = [ar("A", "krf%d" % i, [128, 64], F32) for i in range(2)]
    krw = [ar("A", "krw%d" % i, [128, 64], F32) for i in range(3)]
    krb = ar("A", "krb", [128, 64], BF16)
    kvs = [dict(K=ar("C", "kvK%d" % i, [128, 512], BF16), R=ar("C", "kvR%d" % i, [128, 512], BF16),
                V=ar("C", "kvV%d" % i, [128, 4, 128], BF16)) for i in range(NKVS)]
    kvs_sig = [P.dma_sig("kvs%d" % i) for i in range(NKVS)]
    _k0 = aofs["kvK0"]
    qsq = [sqb[0], sqb[1], sqb[2], ar("C", "qsq3", [128, 512], BF16, at=_k0)]
    qrr = [rr[0], rr[1], ar("C", "qrr2", [128, 512], F32, at=_k0 + 1024), ar("C", "qrr3", [128, 512], F32, at=_k0 + 3072)]
    qxr = [xrb, ar("C", "qxr1", [64, 512], BF16, at=_k0 + 5120)]
    qtc = [tc_[0], tc_[1], ar("C", "qtc2", [128, 512], F32, at=_k0 + 6144), ar("C", "qtc3", [128, 512], F32, at=_k0 + 8192)]
    assert _k0 + 10240 <= aofs["kvK0"] + NKVS * 3072
    Pt = [pers("Pt%d" % i, [128, 512], BF16) for i in range(3)]
    Pacc = [pers("Pacc%d" % i, [128, 512], F32) for i in range(2)]
    pS3 = [pb[2], pb[3], LB(ptr.ap.bitcast(F32), ptr.buf)]
    swA = ar("C", "swA", [128, 4, 4, 64], BF16)
    pcb = ar("C", "pcb", [128, 256], BF16)
    pcr = ar("C", "pcr", [128, 64], BF16)
    sgm = ar("D", "sgm", [128, 4, 512], F32)
    sga = ar("D", "sga", [128, 4, 512], F32)
    tmpm = ar("D", "tmpm", [128, 4, 512], F32)
    td = [ar("D", "td%d" % i, [128, 512], F32) for i in range(2)]
    mergedT = ar("D", "mergedT", [128, 8, 512], BF16)
    yout = [ar("D", "yout%d" % i, [128, 1024], F32) for i in range(2)]
    xres = [ar("D", "xres%d" % i, [128, D], F32) for i in range(4)]
    xres_sig = [P.dma_sig("xres%d" % i) for i in range(4)]
    v_ext = pers("v_ext", [128, 4, 4, 257], BF16)
    Pd = [pers("Pd%d" % i, [128, 512], BF16) for i in range(4)]

    scr_sig = {k: P.dma_sig("scrw" + k) for k in "KRV"}
    ckvo_sig = [P.dma_sig("ckvo%d" % i) for i in range(2)]
    kro_sig = [P.dma_sig("kro%d" % i) for i in range(2)]
    yo_sig = [P.dma_sig("yo%d" % i) for i in range(2)]
    pcb_sig = P.dma_sig("pcb")
    pcr_sig = P.dma_sig("pcr")
    kscr_b = {}

    def scr_buf(si, kb):
        key = ("scr", kb)
        if key not in kscr_b:
            kscr_b[key] = {k: P.buf("scr%s_%d" % (k, kb)) for k in "KRV"}
        return kscr_b[key]

    MEMSET(v_ext[:, :, :, 256:257], 1.0, [v_ext])
    MEMSET(krT[64:128, :], 0.0, [krT])
    for i in range(4):
        MEMSET(Pd[i].ap, 0.0, [Pd[i]])

    def kv_block(si, kb, NK, to_scratch):
        TSk = min(NK, 128)
        for h in range(HA):
            pk = pb[h % 2]
            for c2 in range(2):
                MM(pk[:, 0:NK], wukvk[:, c2, h * 128:(h + 1) * 128], ckvnT[:, c2, 0:NK], c2 == 0, c2 == 1,
                   [wukvk, ckvnT], [pk])
            sq = sqb[h % 2]
            ACT(sq[:, 0:NK], pk[:, 0:NK], AF.Square, [pk], [sq])
            ps = pb[2 + h % 2]
            MM(ps[:, 0:NK], onesb, sq[:, 0:NK], True, True, [cb, sq], [ps])
            r = rr[h % 2]
            rsqrt_act(r[:, 0:NK], ps[:, 0:NK], 1.0 / NOPE, [ps, cf], [r], None)
            STT(knT[:, h, 0:NK], pk[:, 0:NK], gkn_c, r[:, 0:NK], ALU.mult, ALU.mult, [pk, r, cf], [knT])
        for jt in range((NK + 127) // 128):
            for g in range(2):
                pv = pb[4 + g]
                for c2 in range(2):
                    MM(pv[0:TSk, :], ckvnT[:, c2, jt * 128:jt * 128 + TSk], wukvv[:, c2, g * 512:(g + 1) * 512],
                       c2 == 0, c2 == 1, [ckvnT, wukvv], [pv])
                ACT(Vcur[0:TSk, jt, g * 512:(g + 1) * 512], pv[0:TSk, :], AF.Copy, [pv], [Vcur])
        if to_scratch:
            sb_ = scr_buf(si, kb)
            k0 = kb * 512
            DMA("sp", kscr[:, :, k0:k0 + NK].rearrange("h d k -> d h k"), knT[:, :, 0:NK], [knT], [sb_["K"]], scr_sig["K"])
            DMA("sp", rscr[:, k0:k0 + NK], krT[:, 0:NK], [krT], [sb_["R"]], scr_sig["R"])
            nt = NK // 128
            for t_ in range(nt):
                DMA("sp", vscr[:, :, kb * 4 + t_, :].rearrange("h p d -> p h d"),
                    Vcur[:, t_, :].rearrange("p (h d) -> p h d", h=HA), [Vcur], [sb_["V"]], scr_sig["V"])

    def attention(si, T, nkb_past, ucount):
        TSq = min(T, 128)
        ntl = (T + 127) // 128
        loads = [(h_, kb_) for h_ in range(HA) for kb_ in range(nkb_past)]
        nload = [0]

        def issue_load():
            li = nload[0]
            if li >= len(loads):
                return
            nload[0] += 1
            h_, kb_ = loads[li]
            s_ = li % NKVS
            sl = kvs[s_]
            slb = [sl["K"], sl["R"], sl["V"]]
            sb_ = scr_buf(si, kb_)
            k0 = kb_ * 512
            DMA("sp", sl["K"].ap, kscr[h_, :, k0:k0 + 512], list(sb_.values()), slb, kvs_sig[s_])
            DMA("sp", sl["R"].ap, rscr[:, k0:k0 + 512], list(sb_.values()), slb, kvs_sig[s_])
            DMA("sp", sl["V"].ap, vscr[h_, :, kb_ * 4:kb_ * 4 + 4, :], list(sb_.values()), slb, kvs_sig[s_])

        for _ in range(NKVS - 1):
            issue_load()
        pending = [None]
        for h in range(HA):
            po, pden = (pb[4], pb[5]) if h % 2 == 0 else (pb[0], pb[1])
            tasks = []
            for kb in range(nkb_past):
                li = h * nkb_past + kb
                sl = kvs[li % NKVS]
                slb = [sl["K"], sl["R"], sl["V"]]
                for jt in range(4):
                    tasks.append(("past", sl, slb, jt))
            for jt in range(ntl):
                tasks.append(("diag", None, None, jt))
            nt_ = len(tasks)

            pacc = Pacc[h % 2]

            def emit_S(i):
                kind_, sl, slb, jt = tasks[i]
                pS = pS3[i % 3]
                if kind_ == "past":
                    MM(pS[:, 0:T], sl["K"][:, jt * 128:(jt + 1) * 128], qnT[:, h, 0:T], True, False, slb + [qnT], [pS])
                    MM(pS[:, 0:T], sl["R"][:, jt * 128:(jt + 1) * 128], qrT[:, h, 0:T], False, True,
                       slb + [qrT], [pS])
                else:
                    c0 = jt * 128
                    MM(pS[0:TSq, c0:T], knT[:, h, c0:c0 + TSq], qnT[:, h, c0:T], True, False, [knT, qnT], [pS])
                    MM(pS[0:TSq, c0:T], krT[:, c0:c0 + TSq], qrT[:, h, c0:T], False, True, [krT, qrT], [pS])

            def emit_rest(i):
                kind_, sl, slb, jt = tasks[i]
                pS = pS3[i % 3]
                first, last = i == 0, i == nt_ - 1
                if kind_ == "past":
                    pt_ = Pt[i % 3]
                    ACT(pt_[:, 0:T], pS[:, 0:T], AF.Exp, [pS], [pt_], scale=ATT_SCALE)
                    MM(po[:, 0:T], sl["V"][:, jt, :], pt_[:, 0:T], first, last, slb + [pt_], [po])
                    if first:
                        CP(pacc[:, 0:T], pt_[:, 0:T], [pt_], [pacc])
                    else:
                        TT(pacc[:, 0:T], pacc[:, 0:T], pt_[:, 0:T], ALU.add, [pacc, pt_], [pacc])
                else:
                    c0 = jt * 128
                    pd_ = Pd[jt]
                    ACT(pd_[0:64, c0:T], pS[0:64, c0:T], AF.Exp, [pS], [pd_], scale=ATT_SCALE)
                    if TSq == 128:
                        ACT(pd_[64:128, c0 + 64:T], pS[64:128, c0 + 64:T], AF.Exp, [pS], [pd_], scale=ATT_SCALE)
                    MM(po[:, c0:T], Vcur[0:TSq, jt, h * 128:(h + 1) * 128], pd_[0:TSq, c0:T], first, last,
                       [Vcur, pd_], [po])
                    if first:
                        if TSq < 128:
                            MEMSET(pacc[:, 0:T], 0.0, [pacc], eng="dve")
                        CP(pacc[0:TSq, 0:T], pd_[0:TSq, 0:T], [pd_], [pacc])
                    else:
                        TT(pacc[0:TSq, c0:T], pacc[0:TSq, c0:T], pd_[0:TSq, c0:T], ALU.add, [pacc, pd_], [pacc])

            emit_S(0)
            if nt_ > 1:
                emit_S(1)
            if pending[0] is not None:
                pending[0]()
                pending[0] = None
            for i in range(nt_):
                if i + 2 < nt_:
                    emit_S(i + 2)
                emit_rest(i)
                if tasks[i][0] == "past" and tasks[i][3] == 1:
                    issue_load()
            def fin(h=h, po=po, pden=pden, pacc=pacc):
                hi_, lo_ = sqb[0], sqb[1]
                CP(hi_[:, 0:T], pacc[:, 0:T], [pacc], [hi_])
                TT(lo_[:, 0:T], pacc[:, 0:T], hi_[:, 0:T], ALU.subtract, [pacc, hi_], [lo_])
                MM(pden[:, 0:T], onesb, hi_[:, 0:T], True, False, [cb, hi_], [pden])
                MM(pden[:, 0:T], onesb, lo_[:, 0:T], False, True, [cb, lo_], [pden])
                u = unit(ucount + 10 + h // 4)
                pz = pmisc
                for kc in range(8):
                    MM(pz[:, 0:T], u[:, kc, (h % 4) * 128:(h % 4 + 1) * 128], hT[:, kc, 0:T], kc == 0, kc == 7,
                       [u, hT], [pz])
                e, zf, d1, n1 = tc_[0], tc_[1], rr[0], rr[1]
                ACT(e[:, 0:T], pz[:, 0:T], AF.Exp, [pz], [e], scale=-1.0)
                ACT(zf[:, 0:T], pz[:, 0:T], AF.Copy, [pz], [zf])
                STT(d1[:, 0:T], e[:, 0:T], 1.0, pden[:, 0:T], ALU.add, ALU.mult, [e, pden], [d1])
                ACT(d1[:, 0:T], d1[:, 0:T], AF.Ln, [d1], [d1])
                ACT(d1[:, 0:T], d1[:, 0:T], AF.Exp, [d1], [d1], scale=-1.0)
                TT(n1[:, 0:T], po[:, 0:T], zf[:, 0:T], ALU.mult, [po, zf], [n1])
                TT(haT[:, h, 0:T], n1[:, 0:T], d1[:, 0:T], ALU.mult, [n1, d1], [haT])
            pending[0] = fin
        if pending[0] is not None:
            pending[0]()
            pending[0] = None

    def block(si, kind, row0, T, pos0, nkb_past, kb_cur, ucount, last_block, write_scratch):
        TSz = min(T, 128)
        ntl = (T + 127) // 128
        NCH = T // 64
        DMA("sp", cosT[:, 0:T], cosT_d[:, pos0:pos0 + T], [], [cosT], tab_sig)
        DMA("sp", sinT[:, 0:T], sinT_d[:, pos0:pos0 + T], [], [cosT], tab_sig)
        for j in range(ntl):
            DMA("sp", cstm[0:TSz, j, :], cstm_d[pos0 + j * 128:pos0 + j * 128 + TSz, :], [], [cosT], tab_sig)
            DMA("sp", sntm[0:TSz, j, :], sntm_d[pos0 + j * 128:pos0 + j * 128 + TSz, :], [], [cosT], tab_sig)
        for j in range(ntl):
            xi = xin[j % 2]
            r0 = row0 + j * 128
            DMA("sp", xi[0:TSz, :], xs[r0:r0 + TSz, :], [], [xi], xin_sig[j % 2])
            sx = small[0:TSz, 0:1]
            ACT(junk[0:TSz, :], xi[0:TSz, :], AF.Square, [xi], [junk, smallb[0]], accum_out=sx)
            rsqrt_act(sx, sx, 1.0 / D, [smallb[0], cf], [smallb[0]], None)
            ACT(xn[0:TSz, :], xi[0:TSz, :], AF.Copy, [xi, smallb[0]], [xn], scale=sx)
            for kc in range(8):
                TR(ptr[:, kc * 128:kc * 128 + TSz], xn[0:TSz, kc * 128:(kc + 1) * 128], identb[0:TSz, 0:TSz],
                   [xn, cb], [ptr])
            TT(hT[:, :, j * 128:j * 128 + TSz], ptr.ap.rearrange("p (k t) -> p k t", k=8)[:, :, 0:TSz],
               ng_c.unsqueeze(2).to_broadcast([128, 8, TSz]), ALU.mult, [ptr, cf], [hT])

        def conv_front(cc):
            u = unit(ucount + cc // 4)
            pc = pb[cc % 2]
            for kc in range(8):
                MM(pc[:, 0:T], u[:, kc, (cc % 4) * 128:(cc % 4 + 1) * 128], hT[:, kc, 0:T], kc == 0, kc == 7,
                   [u, hT], [pc])
            xb_ = xcb[cc % 2]
            CP(xb_[:, 0:3], hist[:, cc, :], [hist], [xb_])
            ACT(xb_[:, 3:3 + T], pc[:, 0:T], AF.Copy, [pc], [xb_])

        def conv_silu(cc):
            uc, e = ucv[cc % 2], ta[cc % 2]
            ACT(e[:, 0:T], uc[:, 0:T], AF.Exp, [uc], [e], scale=-1.0)
            ACT(e[:, 0:T], e[:, 0:T], AF.Ln, [e], [e], bias=1.0)
            ACT(e[:, 0:T], e[:, 0:T], AF.Exp, [e], [e], scale=-1.0)

        def conv_taps(cc):
            xb_, uc = xcb[cc % 2], ucv[cc % 2]
            CP(hist[:, cc, :], xb_[:, T:T + 3], [xb_], [hist])
            TS(uc[:, 0:T], xb_[:, 0:T], cw_c[:, cc * 4:cc * 4 + 1], cbias_c[:, cc:cc + 1], ALU.mult, ALU.add,
               [xb_, cf], [uc])
            for jj in range(1, 4):
                STT(uc[:, 0:T], xb_[:, jj:jj + T], cw_c[:, cc * 4 + jj:cc * 4 + jj + 1], uc[:, 0:T], ALU.mult, ALU.add,
                    [xb_, uc, cf], [uc])

        def conv_mult(cc):
            uc, e = ucv[cc % 2], ta[cc % 2]
            TT(uhT[:, cc, 0:T], uc[:, 0:T], e[:, 0:T], ALU.mult, [uc, e], [uhT])

        for k in range(9):
            if k < 8:
                conv_front(k)
            if k >= 1:
                conv_silu(k - 1)
            if k < 8:
                conv_taps(k)
            if k >= 1:
                conv_mult(k - 1)

        for g in range(2):
            u = unit(ucount + 2 + g)
            for j in range(ntl):
                pv = pb[(g * ntl + j) % 2]
                for kc in range(8):
                    MM(pv[0:TSz, :], hT[:, kc, j * 128:j * 128 + TSz], u[:, kc, :], kc == 0, kc == 7, [hT, u], [pv])
                ACT(v_ext[0:TSz, j, 2 * g:2 * g + 2, 0:256], pv[0:TSz, :].rearrange("p (h e) -> p h e", h=2), AF.Copy,
                    [pv], [v_ext])
        for g in range(2):
            uo = unit(ucount + 4 + 2 * g)
            uz = unit(ucount + 5 + 2 * g, oldest=ucount + 4 + 2 * g)
            for j in range(ntl):
                po_, pz_ = (pb[0], pb[1]) if (g * ntl + j) % 2 == 0 else (pb[2], pb[3])
                for kc in range(8):
                    MM(po_[0:TSz, :], hT[:, kc, j * 128:j * 128 + TSz], uo[:, kc, :], kc == 0, kc == 7, [hT, uo], [po_])
                for kc in range(8):
                    MM(pz_[0:TSz, :], hT[:, kc, j * 128:j * 128 + TSz], uz[:, kc, :], kc == 0, kc == 7, [hT, uz], [pz_])
                eo, ez, g1 = ta[0], ta[1], ta[2]
                ACT(eo[0:TSz, :], po_[0:TSz, :], AF.Exp, [po_], [eo], scale=-1.0)
                ACT(ez[0:TSz, :], pz_[0:TSz, :], AF.Exp, [pz_], [ez], scale=-1.0)
                TT(g1[0:TSz, :], pz_[0:TSz, :], hng_bc[0:TSz, g * 512:(g + 1) * 512], ALU.mult, [pz_, bc], [g1])
                ACT(eo[0:TSz, :], eo[0:TSz, :], AF.Ln, [eo], [eo], bias=1.0)
                ACT(ez[0:TSz, :], ez[0:TSz, :], AF.Ln, [ez], [ez], bias=1.0)
                TT(eo[0:TSz, :], eo[0:TSz, :], ez[0:TSz, :], ALU.add, [eo, ez], [eo])
                ACT(eo[0:TSz, :], eo[0:TSz, :], AF.Exp, [eo], [eo], scale=-1.0)
                TT(gmg[0:TSz, j, g * 512:(g + 1) * 512], g1[0:TSz, :], eo[0:TSz, :], ALU.mult, [g1, eo], [gmg])

        u8 = unit(ucount + 8)
        for j in range(ntl):
            pk = pb[j % 2]
            r0 = row0 + j * 128
            for kc in range(8):
                MM(pk[0:TSz, 0:328], hT[:, kc, j * 128:j * 128 + TSz], u8[:, kc, 0:328], kc == 0, kc == 7, [hT, u8], [pk])
            s1, s2 = small[0:TSz, 1:2], small[0:TSz, 2:3]
            gb = smallb[3]
            cf_, kf = ckvf[j % 2], krf[j % 2]
            xr, xsw, tA = krw[0], krw[1], krw[2]
            ACT(junk[0:TSz, 0:256], pk[0:TSz, 0:256], AF.Square, [pk], [junk, smallb[1]], accum_out=s1)
            ACT(junk[0:TSz, 256:320], pk[0:TSz, 256:320], AF.Square, [pk], [junk, smallb[2]], accum_out=s2)
            TT(gcol[0:TSz, j, 0:8], pk[0:TSz, 320:328], bif_bc[0:TSz, :], ALU.add, [pk, bc], [gb])
            ACT(s1, s1, AF.Ln, [smallb[1], cf], [smallb[1]], scale=1.0 / KVL, bias=eps_ap(s1))
            ACT(s2, s2, AF.Ln, [smallb[2], cf], [smallb[2]], scale=1.0 / ROPE, bias=eps_ap(s2))
            ACT(gcol[0:TSz, j, 8:12], gcol[0:TSz, j, 4:8], AF.Exp, [gb], [gb], scale=-1.0)
            ACT(s1, s1, AF.Exp, [smallb[1]], [smallb[1]], scale=-0.5)
            ACT(s2, s2, AF.Exp, [smallb[2]], [smallb[2]], scale=-0.5)
            ACT(gcol[0:TSz, j, 8:12], gcol[0:TSz, j, 8:12], AF.Ln, [gb], [gb], bias=1.0)
            STT(cf_[0:TSz, :], pk[0:TSz, 0:256], s1, kvng_bc[0:TSz, :], ALU.mult, ALU.mult, [pk, smallb[1], bc], [cf_])
            STT(xr[0:TSz, :], pk[0:TSz, 256:320], s2, gkr_bc[0:TSz, :], ALU.mult, ALU.mult, [pk, smallb[2], bc], [xr])
            MM(pmisc[0:TSz, 0:4], tri[0:TSz, 0:TSz], gcol[0:TSz, j, 8:12], True, True, [cf, gb], [pmisc])
            DMA("sp", ckv_o[r0:r0 + TSz, :], cf_[0:TSz, :], [cf_], [], ckvo_sig[j % 2])
            ACT(ckvb[0:TSz, :], cf_[0:TSz, :], AF.Copy, [cf_], [ckvb])
            CP(xsw[0:TSz, 0:32], xr[0:TSz, 32:64], [xr], [xsw])
            CP(xsw[0:TSz, 32:64], xr[0:TSz, 0:32], [xr], [xsw])
            TT(tA[0:TSz, :], xr[0:TSz, :], cstm[0:TSz, j, :], ALU.mult, [xr, cosT], [tA])
            TT(xsw[0:TSz, :], xsw[0:TSz, :], sntm[0:TSz, j, :], ALU.mult, [xsw, cosT], [xsw])
            TT(kf[0:TSz, :], tA[0:TSz, :], xsw[0:TSz, :], ALU.add, [tA, xsw], [kf])
            CP(gcol[0:TSz, j, 12:16], pmisc[0:TSz, 0:4], [pmisc], [gb])
            TT(gcol[0:TSz, j, 4:8], gcol[0:TSz, j, 0:4], gcol[0:TSz, j, 12:16], ALU.add, [gb], [gb])
            for c2 in range(2):
                TR(ptr[:, c2 * 128:c2 * 128 + TSz], ckvb[0:TSz, c2 * 128:(c2 + 1) * 128], identb[0:TSz, 0:TSz],
                   [ckvb, cb], [ptr])
            DMA("sp", kr_o[r0:r0 + TSz, :], kf[0:TSz, :], [kf], [], kro_sig[j % 2])
            ACT(krb[0:TSz, :], kf[0:TSz, :], AF.Copy, [kf], [krb])
            TR(pmisc[0:4, 128:128 + TSz], gcol[0:TSz, j, 4:8], identf[0:TSz, 0:TSz], [gb, cf], [pmisc])
            TR(pmisc[0:4, 256:256 + TSz], gcol[0:TSz, j, 12:16], identf[0:TSz, 0:TSz], [gb, cf], [pmisc])
            TR(ptr[0:64, 256:256 + TSz], krb[0:TSz, :], identb[0:TSz, 0:TSz], [krb, cb], [ptr])
            CP(ckvnT[:, :, j * 128:j * 128 + TSz], ptr.ap.rearrange("p (k t) -> p k t", k=8)[:, 0:2, 0:TSz],
               [ptr], [ckvnT])
            CP(krT[0:64, j * 128:j * 128 + TSz], ptr[0:64, 256:256 + TSz], [ptr], [krT])
            CP(grow[:, 0, j * 128:j * 128 + TSz], pmisc[0:4, 128:128 + TSz], [pmisc], [grow])
            CP(grow[:, 1, j * 128:j * 128 + TSz], pmisc[0:4, 256:256 + TSz], [pmisc], [grow])
        av = grow[:, 0, 0:T].rearrange("p (c t) -> p c t", t=64)
        cv = grow[:, 1, 0:T].rearrange("p (c t) -> p c t", t=64)
        mb = P.buf("mallb") if "mallb" not in kscr_b else kscr_b["mallb"]
        kscr_b["mallb"] = mb
        P.op("dve", "tensor_reduce", dict(out=mall[:, 10:10 + NCH], in_=av, axis=AX.X, op=ALU.max), bl([grow]), bl([mb]))
        TSM(mall[:, 20:20 + NCH], cv[:, :, 63], -1.0, [grow], [mb])
        P.op("dve", "tensor_tensor_scan", dict(out=mall[:, 1:1 + NCH], data0=mall[:, 10:10 + NCH], data1=mall[:, 20:20 + NCH],
                                               initial=mall[:, 0:1], op0=ALU.max, op1=ALU.add), bl([mb]), bl([mb]))
        TT(mall[:, 10:10 + NCH], mall[:, 1:1 + NCH], mall[:, 20:20 + NCH], ALU.subtract, [mb], [mb])
        TT(mall[:, 20:20 + NCH], mall[:, 0:NCH], mall[:, 10:10 + NCH], ALU.subtract, [mb], [mb])
        ACT(mall[:, 20:20 + NCH], mall[:, 20:20 + NCH], AF.Exp, [mb], [mb])
        Mb = mall[:, 10:10 + NCH].unsqueeze(2).to_broadcast([4, NCH, 64])
        TT(av, av, Mb, ALU.subtract, [grow, mb], [grow])
        TT(cv, cv, Mb, ALU.subtract, [grow, mb], [grow])
        ACT(grow[:, 0:2, 0:T], grow[:, 0:2, 0:T], AF.Exp, [grow], [grow])
        TT(decbd[:, 0:NCH, :], mall[:, 20:20 + NCH].unsqueeze(2).to_broadcast([4, NCH, 4]),
           eye4.unsqueeze(1).to_broadcast([4, NCH, 4]), ALU.mult, [mb, cf], [decbd])
        MM(pmisc[:, 0:NCH * 4], ones4, decbd[:, 0:NCH, :].rearrange("p c h -> p (c h)"), True, True, [cf, decbd], [pmisc])
        CP(decbc[:, 0:NCH, :].rearrange("p c h -> p (c h)"), pmisc[:, 0:NCH * 4], [pmisc], [decbc])
        for j in range(ntl):
            TR(pmisc[0:TSz, 384:388], grow[:, 0, j * 128:j * 128 + TSz], eye4, [grow, cf], [pmisc])
            TR(pmisc[0:TSz, 392:396], grow[:, 1, j * 128:j * 128 + TSz], eye4, [grow, cf], [pmisc])
            CP(wkc[0:TSz, j, 0:4], pmisc[0:TSz, 384:388], [pmisc], [wkc])
            ACT(wkc[0:TSz, j, 4:8], pmisc[0:TSz, 384:388], AF.Copy, [pmisc], [wkc], scale=1.0 / 16)
            CP(wkc[0:TSz, j, 8:12], pmisc[0:TSz, 392:396], [pmisc], [wkc])
        if last_block:
            DMA("sp", m_o[si], mall[:, NCH:NCH + 1], [mb], [], sigM)
        CP(mall[:, 0:1], mall[:, NCH:NCH + 1], [mb], [mb])

        for h in range(HM):
            for ec in range(2):
                pq, pk_ = pb[0], pb[1]
                for dc in range(2):
                    MM(pq[:, 0:T], wq[:, h, dc, ec * 128:(ec + 1) * 128], uhT[:, 2 * h + dc, 0:T], dc == 0, dc == 1,
                       [wq, uhT], [pq])
                for dc in range(2):
                    MM(pk_[:, 0:T], wk[:, h, dc, ec * 128:(ec + 1) * 128], uhT[:, 2 * h + dc, 0:T], dc == 0, dc == 1,
                       [wk, uhT], [pk_])
                CP(qTm[:, h, ec, 0:T], pq[:, 0:T], [pq], [qTm])
                ACT(kTm[:, h, ec, 0:T], pk_[:, 0:T], AF.Copy, [pk_], [kTm], scale=1.0 / 16)
            for j in range(ntl):
                pkt = pb[2 + j % 2]
                for dc in range(2):
                    MM(pkt[0:TSz, 0:256], uhT[:, 2 * h + dc, j * 128:j * 128 + TSz], wk[:, h, dc, :], dc == 0, dc == 1,
                       [uhT, wk], [pkt])
                ACT(kw[0:TSz, j, h, :], pkt[0:TSz, 0:256], AF.Copy, [pkt, wkc], [kw], scale=wkc[0:TSz, j, 4 + h:5 + h])

        for c in range(NCH):
            j, ph = c // 2, 64 * (c % 2)
            t0 = c * 64
            for h in range(HM):
                reg = pms[(c * HM + h) % 6]
                for dc in range(2):
                    MM(reg[ph:ph + 64, :], kTm[:, h, dc, t0:t0 + 64], qTm[:, h, dc, t0:t0 + 64], dc == 0, dc == 1,
                       [kTm, qTm], [reg])
                STT(swA[ph:ph + 64, j, h, :], reg[ph:ph + 64, :], wkc[ph:ph + 64, j, h:h + 1], cmask[ph:ph + 64, :],
                    ALU.mult, ALU.mult, [reg, wkc, cf], [swA])
        for c in range(NCH):
            j, ph = c // 2, 64 * (c % 2)
            t0 = c * 64
            hr = hraw[j % 2]
            for pair in ((0, 1), (2, 3)):
                for h in pair:
                    dec = decbc[:, c, h:h + 1]
                    for dc in range(2):
                        if dc == 0:
                            ACT(Cbf[h][dc].ap, CstH[h][:, dc, :], AF.Copy, [CstH[h], decbc], [Cbf[h][dc]], scale=dec)
                        else:
                            TSM(Cbf[h][dc].ap, CstH[h][:, dc, :], dec, [CstH[h], decbc], [Cbf[h][dc]])
                for i, h in enumerate(pair):
                    for dc in range(2):
                        pU = pb[2 * i + dc]
                        MM(pU[:, 0:257], kw[ph:ph + 64, j, h, dc * 128:(dc + 1) * 128], v_ext[ph:ph + 64, j, h, :],
                           True, True, [kw, v_ext], [pU])
                for i, h in enumerate(pair):
                    dec = decbc[:, c, h:h + 1]
                    for dc in range(2):
                        pU = pb[2 * i + dc]
                        STT(CstH[h][:, dc, :], CstH[h][:, dc, :], dec, pU[:, 0:257], ALU.mult, ALU.add,
                            [CstH[h], decbc, pU], [CstH[h]])
                for i, h in enumerate(pair):
                    pN = pb[4 + i]
                    for dc in range(2):
                        MM(pN[ph:ph + 64, 0:257], qTm[:, h, dc, t0:t0 + 64], Cbf[h][dc].ap, dc == 0, False,
                           [qTm, Cbf[h][dc]], [pN])
                    MM(pN[ph:ph + 64, 0:257], swA[ph:ph + 64, j, h, :], v_ext[ph:ph + 64, j, h, :], False, True,
                       [swA, v_ext], [pN])
                for i, h in enumerate(pair):
                    pN = pb[4 + i]
                    ACT(hr[ph:ph + 64, h, :], pN[ph:ph + 64, 0:257], AF.Copy, [pN], [hr])
            if c % 2 == 1 or c == NCH - 1:
                hb = smallb[4]
                den4 = hr[0:TSz, :, 256]
                STT(small[0:TSz, 8:12], den4, -1.0, den4, ALU.mult, ALU.max, [hr], [hb])
                TT(small[0:TSz, 8:12], small[0:TSz, 8:12], wkc[0:TSz, j, 8:12], ALU.max, [hb, wkc], [hb])
                RCP(small[0:TSz, 8:12], small[0:TSz, 8:12], [hb], [hb])
                TT(hh[0:TSz, :, :], hr[0:TSz, :, 0:256], small[0:TSz, 8:12].unsqueeze(2).to_broadcast([TSz, 4, 256]),
                   ALU.mult, [hr, hb], [hh])
                for h in range(HM):
                    ACT(junk[0:TSz, 0:256], hh[0:TSz, h, :], AF.Square, [hh], [junk, smallb[5]],
                        accum_out=small[0:TSz, 12 + h:13 + h])
                rsqrt_act(small[0:TSz, 12:16], small[0:TSz, 12:16], 1.0 / DH, [smallb[5], cf], [smallb[5]], None)
                for h in range(HM):
                    STT(hmg[0:TSz, h * 256:(h + 1) * 256], hh[0:TSz, h, :], small[0:TSz, 12 + h:13 + h],
                        gmg[0:TSz, j, h * 256:(h + 1) * 256], ALU.mult, ALU.mult, [hh, smallb[5], gmg], [hmg])
                for kc in range(8):
                    TR(ptr[:, kc * 128:kc * 128 + TSz], hmg[0:TSz, kc * 128:(kc + 1) * 128], identb[0:TSz, 0:TSz],
                       [hmg, cb], [ptr])
                ACT(hmT[:, :, j * 128:j * 128 + TSz], ptr.ap.rearrange("p (k t) -> p k t", k=8)[:, :, 0:TSz], AF.Copy,
                    [ptr], [hmT])

        u9 = unit(ucount + 9)
        for k3 in range(3):
            pq = pb[k3 % 2]
            for kc in range(8):
                MM(pq[:, 0:T], u9[:, kc, k3 * 128:(k3 + 1) * 128], hT[:, kc, 0:T], kc == 0, kc == 7, [u9, hT], [pq])
            ACT(cqf[:, k3, 0:T], pq[:, 0:T], AF.Copy, [pq], [cqf])
            ACT(sqb[k3][:, 0:T], pq[:, 0:T], AF.Square, [pq], [sqb[k3]])
        psq = pb[2]
        for k3 in range(3):
            MM(psq[:, 0:T], onesb, sqb[k3][:, 0:T], k3 == 0, k3 == 2, [cb, sqb[k3]], [psq])
        rsqrt_act(rr[0][:, 0:T], psq[:, 0:T], 1.0 / QL, [psq, cf], [rr[0]], None)
        for k3 in range(3):
            STT(cqn[:, k3, 0:T], cqf[:, k3, 0:T], qng_c[:, k3:k3 + 1], rr[0][:, 0:T], ALU.mult, ALU.mult,
                [cqf, rr[0], cf], [cqn])
        MEMSET(qrT[64:128, :, :], 0.0, [qrT], eng="dve")
        for h0 in range(0, HA, 2):
            pair = (h0, h0 + 1)
            for s_, h in enumerate(pair):
                pn, pr = pb[3 * s_], pb[3 * s_ + 1]
                for k3 in range(3):
                    MM(pn[:, 0:T], wuqn[:, k3, h * 128:(h + 1) * 128], cqn[:, k3, 0:T], k3 == 0, k3 == 2, [wuqn, cqn], [pn])
                for k3 in range(3):
                    MM(pr[0:64, 0:T], wuqr[:, k3, h * 64:(h + 1) * 64], cqn[:, k3, 0:T], k3 == 0, k3 == 2, [wuqr, cqn], [pr])
            for s_, h in enumerate(pair):
                pn, pr = pb[3 * s_], pb[3 * s_ + 1]
                ACT(qsq[2 * s_][:, 0:T], pn[:, 0:T], AF.Square, [pn], [qsq[2 * s_]])
                ACT(qsq[2 * s_ + 1][0:64, 0:T], pr[0:64, 0:T], AF.Square, [pr], [qsq[2 * s_ + 1]])
            for s_, h in enumerate(pair):
                ps1, ps2 = pb[3 * s_ + 2], (pmisc if s_ == 0 else pS3[2])
                MM(ps1[:, 0:T], onesb, qsq[2 * s_][:, 0:T], True, True, [cb, qsq[2 * s_]], [ps1])
                MM(ps2[0:64, 0:T], onesb[0:64, 0:64], qsq[2 * s_ + 1][0:64, 0:T], True, True, [cb, qsq[2 * s_ + 1]], [ps2])
            for s_, h in enumerate(pair):
                ps1, ps2 = pb[3 * s_ + 2], (pmisc if s_ == 0 else pS3[2])
                rsqrt_act(qrr[2 * s_][:, 0:T], ps1[:, 0:T], 1.0 / NOPE, [ps1, cf], [qrr[2 * s_]], None)
                rsqrt_act(qrr[2 * s_ + 1][0:64, 0:T], ps2[0:64, 0:T], 1.0 / ROPE, [ps2, cf], [qrr[2 * s_ + 1]], None)
            for s_, h in enumerate(pair):
                pn, pr = pb[3 * s_], pb[3 * s_ + 1]
                STT(qnT[:, h, 0:T], pn[:, 0:T], gqn_c, qrr[2 * s_][:, 0:T], ALU.mult, ALU.mult, [pn, qrr[2 * s_], cf], [qnT])
                STT(qxr[s_][:, 0:T], pr[0:64, 0:T], gqr_c[0:64, :], qrr[2 * s_ + 1][0:64, 0:T], ALU.mult, ALU.mult,
                    [pr, qrr[2 * s_ + 1], cf], [qxr[s_]])
            for s_, h in enumerate(pair):
                prr = pb[3 * s_ + 2]
                MM(prr[0:64, 0:T], RTb, qxr[s_][:, 0:T], True, True, [cb, qxr[s_]], [prr])
            for s_, h in enumerate(pair):
                prr = pb[3 * s_ + 2]
                t1, t2 = qtc[2 * s_], qtc[2 * s_ + 1]
                TT(t1[0:64, 0:T], qxr[s_][:, 0:T], cosT[:, 0:T], ALU.mult, [qxr[s_], cosT], [t1])
                TT(t2[0:64, 0:T], prr[0:64, 0:T], sinT[:, 0:T], ALU.mult, [prr, cosT], [t2])
                TT(qrT[0:64, h, 0:T], t1[0:64, 0:T], t2[0:64, 0:T], ALU.add, [t1, t2], [qrT])

        kv_block(si, kb_cur, T, write_scratch)
        attention(si, T, nkb_past, ucount)
        if debug and last_block:
            DMA("sp", dbg_d["qn"], qnT.ap, [qnT], [], dbg_sig)
            DMA("sp", dbg_d["kn"], knT.ap, [knT], [], dbg_sig)

        for j in range(ntl):
            r0 = row0 + j * 128
            DMA("sp", xres[j][0:TSz, :], xs[r0:r0 + TSz, :], [], [xres[j]], xres_sig[j])
        for half in range(2):
            ub = ucount + 12 + half * 4
            for which, dst in ((0, sgm), (1, sga)):
                u = unit(ub + which)
                for c in range(4):
                    pg = pb[c % 2]
                    for kc in range(8):
                        MM(pg[:, 0:T], u[:, kc, c * 128:(c + 1) * 128], hT[:, kc, 0:T], kc == 0, kc == 7, [u, hT], [pg])
                    e = td[c % 2]
                    ACT(e[:, 0:T], pg[:, 0:T], AF.Exp, [pg], [e], scale=-1.0)
                    ACT(e[:, 0:T], e[:, 0:T], AF.Ln, [e], [e], bias=1.0)
                    ACT(dst[:, c, 0:T], e[:, 0:T], AF.Exp, [e], [dst], scale=-1.0)
            u = unit(ub + 2)
            for c in range(4):
                pg = pb[2 + c % 2]
                for kc in range(8):
                    MM(pg[:, 0:T], u[:, kc, c * 128:(c + 1) * 128], hmT[:, kc, 0:T], kc == 0, kc == 7, [u, hmT], [pg])
                TT(tmpm[:, c, 0:T], pg[:, 0:T], sgm[:, c, 0:T], ALU.mult, [pg, sgm], [tmpm])
            u = unit(ub + 3)
            for c in range(4):
                pg = pb[c % 2]
                for kc in range(8):
                    MM(pg[:, 0:T], u[:, kc, c * 128:(c + 1) * 128], haT[:, kc, 0:T], kc == 0, kc == 7, [u, haT], [pg])
                e = td[c % 2]
                TT(e[:, 0:T], pg[:, 0:T], sga[:, c, 0:T], ALU.mult, [pg, sga], [e])
                TT(mergedT[:, half * 4 + c, 0:T], e[:, 0:T], tmpm[:, c, 0:T], ALU.add, [e, tmpm], [mergedT])
        uo0 = unit(ucount + 20)
        uo1 = unit(ucount + 21, oldest=ucount + 20)
        for j in range(ntl):
            xi = xres[j]
            r0 = row0 + j * 128
            yo = yout[j % 2]
            for g, u in ((0, uo0), (1, uo1)):
                py = pb[2 + g]
                for kc in range(8):
                    MM(py[0:TSz, :], mergedT[:, kc, j * 128:j * 128 + TSz], u[:, kc, :], kc == 0, kc == 7,
                       [mergedT, u], [py])
                TT(yo[0:TSz, g * 512:(g + 1) * 512], py[0:TSz, :], xi[0:TSz, g * 512:(g + 1) * 512], ALU.add,
                   [py, xi], [yo])
            DMA("sp", y_d[r0:r0 + TSz, :], yo[0:TSz, :], [yo], [], yo_sig[j % 2])

    nblocks = sum((L + 511) // 512 for (_, L, _, _) in seqs)
    ring_state["total"] = nblocks * NUNIT
    ucount = 0
    for si, (kind, L, row0, npast) in enumerate(seqs):
        mb = P.buf("mallb") if "mallb" not in kscr_b else kscr_b["mallb"]
        kscr_b["mallb"] = mb
        if npast == 0:
            MEMSET(Cst.ap, 0.0, CstH, eng="dve")
            MEMSET(hist.ap, 0.0, [hist], eng="dve")
            MEMSET(mall[:, 0:1], 0.0, [mb], eng="dve")
        else:
            DMA("sp", Cst[:, :, :, 0:256], sC_d.rearrange("h (dc p) e -> p h dc e", p=128), [], CstH, sigC)
            for h in range(HM):
                DMA("sp", Cst[:, h, :, 256], sn_d[h].rearrange("(dc p) -> p dc", p=128), [], CstH, sigC,
                    allow_slow_non_contiguous=True)
            for jj in range(3):
                DMA("sp", hist[:, :, jj], sconv_d[jj].rearrange("(c p) -> p c", p=128), [], [hist], sigH,
                    allow_slow_non_contiguous=True)
            DMA("sp", mall[:, 0:1], sm_d, [], [mb], sigM)
            for kb in range(npast // 512):
                for jt in range(4):
                    k0 = kb * 512 + jt * 128
                    DMA("sp", pcb.ap, cckv_bf[k0:k0 + 128, :], [wbfB], [pcb], pcb_sig)
                    DMA("sp", pcr.ap, ckr_bf[k0:k0 + 128, :], [wbfB], [pcr], pcr_sig)
                    for c2 in range(2):
                        TR(ptr[:, c2 * 128:(c2 + 1) * 128], pcb[:, c2 * 128:(c2 + 1) * 128], identb, [pcb, cb], [ptr])
                    TR(ptr[0:64, 256:384], pcr.ap, identb, [pcr, cb], [ptr])
                    CP(ckvnT[:, :, jt * 128:(jt + 1) * 128], ptr.ap.rearrange("p (k t) -> p k t", k=8)[:, 0:2, :],
                       [ptr], [ckvnT])
                    CP(krT[0:64, jt * 128:(jt + 1) * 128], ptr[0:64, 256:384], [ptr], [krT])
                kv_block(si, kb, 512, True)
        nb = (L + 511) // 512
        for b in range(nb):
            T = min(512, L - b * 512)
            block(si, kind, row0 + b * 512, T, npast + b * 512, npast // 512 + b, npast // 512 + b, ucount,
                  b == nb - 1, b < nb - 1)
            ucount += NUNIT
        DMA("sp", C_o[si].rearrange("h (dc p) e -> p h dc e", p=128), Cst[:, :, :, 0:256], CstH, [], sigC)
        for h in range(HM):
            DMA("sp", n_o[si, h].rearrange("(dc p) -> p dc", p=128), Cst[:, h, :, 256], CstH, [], sigC,
                allow_slow_non_contiguous=True)
        for jj in range(3):
            DMA("sp", conv_o[si, jj].rearrange("(c p) -> p c", p=128), hist[:, :, jj], [hist], [], sigH,
                allow_slow_non_contiguous=True)

    if debug:
        for n, lb in (("hm", hmT), ("ha", haT), ("mg", mergedT), ("hT", hT)):
            DMA("sp", dbg_d[n], lb.ap, [lb], [], dbg_sig)
    stats = P.lower()
    es.close()
    return nc, stats


def _rope_tables():
    half = ROPE // 2
    inv = np.power(np.float32(10000.0), -np.arange(half, dtype=np.float32) / np.float32(half)).astype(np.float32)
    pos = np.arange(NPOS, dtype=np.float32)
    ang = (pos[:, None] * inv[None, :]).astype(np.float32)
    cos = np.cos(ang.astype(np.float64)).astype(np.float32)
    sin = np.sin(ang.astype(np.float64)).astype(np.float32)
    cs_tm = np.concatenate([cos, cos], axis=1)
    sn_tm = np.concatenate([-sin, sin], axis=1)
    return (np.ascontiguousarray(cs_tm.T), np.ascontiguousarray(np.concatenate([sin, sin], axis=1).T),
            np.ascontiguousarray(cs_tm), np.ascontiguousarray(sn_tm))


def _prep_shared(norm_g, w_in, b_if, conv_w, conv_b, wq_m, wk_m, hnorm_g, qn_g, w_uq, kvn_g, w_ukv,
                 g_qn, g_qr, g_kn, g_kr, w_pm, w_pa, w_out):
    f = np.float32
    w_in, w_pm, w_pa, w_out = w_in[0], w_pm[0], w_pa[0], w_out[0]
    o = np.cumsum([0, 1024, 1024, 4, 4, 1024, 1024, QL, KVL, ROPE, 1024, 1024, 1024])
    seg = {n: (o[i], o[i + 1]) for i, n in enumerate(
        ["xc", "vm", "ig", "fg", "op", "zm", "cq", "ckv", "kr", "za", "gm", "ga"])}

    def cols(n, a=0, b=None):
        s, e = seg[n]
        return w_in[:, s + a:(e if b is None else s + b)]

    z = lambda n: np.zeros((1024, n), f)
    units = [cols("xc", 0, 512), cols("xc", 512, 1024), cols("vm", 0, 512), cols("vm", 512, 1024),
             cols("op", 0, 512), cols("zm", 0, 512), cols("op", 512, 1024), cols("zm", 512, 1024),
             np.concatenate([cols("ckv"), cols("kr"), cols("ig"), cols("fg"), z(512 - 328)], axis=1),
             np.concatenate([cols("cq"), z(512 - QL)], axis=1),
             cols("za", 0, 512), cols("za", 512, 1024),
             cols("gm", 0, 512), cols("ga", 0, 512), w_pm[:, 0:512], w_pa[:, 0:512],
             cols("gm", 512, 1024), cols("ga", 512, 1024), w_pm[:, 512:1024], w_pa[:, 512:1024],
             w_out[:, 0:512], w_out[:, 512:1024]]
    assert len(units) == NUNIT
    wcat = np.stack([u.reshape(8, 128, 512).transpose(1, 0, 2) for u in units]).astype(f)
    wuq = w_uq[0].reshape(3, 128, HA, NOPE + ROPE).transpose(1, 0, 2, 3)
    wuqn = np.ascontiguousarray(wuq[..., :NOPE].reshape(128, 3, HA * NOPE))
    wuqr = np.ascontiguousarray(wuq[..., NOPE:].reshape(128, 3, HA * ROPE))
    wukv = w_ukv[0].reshape(2, 128, HA, NOPE + VD).transpose(1, 0, 2, 3)
    wukvk = np.ascontiguousarray(wukv[..., :NOPE].reshape(128, 2, HA * NOPE))
    wukvv = np.ascontiguousarray(wukv[..., NOPE:].reshape(128, 2, HA * VD))
    wq = np.ascontiguousarray(wq_m[0].reshape(HM, 2, 128, DH).transpose(2, 0, 1, 3))
    wk = np.ascontiguousarray(wk_m[0].reshape(HM, 2, 128, DH).transpose(2, 0, 1, 3))
    cf = np.zeros((128, 512), f)
    cf[:, 0:128] = np.eye(128, dtype=f)
    s_ = np.arange(128)[:, None]
    t_ = np.arange(128)[None, :]
    cf[:, 128:256] = ((s_ <= t_) & (s_ // 64 == t_ // 64)).astype(f)
    cf[:, 256:320] = ((s_ % 64) <= np.arange(64)[None, :]).astype(f)
    cf[:, 320:328] = norm_g[0].reshape(8, 128).T
    cf[:, 328:360] = conv_w[0].reshape(4, 8, 128).transpose(2, 1, 0).reshape(128, 32)
    cf[:, 360:368] = conv_b[0].reshape(8, 128).T
    cf[:, 368:371] = qn_g[0].reshape(3, 128).T
    cf[:, 371] = g_qn[0]
    cf[:, 372] = g_kn[0]
    cf[0:64, 373] = g_qr[0]
    cf[:, 374] = EPS
    cf[0:4, 384:512] = 1.0
    cbm = np.zeros((128, 448), f)
    cbm[:, 0:128] = np.eye(128, dtype=f)
    cbm[:, 128:256] = 1.0
    RT = np.zeros((64, 64), f)
    for i in range(32):
        RT[i + 32, i] = -1.0
        RT[i, i + 32] = 1.0
    cbm[0:64, 256:320] = RT
    bcm = np.concatenate([hnorm_g[0].reshape(-1), kvn_g[0], g_kr[0], b_if[0]]).astype(f)
    bcm = np.ascontiguousarray(np.broadcast_to(bcm[None, :], (128, bcm.shape[0])))
    cosT, sinT, cstm, sntm = _rope_tables()
    return dict(wcat=wcat, wuqn=wuqn, wuqr=wuqr, wukvk=wukvk, wukvv=wukvv, wq=wq, wk=wk, cf=cf, cb=cbm, bc=bcm,
                cosT=cosT, sinT=sinT, cstm=cstm, sntm=sntm)


_CACHE = {}


def kernel(x_prompt, x_sample, cache_ckv, cache_kr, state_conv, state_C, state_n, state_m,
           norm_g, w_in, b_if, conv_w, conv_b, wq_m, wk_m, hnorm_g,
           qn_g, w_uq, kvn_g, w_ukv, g_qn, g_qr, g_kn, g_kr, w_pm, w_pa, w_out):
    A = lambda a: np.ascontiguousarray(np.asarray(a, dtype=np.float32))
    x_prompt, x_sample = A(x_prompt), A(x_sample)
    B, S, _ = x_prompt.shape
    DB, DS, _ = x_sample.shape
    NCORE = 8
    bpc = B // NCORE
    P_ = np.asarray(cache_ckv).shape[2]
    seqs = [("prompt", S, i * S, 0) for i in range(bpc)] + [("sample", DS, bpc * S, P_)]
    ntok = bpc * S + DS
    shared = _prep_shared(*[A(a) for a in (norm_g, w_in, b_if, conv_w, conv_b, wq_m, wk_m, hnorm_g, qn_g, w_uq, kvn_g,
                                            w_ukv, g_qn, g_qr, g_kn, g_kr, w_pm, w_pa, w_out)])
    key = (tuple(seqs), ntok)
    if key not in _CACHE:
        _CACHE[key] = build(seqs, ntok)[0]
    nc = _CACHE[key]
    in_maps = []
    for c in range(NCORE):
        xs = np.concatenate([x_prompt[c * bpc:(c + 1) * bpc].reshape(bpc * S, D), x_sample[c]], axis=0)
        m = dict(shared)
        m.update(xs=np.ascontiguousarray(xs), cckv=A(cache_ckv)[0, c], ckr=A(cache_kr)[0, c],
                 sconv=A(state_conv)[0, c], sC=A(state_C)[0, c], sn=A(state_n)[0, c],
                 sm=A(state_m)[0, c].reshape(HM, 1))
        in_maps.append(m)
    res = run_bass_kernel_spmd(nc, in_maps, core_ids=list(range(NCORE)))
    R = res.results
    cat = lambda k: [r[k] for r in R]
    yp = np.stack([r["y"][:bpc * S].reshape(bpc, S, D) for r in R]).reshape(B, S, D)
    ys = np.stack([r["y"][bpc * S:] for r in R])
    ckv_p = np.stack([r["ckv_o"][:bpc * S].reshape(bpc, S, KVL) for r in R]).reshape(1, B, S, KVL)
    ckv_s = np.stack([r["ckv_o"][bpc * S:] for r in R])[None]
    kr_p = np.stack([r["kr_o"][:bpc * S].reshape(bpc, S, ROPE) for r in R]).reshape(1, B, S, ROPE)
    kr_s = np.stack([r["kr_o"][bpc * S:] for r in R])[None]
    conv_p = np.stack([r["conv_o"][:bpc] for r in R]).reshape(1, B, 3, D)
    conv_s = np.stack([r["conv_o"][bpc] for r in R])[None]
    C_p = np.stack([r["C_o"][:bpc] for r in R]).reshape(1, B, HM, DH, DH)
    C_s = np.stack([r["C_o"][bpc] for r in R])[None]
    n_p = np.stack([r["n_o"][:bpc] for r in R]).reshape(1, B, HM, DH)
    n_s = np.stack([r["n_o"][bpc] for r in R])[None]
    m_p = np.stack([r["m_o"][:bpc] for r in R]).reshape(1, B, HM)
    m_s = np.stack([r["m_o"][bpc] for r in R]).reshape(1, DB, HM)
    f = np.float32
    return tuple(np.ascontiguousarray(a, dtype=f) for a in
                 (yp, ys, ckv_p, kr_p, conv_p, C_p, n_p, m_p, ckv_s, kr_s, conv_s, C_s, n_s, m_s))
```

```python
import numpy as np
from contextlib import ExitStack
import concourse.bass as bass
import concourse.mybir as mybir
from concourse.bass_utils import run_bass_kernel_spmd

F32 = mybir.dt.float32
BF16 = mybir.dt.bfloat16
AF = mybir.ActivationFunctionType
ALU = mybir.AluOpType
AX = mybir.AxisListType

D = 1024
HM, DH = 4, 256
HA, NOPE, ROPE, VD = 8, 128, 64, 128
QL, KVL = 384, 256
EPS = 1e-6
ATT_SCALE = float((NOPE + ROPE) ** -0.5)
NUNIT = 22
RING = 3
NKVS = 4
NONLEGACY = ("A.", "C.", "D.")
NPOS = 4096 + 1024


class Sig:
    def __init__(self, sem, unit, name):
        self.sem, self.unit, self.name, self.n = sem, unit, name, 0


class Buf:
    def __init__(self, name, rng=None):
        self.name, self.rng = name, rng
        self.w = None
        self.r = {}
        self.over = []
        self.legacy = True
        self.psum = False


class Op:
    __slots__ = ("eng", "meth", "kw", "deps", "sig", "inc", "val", "dma", "w_bufs")

    def __init__(self, eng, meth, kw, sig, dma):
        self.eng, self.meth, self.kw, self.sig, self.dma = eng, meth, kw, sig, dma
        self.deps, self.inc, self.val = [], dma, 0


class Prog:
    def __init__(self, nc, es):
        self.nc, self.es = nc, es
        self.h = {"pe": nc.tensor, "act": nc.scalar, "dve": nc.vector, "pool": nc.gpsimd, "sp": nc.sync}
        self.sig = {k: Sig(es.enter_context(nc.semaphore("s_" + k)), 1, k) for k in ("pe", "act", "dve", "pool")}
        self.ops = []
        self.bufs = []
        self.dsigs = []

    def buf(self, name, rng=None):
        b = Buf(name, rng)
        if rng is not None:
            for o in self.bufs:
                if o.rng is not None and o.rng[0] == rng[0] and o.rng[1] < rng[2] and rng[1] < o.rng[2]:
                    o.over.append(b)
                    b.over.append(o)
        self.bufs.append(b)
        if any(name.startswith(p) for p in NONLEGACY):
            b.legacy = False
        return b

    def dma_sig(self, name):
        s = Sig(self.es.enter_context(self.nc.semaphore("d_" + name)), 16, name)
        self.dsigs.append(s)
        return s

    def _need(self, o, p, raw):
        if p is o:
            return False
        if not o.dma and not p.dma and o.eng == p.eng:
            return raw == "raw" and o.eng != "pe"
        if o.dma and p.dma and o.sig is p.sig and raw == "waw":
            return False
        return True

    def op(self, eng, meth, kw, reads=(), writes=(), sig=None):
        dma = sig is not None
        o = Op(eng, meth, kw, sig if dma else self.sig[eng], dma)
        deps = {}
        for b in reads:
            for x in [b] + b.over:
                if x.w is not None and self._need(o, x.w, "raw"):
                    deps[id(x.w)] = x.w
                if x.psum:
                    for r in x.r.values():
                        if r.eng != o.eng:
                            deps[id(r)] = r
        for b in writes:
            for x in [b] + b.over:
                for r in x.r.values():
                    if self._need(o, r, "war"):
                        deps[id(r)] = r
                if x.w is not None and self._need(o, x.w, "waw"):
                    deps[id(x.w)] = x.w
        for b in reads:
            for x in ([b] + b.over) if b.legacy else [b]:
                x.r[id(o.sig)] = o
        for b in writes:
            for x in ([b] + b.over) if b.legacy else [b]:
                x.w = o
                x.r = {}
        o.deps = list(deps.values())
        self.ops.append(o)
        return o

    def lower(self):
        for o in self.ops:
            for d in o.deps:
                d.inc = True
        for o in self.ops:
            if o.inc:
                o.sig.n += o.sig.unit
                o.val = o.sig.n
        seen = {k: {} for k in self.h}
        nwait = 0
        for o in self.ops:
            hd = self.h[o.eng]
            sn = seen[o.eng]
            best = {}
            for d in o.deps:
                k = id(d.sig)
                if sn.get(k, 0) < d.val and best.get(k, (0, None))[0] < d.val:
                    best[k] = (d.val, d.sig)
            for k, (v, s) in best.items():
                hd.wait_ge(s.sem, v)
                sn[k] = v
                nwait += 1
            ins = getattr(hd, o.meth)(**o.kw)
            if o.inc:
                ins.then_inc(o.sig.sem, o.sig.unit)
        for s in self.dsigs:
            if s.n > 0:
                self.nc.sync.wait_ge(s.sem, s.n)
        return len(self.ops), nwait


class LB:
    def __init__(self, ap, buf):
        self.ap, self.buf = ap, buf

    def __getitem__(self, k):
        return self.ap[k]


def build(seqs, ntok, debug=False):
    nc = bass.Bass("TRN2", target_bir_lowering=False)
    es = ExitStack()
    nseq = len(seqs)
    maxkeys = max(L + npast for (_, L, _, npast) in seqs)
    maxkeys = ((maxkeys + 511) // 512) * 512

    def din(name, shape, dt=F32):
        return nc.dram_tensor(name, list(shape), dt, kind="ExternalInput").ap()

    def dout(name, shape, dt=F32):
        return nc.dram_tensor(name, list(shape), dt, kind="ExternalOutput").ap()

    xs = din("xs", [ntok, D])
    wcat = din("wcat", [NUNIT, 128, 8, 512])
    wuqn_d = din("wuqn", [128, 3, 1024])
    wuqr_d = din("wuqr", [128, 3, 512])
    wukvk_d = din("wukvk", [128, 2, 1024])
    wukvv_d = din("wukvv", [128, 2, 1024])
    wq_d = din("wq", [128, 4, 2, 256])
    wk_d = din("wk", [128, 4, 2, 256])
    cf_d = din("cf", [128, 512])
    cb_d = din("cb", [128, 448])
    bc_d = din("bc", [128, 1024 + 256 + 64 + 8])
    cosT_d = din("cosT", [64, NPOS])
    sinT_d = din("sinT", [64, NPOS])
    cstm_d = din("cstm", [NPOS, 64])
    sntm_d = din("sntm", [NPOS, 64])
    cckv_d = din("cckv", [1024, KVL])
    ckr_d = din("ckr", [1024, ROPE])
    sconv_d = din("sconv", [3, D])
    sC_d = din("sC", [HM, DH, DH])
    sn_d = din("sn", [HM, DH])
    sm_d = din("sm", [HM, 1])

    y_d = dout("y", [ntok, D])
    ckv_o = dout("ckv_o", [ntok, KVL])
    kr_o = dout("kr_o", [ntok, ROPE])
    conv_o = dout("conv_o", [nseq, 3, D])
    C_o = dout("C_o", [nseq, HM, DH, DH])
    n_o = dout("n_o", [nseq, HM, DH])
    m_o = dout("m_o", [nseq, HM, 1])

    if debug:
        dbg_d = {n: nc.dram_tensor("dbg_" + n, [128, 8, 512], BF16, kind="ExternalOutput").ap() for n in ("hm", "ha", "mg", "qn", "kn", "hT")}
    wbf = nc.dram_tensor("wbf", [NUNIT, 128, 8 * 512], BF16, kind="Internal").ap()
    cckv_bf = nc.dram_tensor("cckv_bf", [1024, KVL], BF16, kind="Internal").ap()
    ckr_bf = nc.dram_tensor("ckr_bf", [1024, ROPE], BF16, kind="Internal").ap()
    kscr = nc.dram_tensor("kscr", [HA, 128, maxkeys], BF16, kind="Internal").ap()
    vscr = nc.dram_tensor("vscr", [HA, 128, maxkeys // 128, 128], BF16, kind="Internal").ap()
    rscr = nc.dram_tensor("rscr", [128, maxkeys], BF16, kind="Internal").ap()

    P = Prog(nc, es)
    dbg_sig = P.dma_sig("dbg") if debug else None

    def sbt(name, shape, dt):
        return es.enter_context(nc.sbuf_tensor(name, list(shape), dt))

    def pst(name, shape, dt):
        return es.enter_context(nc.psum_tensor(name, list(shape), dt))

    def pers(name, shape, dt):
        t = sbt(name, shape, dt)
        return LB(t[:], P.buf(name))

    ARENA_BYTES = 68 * 1024
    arena = sbt("arena", [128, ARENA_BYTES // 2], BF16)
    aoff = {}

    aofs = {}

    def ar(phase, name, shape, dt, at=None):
        nb = int(np.prod(shape[1:])) * (4 if dt == F32 else 2)
        nb = (nb + 63) // 64 * 64
        o = aoff.get(phase, 0) if at is None else at
        assert o + nb <= ARENA_BYTES, (phase, name, o + nb)
        if at is None:
            aoff[phase] = o + nb
        aofs[name] = o
        ap = arena[0:shape[0], o // 2:(o + nb) // 2]
        n_el = int(np.prod(shape[1:]))
        if dt == F32:
            ap = ap.bitcast(F32)[:, 0:n_el]
        else:
            ap = ap[:, 0:n_el]
        if len(shape) > 2:
            names = " ".join("d%d" % i for i in range(1, len(shape)))
            ap = ap.rearrange("p (%s) -> p %s" % (names, names), **{"d%d" % i: shape[i] for i in range(2, len(shape))})
        return LB(ap, P.buf(phase + "." + name, ("arena", o, o + nb)))

    pb = []
    for i in range(6):
        t = pst("pb%d" % i, [128, 512], F32)
        pb.append(LB(t[:], P.buf("pb%d" % i)))
        pb[-1].buf.psum = True
    t = pst("ptr", [128, 1024], BF16)
    ptr = LB(t[:], P.buf("ptr"))
    ptr.buf.psum = True
    t = pst("pmisc", [128, 512], F32)
    pmisc = LB(t[:], P.buf("pmisc"))
    pmisc.buf.psum = True
    pms = [LB(pb[k].ap[:, 0:64], pb[k].buf) for k in range(6)]

    ring = [pers("ring%d" % i, [128, 8, 512], BF16) for i in range(RING)]
    ring_sig = [P.dma_sig("ring%d" % i) for i in range(RING)]
    xin = [pers("xin%d" % i, [128, D], F32) for i in range(2)]
    xin_sig = [P.dma_sig("xin%d" % i) for i in range(2)]
    xn = pers("xn", [128, D], BF16)
    junk = pers("junk", [128, D], BF16)
    hT = pers("hT", [128, 8, 512], BF16)
    hmT = pers("hmT", [128, 8, 512], BF16)
    haT = pers("haT", [128, 8, 512], BF16)
    Cst_t = sbt("Cst", [128, HM, 2, 257], F32)
    Cst = LB(Cst_t[:], None)
    CstH = [LB(Cst_t[:, h], P.buf("Cst%d" % h)) for h in range(HM)]
    Cbf = [[pers("Cbf%d_%d" % (h, dc), [128, 257], BF16) for dc in range(2)] for h in range(HM)]
    hist = pers("hist", [128, 8, 3], F32)
    wuqn = pers("wuqn_s", [128, 3, 1024], BF16)
    wuqr = pers("wuqr_s", [128, 3, 512], BF16)
    wukvk = pers("wukvk_s", [128, 2, 1024], BF16)
    wukvv = pers("wukvv_s", [128, 2, 1024], BF16)
    wq = pers("wq_s", [128, 4, 2, 256], BF16)
    wk = pers("wk_s", [128, 4, 2, 256], BF16)
    cf = pers("cf_s", [128, 512], F32)
    cb = pers("cb_s", [128, 448], BF16)
    bc = pers("bc_s", [128, 1024 + 256 + 64 + 8], F32)
    cosT = pers("cosT_s", [64, 512], F32)
    sinT = pers("sinT_s", [64, 512], F32)
    cstm = pers("cstm_s", [128, 4, 64], F32)
    sntm = pers("sntm_s", [128, 4, 64], F32)
    tab_sig = P.dma_sig("tab")
    small = pers("small", [128, 64], F32)
    smallb = [P.buf("small%d" % i) for i in range(8)]
    gcol = pers("gcol", [128, 4, 16], F32)
    grow = pers("grow", [4, 2, 512], F32)
    mall = pers("mall", [4, 40], F32)
    decbd = pers("decbd", [4, 8, 4], F32)
    decbc = pers("decbc", [128, 8, 4], F32)
    wkc = pers("wkc", [128, 4, 12], F32)
    init_sig = P.dma_sig("init")
    sigC = P.dma_sig("stC")
    sigH = P.dma_sig("stH")
    sigM = P.dma_sig("stM")

    identf = cf[:, 0:128]
    tri = cf[:, 128:256]
    cmask = cf[:, 256:320]
    ng_c = cf[:, 320:328]
    cw_c = cf[:, 328:360]
    cbias_c = cf[:, 360:368]
    qng_c = cf[:, 368:371]
    gqn_c = cf[:, 371:372]
    gkn_c = cf[:, 372:373]
    gqr_c = cf[:, 373:374]
    eye4 = cf[0:4, 0:4]
    ones4 = cf[0:4, 384:512]
    identb = cb[:, 0:128]
    onesb = cb[:, 128:256]
    RTb = cb[0:64, 256:320]
    hng_bc = bc[:, 0:1024]
    kvng_bc = bc[:, 1024:1280]
    gkr_bc = bc[:, 1280:1344]
    bif_bc = bc[:, 1344:1352]

    def bl(xs_):
        return [x.buf if isinstance(x, LB) else x for x in xs_]

    def MM(out, lhsT, rhs, start, stop, R, W):
        P.op("pe", "matmul", dict(out=out, lhsT=lhsT, rhs=rhs, start=start, stop=stop), bl(R), bl(W))

    def TR(out, in_, ident, R, W):
        P.op("pe", "transpose", dict(out=out, in_=in_, identity=ident), bl(R), bl(W))

    def ACT(out, in_, func, R, W, **kw):
        P.op("act", "activation", dict(out=out, in_=in_, func=func, **kw), bl(R), bl(W))

    def TT(out, in0, in1, op, R, W, eng="dve"):
        P.op(eng, "tensor_tensor", dict(out=out, in0=in0, in1=in1, op=op), bl(R), bl(W))

    def TS(out, in0, s1, s2, op0, op1, R, W, eng="dve"):
        P.op(eng, "tensor_scalar", dict(out=out, in0=in0, scalar1=s1, scalar2=s2, op0=op0, op1=op1), bl(R), bl(W))

    def TSA(out, in0, s1, R, W, eng="dve"):
        P.op(eng, "tensor_scalar_add", dict(out=out, in0=in0, scalar1=s1), bl(R), bl(W))

    def TSM(out, in0, s1, R, W, eng="dve"):
        P.op(eng, "tensor_scalar_mul", dict(out=out, in0=in0, scalar1=s1), bl(R), bl(W))

    def STT(out, in0, scalar, in1, op0, op1, R, W, eng="dve"):
        P.op(eng, "scalar_tensor_tensor", dict(out=out, in0=in0, scalar=scalar, in1=in1, op0=op0, op1=op1), bl(R), bl(W))

    def CP(out, in_, R, W, eng="dve"):
        P.op(eng, "tensor_copy", dict(out=out, in_=in_), bl(R), bl(W))

    def RCP(out, in_, R, W):
        P.op("dve", "reciprocal", dict(out=out, in_=in_), bl(R), bl(W))

    def MEMSET(ap, v, W, eng="dve"):
        P.op(eng, "memset", dict(ap=ap, constant=v), [], bl(W))

    def DMA(q, out, in_, R, W, sig, **kw):
        P.op(q, "dma_start", dict(out=out, in_=in_, **kw), bl(R), bl(W), sig=sig)

    def rsqrt_act(out, in_, scale, R, W, tmpbuf):
        ACT(out, in_, AF.Ln, R, W, scale=scale, bias=eps_ap(out))
        ACT(out, out, AF.Exp, W, W, scale=-0.5)

    epsb = P.buf("epsb")

    def eps_ap(like):
        np_ = like.shape[0]
        p0 = like.base_partition() if hasattr(like, "base_partition") else 0
        return cf[p0:p0 + np_, 374:375]

    init_sig2 = P.dma_sig("init2")
    DMA("sp", cf.ap, cf_d, [], [cf], init_sig2)
    DMA("sp", bc.ap, bc_d, [], [bc], init_sig2)
    for b_ in (cf, bc):
        b_.buf.w = P.ops[-1]
    DMA("pool", cb.ap, cb_d, [], [cb], init_sig)
    DMA("pool", wuqn.ap, wuqn_d, [], [wuqn], init_sig)
    DMA("pool", wuqr.ap, wuqr_d, [], [wuqr], init_sig)
    DMA("pool", wukvk.ap, wukvk_d, [], [wukvk], init_sig)
    DMA("pool", wukvv.ap, wukvv_d, [], [wukvv], init_sig)
    DMA("pool", wq.ap, wq_d, [], [wq], init_sig)
    DMA("pool", wk.ap, wk_d, [], [wk], init_sig)
    for b_ in (cb, wuqn, wuqr, wukvk, wukvv, wq, wk):
        b_.buf.w = P.ops[-1]

    wbf_sig = P.dma_sig("wbf")
    wbfB = P.buf("wbfB")
    wbfA_sig = P.dma_sig("wbfA")
    wbfA = P.buf("wbfA")
    NEARLY = 4
    for k_ in range(NUNIT):
        DMA("pool", wbf[k_], wcat[k_].rearrange("p k c -> p (k c)"), [], [wbfA if k_ < NEARLY else wbfB],
            wbfA_sig if k_ < NEARLY else wbf_sig)
    DMA("pool", cckv_bf, cckv_d, [], [wbfB], wbf_sig)
    DMA("pool", ckr_bf, ckr_d, [], [wbfB], wbf_sig)
    ring_state = {"issued": 0, "total": 0}

    def ring_prefetch(upto):
        while ring_state["issued"] < min(upto + 1, ring_state["total"]):
            k = ring_state["issued"]
            s = k % RING
            DMA("sp", ring[s].ap, wbf[k % NUNIT].rearrange("p (k c) -> p k c", k=8),
                [wbfA if (k % NUNIT) < NEARLY else wbfB], [ring[s]], ring_sig[s])
            ring_state["issued"] += 1

    def unit(k, oldest=None):
        ring_prefetch((k if oldest is None else oldest) + RING - 1)
        return ring[k % RING]

    uhT = ar("A", "uhT", [128, 8, 512], BF16)
    qTm = ar("A", "qTm", [128, 4, 2, 512], BF16)
    kTm = ar("A", "kTm", [128, 4, 2, 512], BF16)
    kw = ar("A", "kw", [128, 4, 4, 256], BF16)
    gmg = ar("A", "gmg", [128, 4, 1024], BF16)
    xcb = [ar("A", "xcb%d" % i, [128, 515], F32) for i in range(2)]
    ucv = [ar("A", "ucv%d" % i, [128, 512], F32) for i in range(2)]
    ta = [ar("A", "ta%d" % i, [128, 512], F32) for i in range(3)]
    tb = [ar("A", "tb%d" % i, [128, 512], F32) for i in range(3)]
    tg = [ta, tb]
    _o = aofs["xcb0"]
    hraw = [ar("A", "hraw%d" % i, [128, 4, 257], F32, at=_o + i * 4160) for i in range(2)]
    hh = ar("A", "hh", [128, 4, 256], F32, at=_o + 8320)
    hmg = ar("A", "hmg", [128, 1024], BF16, at=_o + 8320 + 4096)
    assert _o + 8320 + 4096 + 2048 <= aoff["A"]
    qnT = ar("C", "qnT", [128, 8, 512], BF16)
    qrT = ar("C", "qrT", [128, 8, 512], BF16)
    knT = ar("C", "knT", [128, 8, 512], BF16)
    Vcur = ar("C", "Vcur", [128, 4, 1024], BF16)
    cqf = ar("C", "cqf", [128, 3, 512], F32)
    cqn = ar("C", "cqn", [128, 3, 512], BF16)
    sqb = [ar("C", "sqb%d" % i, [128, 512], BF16) for i in range(3)]
    rr = [ar("C", "rr%d" % i, [128, 512], F32) for i in range(2)]
    tc_ = [ar("C", "tc%d" % i, [128, 512], F32) for i in range(2)]
    xrb = ar("C", "xrb", [64, 512], BF16)
    ckvnT = pers("ckvnT", [128, 2, 512], BF16)
    krT = pers("krT", [128, 512], BF16)
    ckvf = [ar("A", "ckvf%d" % i, [128, 256], F32) for i in range(2)]
    ckvb = ar("A", "ckvb", [128, 256], BF16)
    krf = [ar("A", "krf%d" % i, [128, 64], F32) for i in range(2)]
    krw = [ar("A", "krw%d" % i, [128, 64], F32) for i in range(3)]
    krb = ar("A", "krb", [128, 64], BF16)
    kvs = [dict(K=ar("C", "kvK%d" % i, [128, 512], BF16), R=ar("C", "kvR%d" % i, [128, 512], BF16),
                V=ar("C", "kvV%d" % i, [128, 4, 128], BF16)) for i in range(NKVS)]
    kvs_sig = [P.dma_sig("kvs%d" % i) for i in range(NKVS)]
    _k0 = aofs["kvK0"]
    qsq = [sqb[0], sqb[1], sqb[2], ar("C", "qsq3", [128, 512], BF16, at=_k0)]
    qrr = [rr[0], rr[1], ar("C", "qrr2", [128, 512], F32, at=_k0 + 1024), ar("C", "qrr3", [128, 512], F32, at=_k0 + 3072)]
    qxr = [xrb, ar("C", "qxr1", [64, 512], BF16, at=_k0 + 5120)]
    qtc = [tc_[0], tc_[1], ar("C", "qtc2", [128, 512], F32, at=_k0 + 6144), ar("C", "qtc3", [128, 512], F32, at=_k0 + 8192)]
    assert _k0 + 10240 <= aofs["kvK0"] + NKVS * 3072
    Pt = [pers("Pt%d" % i, [128, 512], BF16) for i in range(3)]
    Pacc = [pers("Pacc%d" % i, [128, 512], F32) for i in range(2)]
    pS3 = [pb[2], pb[3], LB(ptr.ap.bitcast(F32), ptr.buf)]
    swA = ar("C", "swA", [128, 4, 4, 64], BF16)
    pcb = ar("C", "pcb", [128, 256], BF16)
    pcr = ar("C", "pcr", [128, 64], BF16)
    sgm = ar("D", "sgm", [128, 4, 512], F32)
    sga = ar("D", "sga", [128, 4, 512], F32)
    tmpm = ar("D", "tmpm", [128, 4, 512], F32)
    td = [ar("D", "td%d" % i, [128, 512], F32) for i in range(2)]
    mergedT = ar("D", "mergedT", [128, 8, 512], BF16)
    yout = [ar("D", "yout%d" % i, [128, 1024], F32) for i in range(2)]
    xres = [ar("D", "xres%d" % i, [128, D], F32) for i in range(4)]
    xres_sig = [P.dma_sig("xres%d" % i) for i in range(4)]
    v_ext = pers("v_ext", [128, 4, 4, 257], BF16)
    Pd = [pers("Pd%d" % i, [128, 512], BF16) for i in range(4)]

    scr_sig = {k: P.dma_sig("scrw" + k) for k in "KRV"}
    ckvo_sig = [P.dma_sig("ckvo%d" % i) for i in range(2)]
    kro_sig = [P.dma_sig("kro%d" % i) for i in range(2)]
    yo_sig = [P.dma_sig("yo%d" % i) for i in range(2)]
    pcb_sig = P.dma_sig("pcb")
    pcr_sig = P.dma_sig("pcr")
    kscr_b = {}

    def scr_buf(si, kb):
        key = ("scr", kb)
        if key not in kscr_b:
            kscr_b[key] = {k: P.buf("scr%s_%d" % (k, kb)) for k in "KRV"}
        return kscr_b[key]

    MEMSET(v_ext[:, :, :, 256:257], 1.0, [v_ext])
    MEMSET(krT[64:128, :], 0.0, [krT])
    for i in range(4):
        MEMSET(Pd[i].ap, 0.0, [Pd[i]])

    def kv_block(si, kb, NK, to_scratch):
        TSk = min(NK, 128)
        for h in range(HA):
            pk = pb[h % 2]
            for c2 in range(2):
                MM(pk[:, 0:NK], wukvk[:, c2, h * 128:(h + 1) * 128], ckvnT[:, c2, 0:NK], c2 == 0, c2 == 1,
                   [wukvk, ckvnT], [pk])
            sq = sqb[h % 2]
            ACT(sq[:, 0:NK], pk[:, 0:NK], AF.Square, [pk], [sq])
            ps = pb[2 + h % 2]
            MM(ps[:, 0:NK], onesb, sq[:, 0:NK], True, True, [cb, sq], [ps])
            r = rr[h % 2]
            rsqrt_act(r[:, 0:NK], ps[:, 0:NK], 1.0 / NOPE, [ps, cf], [r], None)
            STT(knT[:, h, 0:NK], pk[:, 0:NK], gkn_c, r[:, 0:NK], ALU.mult, ALU.mult, [pk, r, cf], [knT])
        for jt in range((NK + 127) // 128):
            for g in range(2):
                pv = pb[4 + g]
                for c2 in range(2):
                    MM(pv[0:TSk, :], ckvnT[:, c2, jt * 128:jt * 128 + TSk], wukvv[:, c2, g * 512:(g + 1) * 512],
                       c2 == 0, c2 == 1, [ckvnT, wukvv], [pv])
                ACT(Vcur[0:TSk, jt, g * 512:(g + 1) * 512], pv[0:TSk, :], AF.Copy, [pv], [Vcur])
        if to_scratch:
            sb_ = scr_buf(si, kb)
            k0 = kb * 512
            DMA("sp", kscr[:, :, k0:k0 + NK].rearrange("h d k -> d h k"), knT[:, :, 0:NK], [knT], [sb_["K"]], scr_sig["K"])
            DMA("sp", rscr[:, k0:k0 + NK], krT[:, 0:NK], [krT], [sb_["R"]], scr_sig["R"])
            nt = NK // 128
            for t_ in range(nt):
                DMA("sp", vscr[:, :, kb * 4 + t_, :].rearrange("h p d -> p h d"),
                    Vcur[:, t_, :].rearrange("p (h d) -> p h d", h=HA), [Vcur], [sb_["V"]], scr_sig["V"])

    def attention(si, T, nkb_past, ucount):
        TSq = min(T, 128)
        ntl = (T + 127) // 128
        loads = [(h_, kb_) for h_ in range(HA) for kb_ in range(nkb_past)]
        nload = [0]

        def issue_load():
            li = nload[0]
            if li >= len(loads):
                return
            nload[0] += 1
            h_, kb_ = loads[li]
            s_ = li % NKVS
            sl = kvs[s_]
            slb = [sl["K"], sl["R"], sl["V"]]
            sb_ = scr_buf(si, kb_)
            k0 = kb_ * 512
            DMA("sp", sl["K"].ap, kscr[h_, :, k0:k0 + 512], list(sb_.values()), slb, kvs_sig[s_])
            DMA("sp", sl["R"].ap, rscr[:, k0:k0 + 512], list(sb_.values()), slb, kvs_sig[s_])
            DMA("sp", sl["V"].ap, vscr[h_, :, kb_ * 4:kb_ * 4 + 4, :], list(sb_.values()), slb, kvs_sig[s_])

        for _ in range(NKVS - 1):
            issue_load()
        pending = [None]
        for h in range(HA):
            po, pden = (pb[4], pb[5]) if h % 2 == 0 else (pb[0], pb[1])
            tasks = []
            for kb in range(nkb_past):
                li = h * nkb_past + kb
                sl = kvs[li % NKVS]
                slb = [sl["K"], sl["R"], sl["V"]]
                for jt in range(4):
                    tasks.append(("past", sl, slb, jt))
            for jt in range(ntl):
                tasks.append(("diag", None, None, jt))
            nt_ = len(tasks)

            pacc = Pacc[h % 2]

            def emit_S(i):
                kind_, sl, slb, jt = tasks[i]
                pS = pS3[i % 3]
                if kind_ == "past":
                    MM(pS[:, 0:T], sl["K"][:, jt * 128:(jt + 1) * 128], qnT[:, h, 0:T], True, False, slb + [qnT], [pS])
                    MM(pS[:, 0:T], sl["R"][:, jt * 128:(jt + 1) * 128], qrT[:, h, 0:T], False, True,
                       slb + [qrT], [pS])
                else:
                    c0 = jt * 128
                    MM(pS[0:TSq, c0:T], knT[:, h, c0:c0 + TSq], qnT[:, h, c0:T], True, False, [knT, qnT], [pS])
                    MM(pS[0:TSq, c0:T], krT[:, c0:c0 + TSq], qrT[:, h, c0:T], False, True, [krT, qrT], [pS])

            def emit_rest(i):
                kind_, sl, slb, jt = tasks[i]
                pS = pS3[i % 3]
                first, last = i == 0, i == nt_ - 1
                if kind_ == "past":
                    pt_ = Pt[i % 3]
                    ACT(pt_[:, 0:T], pS[:, 0:T], AF.Exp, [pS], [pt_], scale=ATT_SCALE)
                    MM(po[:, 0:T], sl["V"][:, jt, :], pt_[:, 0:T], first, last, slb + [pt_], [po])
                    if first:
                        CP(pacc[:, 0:T], pt_[:, 0:T], [pt_], [pacc])
                    else:
                        TT(pacc[:, 0:T], pacc[:, 0:T], pt_[:, 0:T], ALU.add, [pacc, pt_], [pacc])
                else:
                    c0 = jt * 128
                    pd_ = Pd[jt]
                    ACT(pd_[0:64, c0:T], pS[0:64, c0:T], AF.Exp, [pS], [pd_], scale=ATT_SCALE)
                    if TSq == 128:
                        ACT(pd_[64:128, c0 + 64:T], pS[64:128, c0 + 64:T], AF.Exp, [pS], [pd_], scale=ATT_SCALE)
                    MM(po[:, c0:T], Vcur[0:TSq, jt, h * 128:(h + 1) * 128], pd_[0:TSq, c0:T], first, last,
                       [Vcur, pd_], [po])
                    if first:
                        if TSq < 128:
                            MEMSET(pacc[:, 0:T], 0.0, [pacc], eng="dve")
                        CP(pacc[0:TSq, 0:T], pd_[0:TSq, 0:T], [pd_], [pacc])
                    else:
                        TT(pacc[0:TSq, c0:T], pacc[0:TSq, c0:T], pd_[0:TSq, c0:T], ALU.add, [pacc, pd_], [pacc])

            emit_S(0)
            if nt_ > 1:
                emit_S(1)
            if pending[0] is not None:
                pending[0]()
                pending[0] = None
            for i in range(nt_):
                if i + 2 < nt_:
                    emit_S(i + 2)
                emit_rest(i)
                if tasks[i][0] == "past" and tasks[i][3] == 1:
                    issue_load()
            def fin(h=h, po=po, pden=pden, pacc=pacc):
                hi_, lo_ = sqb[0], sqb[1]
                CP(hi_[:, 0:T], pacc[:, 0:T], [pacc], [hi_])
                TT(lo_[:, 0:T], pacc[:, 0:T], hi_[:, 0:T], ALU.subtract, [pacc, hi_], [lo_])
                MM(pden[:, 0:T], onesb, hi_[:, 0:T], True, False, [cb, hi_], [pden])
                MM(pden[:, 0:T], onesb, lo_[:, 0:T], False, True, [cb, lo_], [pden])
                u = unit(ucount + 10 + h // 4)
                pz = pmisc
                for kc in range(8):
                    MM(pz[:, 0:T], u[:, kc, (h % 4) * 128:(h % 4 + 1) * 128], hT[:, kc, 0:T], kc == 0, kc == 7,
                       [u, hT], [pz])
                e, zf, d1, n1 = tc_[0], tc_[1], rr[0], rr[1]
                ACT(e[:, 0:T], pz[:, 0:T], AF.Exp, [pz], [e], scale=-1.0)
                ACT(zf[:, 0:T], pz[:, 0:T], AF.Copy, [pz], [zf])
                STT(d1[:, 0:T], e[:, 0:T], 1.0, pden[:, 0:T], ALU.add, ALU.mult, [e, pden], [d1])
                ACT(d1[:, 0:T], d1[:, 0:T], AF.Ln, [d1], [d1])
                ACT(d1[:, 0:T], d1[:, 0:T], AF.Exp, [d1], [d1], scale=-1.0)
                TT(n1[:, 0:T], po[:, 0:T], zf[:, 0:T], ALU.mult, [po, zf], [n1])
                TT(haT[:, h, 0:T], n1[:, 0:T], d1[:, 0:T], ALU.mult, [n1, d1], [haT])
            pending[0] = fin
        if pending[0] is not None:
            pending[0]()
            pending[0] = None

    def block(si, kind, row0, T, pos0, nkb_past, kb_cur, ucount, last_block, write_scratch):
        TSz = min(T, 128)
        ntl = (T + 127) // 128
        NCH = T // 64
        DMA("sp", cosT[:, 0:T], cosT_d[:, pos0:pos0 + T], [], [cosT], tab_sig)
        DMA("sp", sinT[:, 0:T], sinT_d[:, pos0:pos0 + T], [], [cosT], tab_sig)
        for j in range(ntl):
            DMA("sp", cstm[0:TSz, j, :], cstm_d[pos0 + j * 128:pos0 + j * 128 + TSz, :], [], [cosT], tab_sig)
            DMA("sp", sntm[0:TSz, j, :], sntm_d[pos0 + j * 128:pos0 + j * 128 + TSz, :], [], [cosT], tab_sig)
        for j in range(ntl):
            xi = xin[j % 2]
            r0 = row0 + j * 128
            DMA("sp", xi[0:TSz, :], xs[r0:r0 + TSz, :], [], [xi], xin_sig[j % 2])
            sx = small[0:TSz, 0:1]
            ACT(junk[0:TSz, :], xi[0:TSz, :], AF.Square, [xi], [junk, smallb[0]], accum_out=sx)
            rsqrt_act(sx, sx, 1.0 / D, [smallb[0], cf], [smallb[0]], None)
            ACT(xn[0:TSz, :], xi[0:TSz, :], AF.Copy, [xi, smallb[0]], [xn], scale=sx)
            for kc in range(8):
                TR(ptr[:, kc * 128:kc * 128 + TSz], xn[0:TSz, kc * 128:(kc + 1) * 128], identb[0:TSz, 0:TSz],
                   [xn, cb], [ptr])
            TT(hT[:, :, j * 128:j * 128 + TSz], ptr.ap.rearrange("p (k t) -> p k t", k=8)[:, :, 0:TSz],
               ng_c.unsqueeze(2).to_broadcast([128, 8, TSz]), ALU.mult, [ptr, cf], [hT])

        def conv_front(cc):
            u = unit(ucount + cc // 4)
            pc = pb[cc % 2]
            for kc in range(8):
                MM(pc[:, 0:T], u[:, kc, (cc % 4) * 128:(cc % 4 + 1) * 128], hT[:, kc, 0:T], kc == 0, kc == 7,
                   [u, hT], [pc])
            xb_ = xcb[cc % 2]
            CP(xb_[:, 0:3], hist[:, cc, :], [hist], [xb_])
            ACT(xb_[:, 3:3 + T], pc[:, 0:T], AF.Copy, [pc], [xb_])

        def conv_silu(cc):
            uc, e = ucv[cc % 2], ta[cc % 2]
            ACT(e[:, 0:T], uc[:, 0:T], AF.Exp, [uc], [e], scale=-1.0)
            ACT(e[:, 0:T], e[:, 0:T], AF.Ln, [e], [e], bias=1.0)
            ACT(e[:, 0:T], e[:, 0:T], AF.Exp, [e], [e], scale=-1.0)

        def conv_taps(cc):
            xb_, uc = xcb[cc % 2], ucv[cc % 2]
            CP(hist[:, cc, :], xb_[:, T:T + 3], [xb_], [hist])
            TS(uc[:, 0:T], xb_[:, 0:T], cw_c[:, cc * 4:cc * 4 + 1], cbias_c[:, cc:cc + 1], ALU.mult, ALU.add,
               [xb_, cf], [uc])
            for jj in range(1, 4):
                STT(uc[:, 0:T], xb_[:, jj:jj + T], cw_c[:, cc * 4 + jj:cc * 4 + jj + 1], uc[:, 0:T], ALU.mult, ALU.add,
                    [xb_, uc, cf], [uc])

        def conv_mult(cc):
            uc, e = ucv[cc % 2], ta[cc % 2]
            TT(uhT[:, cc, 0:T], uc[:, 0:T], e[:, 0:T], ALU.mult, [uc, e], [uhT])

        for k in range(9):
            if k < 8:
                conv_front(k)
            if k >= 1:
                conv_silu(k - 1)
            if k < 8:
                conv_taps(k)
            if k >= 1:
                conv_mult(k - 1)

        for g in range(2):
            u = unit(ucount + 2 + g)
            for j in range(ntl):
                pv = pb[(g * ntl + j) % 2]
                for kc in range(8):
                    MM(pv[0:TSz, :], hT[:, kc, j * 128:j * 128 + TSz], u[:, kc, :], kc == 0, kc == 7, [hT, u], [pv])
                ACT(v_ext[0:TSz, j, 2 * g:2 * g + 2, 0:256], pv[0:TSz, :].rearrange("p (h e) -> p h e", h=2), AF.Copy,
                    [pv], [v_ext])
        g_items = [(g, j) for g in range(2) for j in range(ntl)]
        g_units = {}

        def gate_front(k):
            g, j = g_items[k]
            if g not in g_units:
                uo_ = unit(ucount + 4 + 2 * g)
                uz_ = unit(ucount + 5 + 2 * g, oldest=ucount + 4 + 2 * g)
                g_units[g] = (uo_, uz_)
            uo, uz = g_units[g]
            po_, pz_ = (pb[0], pb[1]) if k % 2 == 0 else (pb[2], pb[3])
            for kc in range(8):
                MM(po_[0:TSz, :], hT[:, kc, j * 128:j * 128 + TSz], uo[:, kc, :], kc == 0, kc == 7, [hT, uo], [po_])
            for kc in range(8):
                MM(pz_[0:TSz, :], hT[:, kc, j * 128:j * 128 + TSz], uz[:, kc, :], kc == 0, kc == 7, [hT, uz], [pz_])
            eo, ez, g1 = tg[k % 2]
            ACT(eo[0:TSz, :], po_[0:TSz, :], AF.Exp, [po_], [eo], scale=-1.0)
            ACT(ez[0:TSz, :], pz_[0:TSz, :], AF.Exp, [pz_], [ez], scale=-1.0)
            TT(g1[0:TSz, :], pz_[0:TSz, :], hng_bc[0:TSz, g * 512:(g + 1) * 512], ALU.mult, [pz_, bc], [g1])

        def gate_mid(k):
            eo, ez, g1 = tg[k % 2]
            ACT(eo[0:TSz, :], eo[0:TSz, :], AF.Ln, [eo], [eo], bias=1.0)
            ACT(ez[0:TSz, :], ez[0:TSz, :], AF.Ln, [ez], [ez], bias=1.0)
            TT(eo[0:TSz, :], eo[0:TSz, :], ez[0:TSz, :], ALU.add, [eo, ez], [eo])

        def gate_back(k):
            g, j = g_items[k]
            eo, ez, g1 = tg[k % 2]
            ACT(eo[0:TSz, :], eo[0:TSz, :], AF.Exp, [eo], [eo], scale=-1.0)
            TT(gmg[0:TSz, j, g * 512:(g + 1) * 512], g1[0:TSz, :], eo[0:TSz, :], ALU.mult, [g1, eo], [gmg])

        for k in range(len(g_items) + 1):
            if k < len(g_items):
                gate_front(k)
            if k >= 1:
                gate_back(k - 1)
            if k < len(g_items):
                gate_mid(k)

        u8 = unit(ucount + 8)
        for j in range(ntl):
            pk = pb[j % 2]
            r0 = row0 + j * 128
            for kc in range(8):
                MM(pk[0:TSz, 0:328], hT[:, kc, j * 128:j * 128 + TSz], u8[:, kc, 0:328], kc == 0, kc == 7, [hT, u8], [pk])
            s1, s2 = small[0:TSz, 1:2], small[0:TSz, 2:3]
            gb = smallb[3]
            cf_, kf = ckvf[j % 2], krf[j % 2]
            xr, xsw, tA = krw[0], krw[1], krw[2]
            ACT(junk[0:TSz, 0:256], pk[0:TSz, 0:256], AF.Square, [pk], [junk, smallb[1]], accum_out=s1)
            ACT(junk[0:TSz, 256:320], pk[0:TSz, 256:320], AF.Square, [pk], [junk, smallb[2]], accum_out=s2)
            TT(gcol[0:TSz, j, 0:8], pk[0:TSz, 320:328], bif_bc[0:TSz, :], ALU.add, [pk, bc], [gb])
            ACT(s1, s1, AF.Ln, [smallb[1], cf], [smallb[1]], scale=1.0 / KVL, bias=eps_ap(s1))
            ACT(s2, s2, AF.Ln, [smallb[2], cf], [smallb[2]], scale=1.0 / ROPE, bias=eps_ap(s2))
            ACT(gcol[0:TSz, j, 8:12], gcol[0:TSz, j, 4:8], AF.Exp, [gb], [gb], scale=-1.0)
            ACT(s1, s1, AF.Exp, [smallb[1]], [smallb[1]], scale=-0.5)
            ACT(s2, s2, AF.Exp, [smallb[2]], [smallb[2]], scale=-0.5)
            ACT(gcol[0:TSz, j, 8:12], gcol[0:TSz, j, 8:12], AF.Ln, [gb], [gb], bias=1.0)
            STT(cf_[0:TSz, :], pk[0:TSz, 0:256], s1, kvng_bc[0:TSz, :], ALU.mult, ALU.mult, [pk, smallb[1], bc], [cf_])
            STT(xr[0:TSz, :], pk[0:TSz, 256:320], s2, gkr_bc[0:TSz, :], ALU.mult, ALU.mult, [pk, smallb[2], bc], [xr])
            MM(pmisc[0:TSz, 0:4], tri[0:TSz, 0:TSz], gcol[0:TSz, j, 8:12], True, True, [cf, gb], [pmisc])
            DMA("sp", ckv_o[r0:r0 + TSz, :], cf_[0:TSz, :], [cf_], [], ckvo_sig[j % 2])
            ACT(ckvb[0:TSz, :], cf_[0:TSz, :], AF.Copy, [cf_], [ckvb])
            CP(xsw[0:TSz, 0:32], xr[0:TSz, 32:64], [xr], [xsw])
            CP(xsw[0:TSz, 32:64], xr[0:TSz, 0:32], [xr], [xsw])
            TT(tA[0:TSz, :], xr[0:TSz, :], cstm[0:TSz, j, :], ALU.mult, [xr, cosT], [tA])
            TT(xsw[0:TSz, :], xsw[0:TSz, :], sntm[0:TSz, j, :], ALU.mult, [xsw, cosT], [xsw])
            TT(kf[0:TSz, :], tA[0:TSz, :], xsw[0:TSz, :], ALU.add, [tA, xsw], [kf])
            CP(gcol[0:TSz, j, 12:16], pmisc[0:TSz, 0:4], [pmisc], [gb])
            TT(gcol[0:TSz, j, 4:8], gcol[0:TSz, j, 0:4], gcol[0:TSz, j, 12:16], ALU.add, [gb], [gb])
            for c2 in range(2):
                TR(ptr[:, c2 * 128:c2 * 128 + TSz], ckvb[0:TSz, c2 * 128:(c2 + 1) * 128], identb[0:TSz, 0:TSz],
                   [ckvb, cb], [ptr])
            DMA("sp", kr_o[r0:r0 + TSz, :], kf[0:TSz, :], [kf], [], kro_sig[j % 2])
            ACT(krb[0:TSz, :], kf[0:TSz, :], AF.Copy, [kf], [krb])
            TR(pmisc[0:4, 128:128 + TSz], gcol[0:TSz, j, 4:8], identf[0:TSz, 0:TSz], [gb, cf], [pmisc])
            TR(pmisc[0:4, 256:256 + TSz], gcol[0:TSz, j, 12:16], identf[0:TSz, 0:TSz], [gb, cf], [pmisc])
            TR(ptr[0:64, 256:256 + TSz], krb[0:TSz, :], identb[0:TSz, 0:TSz], [krb, cb], [ptr])
            CP(ckvnT[:, :, j * 128:j * 128 + TSz], ptr.ap.rearrange("p (k t) -> p k t", k=8)[:, 0:2, 0:TSz],
               [ptr], [ckvnT])
            CP(krT[0:64, j * 128:j * 128 + TSz], ptr[0:64, 256:256 + TSz], [ptr], [krT])
            CP(grow[:, 0, j * 128:j * 128 + TSz], pmisc[0:4, 128:128 + TSz], [pmisc], [grow])
            CP(grow[:, 1, j * 128:j * 128 + TSz], pmisc[0:4, 256:256 + TSz], [pmisc], [grow])
        av = grow[:, 0, 0:T].rearrange("p (c t) -> p c t", t=64)
        cv = grow[:, 1, 0:T].rearrange("p (c t) -> p c t", t=64)
        mb = P.buf("mallb") if "mallb" not in kscr_b else kscr_b["mallb"]
        kscr_b["mallb"] = mb
        P.op("dve", "tensor_reduce", dict(out=mall[:, 10:10 + NCH], in_=av, axis=AX.X, op=ALU.max), bl([grow]), bl([mb]))
        TSM(mall[:, 20:20 + NCH], cv[:, :, 63], -1.0, [grow], [mb])
        P.op("dve", "tensor_tensor_scan", dict(out=mall[:, 1:1 + NCH], data0=mall[:, 10:10 + NCH], data1=mall[:, 20:20 + NCH],
                                               initial=mall[:, 0:1], op0=ALU.max, op1=ALU.add), bl([mb]), bl([mb]))
        TT(mall[:, 10:10 + NCH], mall[:, 1:1 + NCH], mall[:, 20:20 + NCH], ALU.subtract, [mb], [mb])
        TT(mall[:, 20:20 + NCH], mall[:, 0:NCH], mall[:, 10:10 + NCH], ALU.subtract, [mb], [mb])
        ACT(mall[:, 20:20 + NCH], mall[:, 20:20 + NCH], AF.Exp, [mb], [mb])
        Mb = mall[:, 10:10 + NCH].unsqueeze(2).to_broadcast([4, NCH, 64])
        TT(av, av, Mb, ALU.subtract, [grow, mb], [grow])
        TT(cv, cv, Mb, ALU.subtract, [grow, mb], [grow])
        ACT(grow[:, 0:2, 0:T], grow[:, 0:2, 0:T], AF.Exp, [grow], [grow])
        TT(decbd[:, 0:NCH, :], mall[:, 20:20 + NCH].unsqueeze(2).to_broadcast([4, NCH, 4]),
           eye4.unsqueeze(1).to_broadcast([4, NCH, 4]), ALU.mult, [mb, cf], [decbd])
        MM(pmisc[:, 0:NCH * 4], ones4, decbd[:, 0:NCH, :].rearrange("p c h -> p (c h)"), True, True, [cf, decbd], [pmisc])
        CP(decbc[:, 0:NCH, :].rearrange("p c h -> p (c h)"), pmisc[:, 0:NCH * 4], [pmisc], [decbc])
        for j in range(ntl):
            TR(pmisc[0:TSz, 384:388], grow[:, 0, j * 128:j * 128 + TSz], eye4, [grow, cf], [pmisc])
            TR(pmisc[0:TSz, 392:396], grow[:, 1, j * 128:j * 128 + TSz], eye4, [grow, cf], [pmisc])
            CP(wkc[0:TSz, j, 0:4], pmisc[0:TSz, 384:388], [pmisc], [wkc])
            ACT(wkc[0:TSz, j, 4:8], pmisc[0:TSz, 384:388], AF.Copy, [pmisc], [wkc], scale=1.0 / 16)
            CP(wkc[0:TSz, j, 8:12], pmisc[0:TSz, 392:396], [pmisc], [wkc])
        if last_block:
            DMA("sp", m_o[si], mall[:, NCH:NCH + 1], [mb], [], sigM)
        CP(mall[:, 0:1], mall[:, NCH:NCH + 1], [mb], [mb])

        for h in range(HM):
            for ec in range(2):
                pq, pk_ = pb[0], pb[1]
                for dc in range(2):
                    MM(pq[:, 0:T], wq[:, h, dc, ec * 128:(ec + 1) * 128], uhT[:, 2 * h + dc, 0:T], dc == 0, dc == 1,
                       [wq, uhT], [pq])
                for dc in range(2):
                    MM(pk_[:, 0:T], wk[:, h, dc, ec * 128:(ec + 1) * 128], uhT[:, 2 * h + dc, 0:T], dc == 0, dc == 1,
                       [wk, uhT], [pk_])
                CP(qTm[:, h, ec, 0:T], pq[:, 0:T], [pq], [qTm])
                ACT(kTm[:, h, ec, 0:T], pk_[:, 0:T], AF.Copy, [pk_], [kTm], scale=1.0 / 16)
            for j in range(ntl):
                pkt = pb[2 + j % 2]
                for dc in range(2):
                    MM(pkt[0:TSz, 0:256], uhT[:, 2 * h + dc, j * 128:j * 128 + TSz], wk[:, h, dc, :], dc == 0, dc == 1,
                       [uhT, wk], [pkt])
                ACT(kw[0:TSz, j, h, :], pkt[0:TSz, 0:256], AF.Copy, [pkt, wkc], [kw], scale=wkc[0:TSz, j, 4 + h:5 + h])

        for c in range(NCH):
            j, ph = c // 2, 64 * (c % 2)
            t0 = c * 64
            for h in range(HM):
                reg = pms[(c * HM + h) % 6]
                for dc in range(2):
                    MM(reg[ph:ph + 64, :], kTm[:, h, dc, t0:t0 + 64], qTm[:, h, dc, t0:t0 + 64], dc == 0, dc == 1,
                       [kTm, qTm], [reg])
                STT(swA[ph:ph + 64, j, h, :], reg[ph:ph + 64, :], wkc[ph:ph + 64, j, h:h + 1], cmask[ph:ph + 64, :],
                    ALU.mult, ALU.mult, [reg, wkc, cf], [swA])
        for c in range(NCH):
            j, ph = c // 2, 64 * (c % 2)
            t0 = c * 64
            hr = hraw[j % 2]
            for pair in ((0, 1), (2, 3)):
                for h in pair:
                    dec = decbc[:, c, h:h + 1]
                    for dc in range(2):
                        if dc == 0:
                            ACT(Cbf[h][dc].ap, CstH[h][:, dc, :], AF.Copy, [CstH[h], decbc], [Cbf[h][dc]], scale=dec)
                        else:
                            TSM(Cbf[h][dc].ap, CstH[h][:, dc, :], dec, [CstH[h], decbc], [Cbf[h][dc]])
                for i, h in enumerate(pair):
                    for dc in range(2):
                        pU = pb[2 * i + dc]
                        MM(pU[:, 0:257], kw[ph:ph + 64, j, h, dc * 128:(dc + 1) * 128], v_ext[ph:ph + 64, j, h, :],
                           True, True, [kw, v_ext], [pU])
                for i, h in enumerate(pair):
                    dec = decbc[:, c, h:h + 1]
                    for dc in range(2):
                        pU = pb[2 * i + dc]
                        STT(CstH[h][:, dc, :], CstH[h][:, dc, :], dec, pU[:, 0:257], ALU.mult, ALU.add,
                            [CstH[h], decbc, pU], [CstH[h]])
                for i, h in enumerate(pair):
                    pN = pb[4 + i]
                    for dc in range(2):
                        MM(pN[ph:ph + 64, 0:257], qTm[:, h, dc, t0:t0 + 64], Cbf[h][dc].ap, dc == 0, False,
                           [qTm, Cbf[h][dc]], [pN])
                    MM(pN[ph:ph + 64, 0:257], swA[ph:ph + 64, j, h, :], v_ext[ph:ph + 64, j, h, :], False, True,
                       [swA, v_ext], [pN])
                for i, h in enumerate(pair):
                    pN = pb[4 + i]
                    ACT(hr[ph:ph + 64, h, :], pN[ph:ph + 64, 0:257], AF.Copy, [pN], [hr])
            if c % 2 == 1 or c == NCH - 1:
                hb = smallb[4]
                den4 = hr[0:TSz, :, 256]
                STT(small[0:TSz, 8:12], den4, -1.0, den4, ALU.mult, ALU.max, [hr], [hb])
                TT(small[0:TSz, 8:12], small[0:TSz, 8:12], wkc[0:TSz, j, 8:12], ALU.max, [hb, wkc], [hb])
                RCP(small[0:TSz, 8:12], small[0:TSz, 8:12], [hb], [hb])
                TT(hh[0:TSz, :, :], hr[0:TSz, :, 0:256], small[0:TSz, 8:12].unsqueeze(2).to_broadcast([TSz, 4, 256]),
                   ALU.mult, [hr, hb], [hh])
                for h in range(HM):
                    ACT(junk[0:TSz, 0:256], hh[0:TSz, h, :], AF.Square, [hh], [junk, smallb[5]],
                        accum_out=small[0:TSz, 12 + h:13 + h])
                rsqrt_act(small[0:TSz, 12:16], small[0:TSz, 12:16], 1.0 / DH, [smallb[5], cf], [smallb[5]], None)
                for h in range(HM):
                    STT(hmg[0:TSz, h * 256:(h + 1) * 256], hh[0:TSz, h, :], small[0:TSz, 12 + h:13 + h],
                        gmg[0:TSz, j, h * 256:(h + 1) * 256], ALU.mult, ALU.mult, [hh, smallb[5], gmg], [hmg])
                for kc in range(8):
                    TR(ptr[:, kc * 128:kc * 128 + TSz], hmg[0:TSz, kc * 128:(kc + 1) * 128], identb[0:TSz, 0:TSz],
                       [hmg, cb], [ptr])
                ACT(hmT[:, :, j * 128:j * 128 + TSz], ptr.ap.rearrange("p (k t) -> p k t", k=8)[:, :, 0:TSz], AF.Copy,
                    [ptr], [hmT])

        u9 = unit(ucount + 9)
        for k3 in range(3):
            pq = pb[k3 % 2]
            for kc in range(8):
                MM(pq[:, 0:T], u9[:, kc, k3 * 128:(k3 + 1) * 128], hT[:, kc, 0:T], kc == 0, kc == 7, [u9, hT], [pq])
            ACT(cqf[:, k3, 0:T], pq[:, 0:T], AF.Copy, [pq], [cqf])
            ACT(sqb[k3][:, 0:T], pq[:, 0:T], AF.Square, [pq], [sqb[k3]])
        psq = pb[2]
        for k3 in range(3):
            MM(psq[:, 0:T], onesb, sqb[k3][:, 0:T], k3 == 0, k3 == 2, [cb, sqb[k3]], [psq])
        rsqrt_act(rr[0][:, 0:T], psq[:, 0:T], 1.0 / QL, [psq, cf], [rr[0]], None)
        for k3 in range(3):
            STT(cqn[:, k3, 0:T], cqf[:, k3, 0:T], qng_c[:, k3:k3 + 1], rr[0][:, 0:T], ALU.mult, ALU.mult,
                [cqf, rr[0], cf], [cqn])
        MEMSET(qrT[64:128, :, :], 0.0, [qrT], eng="dve")
        for h0 in range(0, HA, 2):
            pair = (h0, h0 + 1)
            for s_, h in enumerate(pair):
                pn, pr = pb[3 * s_], pb[3 * s_ + 1]
                for k3 in range(3):
                    MM(pn[:, 0:T], wuqn[:, k3, h * 128:(h + 1) * 128], cqn[:, k3, 0:T], k3 == 0, k3 == 2, [wuqn, cqn], [pn])
                for k3 in range(3):
                    MM(pr[0:64, 0:T], wuqr[:, k3, h * 64:(h + 1) * 64], cqn[:, k3, 0:T], k3 == 0, k3 == 2, [wuqr, cqn], [pr])
            for s_, h in enumerate(pair):
                pn, pr = pb[3 * s_], pb[3 * s_ + 1]
                ACT(qsq[2 * s_][:, 0:T], pn[:, 0:T], AF.Square, [pn], [qsq[2 * s_]])
                ACT(qsq[2 * s_ + 1][0:64, 0:T], pr[0:64, 0:T], AF.Square, [pr], [qsq[2 * s_ + 1]])
            for s_, h in enumerate(pair):
                ps1, ps2 = pb[3 * s_ + 2], (pmisc if s_ == 0 else pS3[2])
                MM(ps1[:, 0:T], onesb, qsq[2 * s_][:, 0:T], True, True, [cb, qsq[2 * s_]], [ps1])
                MM(ps2[0:64, 0:T], onesb[0:64, 0:64], qsq[2 * s_ + 1][0:64, 0:T], True, True, [cb, qsq[2 * s_ + 1]], [ps2])
            for s_, h in enumerate(pair):
                ps1, ps2 = pb[3 * s_ + 2], (pmisc if s_ == 0 else pS3[2])
                rsqrt_act(qrr[2 * s_][:, 0:T], ps1[:, 0:T], 1.0 / NOPE, [ps1, cf], [qrr[2 * s_]], None)
                rsqrt_act(qrr[2 * s_ + 1][0:64, 0:T], ps2[0:64, 0:T], 1.0 / ROPE, [ps2, cf], [qrr[2 * s_ + 1]], None)
            for s_, h in enumerate(pair):
                pn, pr = pb[3 * s_], pb[3 * s_ + 1]
                STT(qnT[:, h, 0:T], pn[:, 0:T], gqn_c, qrr[2 * s_][:, 0:T], ALU.mult, ALU.mult, [pn, qrr[2 * s_], cf], [qnT])
                STT(qxr[s_][:, 0:T], pr[0:64, 0:T], gqr_c[0:64, :], qrr[2 * s_ + 1][0:64, 0:T], ALU.mult, ALU.mult,
                    [pr, qrr[2 * s_ + 1], cf], [qxr[s_]])
            for s_, h in enumerate(pair):
                prr = pb[3 * s_ + 2]
                MM(prr[0:64, 0:T], RTb, qxr[s_][:, 0:T], True, True, [cb, qxr[s_]], [prr])
            for s_, h in enumerate(pair):
                prr = pb[3 * s_ + 2]
                t1, t2 = qtc[2 * s_], qtc[2 * s_ + 1]
                TT(t1[0:64, 0:T], qxr[s_][:, 0:T], cosT[:, 0:T], ALU.mult, [qxr[s_], cosT], [t1])
                TT(t2[0:64, 0:T], prr[0:64, 0:T], sinT[:, 0:T], ALU.mult, [prr, cosT], [t2])
                TT(qrT[0:64, h, 0:T], t1[0:64, 0:T], t2[0:64, 0:T], ALU.add, [t1, t2], [qrT])

        kv_block(si, kb_cur, T, write_scratch)
        attention(si, T, nkb_past, ucount)
        if debug and last_block:
            DMA("sp", dbg_d["qn"], qnT.ap, [qnT], [], dbg_sig)
            DMA("sp", dbg_d["kn"], knT.ap, [knT], [], dbg_sig)

        for j in range(ntl):
            r0 = row0 + j * 128
            DMA("sp", xres[j][0:TSz, :], xs[r0:r0 + TSz, :], [], [xres[j]], xres_sig[j])
        for half in range(2):
            ub = ucount + 12 + half * 4
            for which, dst in ((0, sgm), (1, sga)):
                u = unit(ub + which)
                for c in range(4):
                    pg = pb[c % 2]
                    for kc in range(8):
                        MM(pg[:, 0:T], u[:, kc, c * 128:(c + 1) * 128], hT[:, kc, 0:T], kc == 0, kc == 7, [u, hT], [pg])
                    e = td[c % 2]
                    ACT(e[:, 0:T], pg[:, 0:T], AF.Exp, [pg], [e], scale=-1.0)
                    ACT(e[:, 0:T], e[:, 0:T], AF.Ln, [e], [e], bias=1.0)
                    ACT(dst[:, c, 0:T], e[:, 0:T], AF.Exp, [e], [dst], scale=-1.0)
            u = unit(ub + 2)
            for c in range(4):
                pg = pb[2 + c % 2]
                for kc in range(8):
                    MM(pg[:, 0:T], u[:, kc, c * 128:(c + 1) * 128], hmT[:, kc, 0:T], kc == 0, kc == 7, [u, hmT], [pg])
                TT(tmpm[:, c, 0:T], pg[:, 0:T], sgm[:, c, 0:T], ALU.mult, [pg, sgm], [tmpm])
            u = unit(ub + 3)
            for c in range(4):
                pg = pb[c % 2]
                for kc in range(8):
                    MM(pg[:, 0:T], u[:, kc, c * 128:(c + 1) * 128], haT[:, kc, 0:T], kc == 0, kc == 7, [u, haT], [pg])
                e = td[c % 2]
                TT(e[:, 0:T], pg[:, 0:T], sga[:, c, 0:T], ALU.mult, [pg, sga], [e])
                TT(mergedT[:, half * 4 + c, 0:T], e[:, 0:T], tmpm[:, c, 0:T], ALU.add, [e, tmpm], [mergedT])
        uo0 = unit(ucount + 20)
        uo1 = unit(ucount + 21, oldest=ucount + 20)
        for j in range(ntl):
            xi = xres[j]
            r0 = row0 + j * 128
            yo = yout[j % 2]
            for g, u in ((0, uo0), (1, uo1)):
                py = pb[2 + g]
                for kc in range(8):
                    MM(py[0:TSz, :], mergedT[:, kc, j * 128:j * 128 + TSz], u[:, kc, :], kc == 0, kc == 7,
                       [mergedT, u], [py])
                TT(yo[0:TSz, g * 512:(g + 1) * 512], py[0:TSz, :], xi[0:TSz, g * 512:(g + 1) * 512], ALU.add,
                   [py, xi], [yo])
            DMA("sp", y_d[r0:r0 + TSz, :], yo[0:TSz, :], [yo], [], yo_sig[j % 2])

    nblocks = sum((L + 511) // 512 for (_, L, _, _) in seqs)
    ring_state["total"] = nblocks * NUNIT
    ucount = 0
    for si, (kind, L, row0, npast) in enumerate(seqs):
        mb = P.buf("mallb") if "mallb" not in kscr_b else kscr_b["mallb"]
        kscr_b["mallb"] = mb
        if npast == 0:
            MEMSET(Cst.ap, 0.0, CstH, eng="dve")
            MEMSET(hist.ap, 0.0, [hist], eng="dve")
            MEMSET(mall[:, 0:1], 0.0, [mb], eng="dve")
        else:
            DMA("sp", Cst[:, :, :, 0:256], sC_d.rearrange("h (dc p) e -> p h dc e", p=128), [], CstH, sigC)
            for h in range(HM):
                DMA("sp", Cst[:, h, :, 256], sn_d[h].rearrange("(dc p) -> p dc", p=128), [], CstH, sigC,
                    allow_slow_non_contiguous=True)
            for jj in range(3):
                DMA("sp", hist[:, :, jj], sconv_d[jj].rearrange("(c p) -> p c", p=128), [], [hist], sigH,
                    allow_slow_non_contiguous=True)
            DMA("sp", mall[:, 0:1], sm_d, [], [mb], sigM)
            for kb in range(npast // 512):
                for jt in range(4):
                    k0 = kb * 512 + jt * 128
                    DMA("sp", pcb.ap, cckv_bf[k0:k0 + 128, :], [wbfB], [pcb], pcb_sig)
                    DMA("sp", pcr.ap, ckr_bf[k0:k0 + 128, :], [wbfB], [pcr], pcr_sig)
                    for c2 in range(2):
                        TR(ptr[:, c2 * 128:(c2 + 1) * 128], pcb[:, c2 * 128:(c2 + 1) * 128], identb, [pcb, cb], [ptr])
                    TR(ptr[0:64, 256:384], pcr.ap, identb, [pcr, cb], [ptr])
                    CP(ckvnT[:, :, jt * 128:(jt + 1) * 128], ptr.ap.rearrange("p (k t) -> p k t", k=8)[:, 0:2, :],
                       [ptr], [ckvnT])
                    CP(krT[0:64, jt * 128:(jt + 1) * 128], ptr[0:64, 256:384], [ptr], [krT])
                kv_block(si, kb, 512, True)
        nb = (L + 511) // 512
        for b in range(nb):
            T = min(512, L - b * 512)
            block(si, kind, row0 + b * 512, T, npast + b * 512, npast // 512 + b, npast // 512 + b, ucount,
                  b == nb - 1, b < nb - 1)
            ucount += NUNIT
        DMA("sp", C_o[si].rearrange("h (dc p) e -> p h dc e", p=128), Cst[:, :, :, 0:256], CstH, [], sigC)
        for h in range(HM):
            DMA("sp", n_o[si, h].rearrange("(dc p) -> p dc", p=128), Cst[:, h, :, 256], CstH, [], sigC,
                allow_slow_non_contiguous=True)
        for jj in range(3):
            DMA("sp", conv_o[si, jj].rearrange("(c p) -> p c", p=128), hist[:, :, jj], [hist], [], sigH,
                allow_slow_non_contiguous=True)

    if debug:
        for n, lb in (("hm", hmT), ("ha", haT), ("mg", mergedT), ("hT", hT)):
            DMA("sp", dbg_d[n], lb.ap, [lb], [], dbg_sig)
    stats = P.lower()
    es.close()
    return nc, stats


def _rope_tables():
    half = ROPE // 2
    inv = np.power(np.float32(10000.0), -np.arange(half, dtype=np.float32) / np.float32(half)).astype(np.float32)
    pos = np.arange(NPOS, dtype=np.float32)
    ang = (pos[:, None] * inv[None, :]).astype(np.float32)
    cos = np.cos(ang.astype(np.float64)).astype(np.float32)
    sin = np.sin(ang.astype(np.float64)).astype(np.float32)
    cs_tm = np.concatenate([cos, cos], axis=1)
    sn_tm = np.concatenate([-sin, sin], axis=1)
    return (np.ascontiguousarray(cs_tm.T), np.ascontiguousarray(np.concatenate([sin, sin], axis=1).T),
            np.ascontiguousarray(cs_tm), np.ascontiguousarray(sn_tm))


def _prep_shared(norm_g, w_in, b_if, conv_w, conv_b, wq_m, wk_m, hnorm_g, qn_g, w_uq, kvn_g, w_ukv,
                 g_qn, g_qr, g_kn, g_kr, w_pm, w_pa, w_out):
    f = np.float32
    w_in, w_pm, w_pa, w_out = w_in[0], w_pm[0], w_pa[0], w_out[0]
    o = np.cumsum([0, 1024, 1024, 4, 4, 1024, 1024, QL, KVL, ROPE, 1024, 1024, 1024])
    seg = {n: (o[i], o[i + 1]) for i, n in enumerate(
        ["xc", "vm", "ig", "fg", "op", "zm", "cq", "ckv", "kr", "za", "gm", "ga"])}

    def cols(n, a=0, b=None):
        s, e = seg[n]
        return w_in[:, s + a:(e if b is None else s + b)]

    z = lambda n: np.zeros((1024, n), f)
    units = [cols("xc", 0, 512), cols("xc", 512, 1024), cols("vm", 0, 512), cols("vm", 512, 1024),
             cols("op", 0, 512), cols("zm", 0, 512), cols("op", 512, 1024), cols("zm", 512, 1024),
             np.concatenate([cols("ckv"), cols("kr"), cols("ig"), cols("fg"), z(512 - 328)], axis=1),
             np.concatenate([cols("cq"), z(512 - QL)], axis=1),
             cols("za", 0, 512), cols("za", 512, 1024),
             cols("gm", 0, 512), cols("ga", 0, 512), w_pm[:, 0:512], w_pa[:, 0:512],
             cols("gm", 512, 1024), cols("ga", 512, 1024), w_pm[:, 512:1024], w_pa[:, 512:1024],
             w_out[:, 0:512], w_out[:, 512:1024]]
    assert len(units) == NUNIT
    wcat = np.stack([u.reshape(8, 128, 512).transpose(1, 0, 2) for u in units]).astype(f)
    wuq = w_uq[0].reshape(3, 128, HA, NOPE + ROPE).transpose(1, 0, 2, 3)
    wuqn = np.ascontiguousarray(wuq[..., :NOPE].reshape(128, 3, HA * NOPE))
    wuqr = np.ascontiguousarray(wuq[..., NOPE:].reshape(128, 3, HA * ROPE))
    wukv = w_ukv[0].reshape(2, 128, HA, NOPE + VD).transpose(1, 0, 2, 3)
    wukvk = np.ascontiguousarray(wukv[..., :NOPE].reshape(128, 2, HA * NOPE))
    wukvv = np.ascontiguousarray(wukv[..., NOPE:].reshape(128, 2, HA * VD))
    wq = np.ascontiguousarray(wq_m[0].reshape(HM, 2, 128, DH).transpose(2, 0, 1, 3))
    wk = np.ascontiguousarray(wk_m[0].reshape(HM, 2, 128, DH).transpose(2, 0, 1, 3))
    cf = np.zeros((128, 512), f)
    cf[:, 0:128] = np.eye(128, dtype=f)
    s_ = np.arange(128)[:, None]
    t_ = np.arange(128)[None, :]
    cf[:, 128:256] = ((s_ <= t_) & (s_ // 64 == t_ // 64)).astype(f)
    cf[:, 256:320] = ((s_ % 64) <= np.arange(64)[None, :]).astype(f)
    cf[:, 320:328] = norm_g[0].reshape(8, 128).T
    cf[:, 328:360] = conv_w[0].reshape(4, 8, 128).transpose(2, 1, 0).reshape(128, 32)
    cf[:, 360:368] = conv_b[0].reshape(8, 128).T
    cf[:, 368:371] = qn_g[0].reshape(3, 128).T
    cf[:, 371] = g_qn[0]
    cf[:, 372] = g_kn[0]
    cf[0:64, 373] = g_qr[0]
    cf[:, 374] = EPS
    cf[0:4, 384:512] = 1.0
    cbm = np.zeros((128, 448), f)
    cbm[:, 0:128] = np.eye(128, dtype=f)
    cbm[:, 128:256] = 1.0
    RT = np.zeros((64, 64), f)
    for i in range(32):
        RT[i + 32, i] = -1.0
        RT[i, i + 32] = 1.0
    cbm[0:64, 256:320] = RT
    bcm = np.concatenate([hnorm_g[0].reshape(-1), kvn_g[0], g_kr[0], b_if[0]]).astype(f)
    bcm = np.ascontiguousarray(np.broadcast_to(bcm[None, :], (128, bcm.shape[0])))
    cosT, sinT, cstm, sntm = _rope_tables()
    return dict(wcat=wcat, wuqn=wuqn, wuqr=wuqr, wukvk=wukvk, wukvv=wukvv, wq=wq, wk=wk, cf=cf, cb=cbm, bc=bcm,
                cosT=cosT, sinT=sinT, cstm=cstm, sntm=sntm)


_CACHE = {}


def kernel(x_prompt, x_sample, cache_ckv, cache_kr, state_conv, state_C, state_n, state_m,
           norm_g, w_in, b_if, conv_w, conv_b, wq_m, wk_m, hnorm_g,
           qn_g, w_uq, kvn_g, w_ukv, g_qn, g_qr, g_kn, g_kr, w_pm, w_pa, w_out):
    A = lambda a: np.ascontiguousarray(np.asarray(a, dtype=np.float32))
    x_prompt, x_sample = A(x_prompt), A(x_sample)
    B, S, _ = x_prompt.shape
    DB, DS, _ = x_sample.shape
    NCORE = 8
    bpc = B // NCORE
    P_ = np.asarray(cache_ckv).shape[2]
    seqs = [("prompt", S, i * S, 0) for i in range(bpc)] + [("sample", DS, bpc * S, P_)]
    ntok = bpc * S + DS
    shared = _prep_shared(*[A(a) for a in (norm_g, w_in, b_if, conv_w, conv_b, wq_m, wk_m, hnorm_g, qn_g, w_uq, kvn_g,
                                            w_ukv, g_qn, g_qr, g_kn, g_kr, w_pm, w_pa, w_out)])
    key = (tuple(seqs), ntok)
    if key not in _CACHE:
        _CACHE[key] = build(seqs, ntok)[0]
    nc = _CACHE[key]
    in_maps = []
    for c in range(NCORE):
        xs = np.concatenate([x_prompt[c * bpc:(c + 1) * bpc].reshape(bpc * S, D), x_sample[c]], axis=0)
        m = dict(shared)
        m.update(xs=np.ascontiguousarray(xs), cckv=A(cache_ckv)[0, c], ckr=A(cache_kr)[0, c],
                 sconv=A(state_conv)[0, c], sC=A(state_C)[0, c], sn=A(state_n)[0, c],
                 sm=A(state_m)[0, c].reshape(HM, 1))
        in_maps.append(m)
    res = run_bass_kernel_spmd(nc, in_maps, core_ids=list(range(NCORE)))
    R = res.results
    cat = lambda k: [r[k] for r in R]
    yp = np.stack([r["y"][:bpc * S].reshape(bpc, S, D) for r in R]).reshape(B, S, D)
    ys = np.stack([r["y"][bpc * S:] for r in R])
    ckv_p = np.stack([r["ckv_o"][:bpc * S].reshape(bpc, S, KVL) for r in R]).reshape(1, B, S, KVL)
    ckv_s = np.stack([r["ckv_o"][bpc * S:] for r in R])[None]
    kr_p = np.stack([r["kr_o"][:bpc * S].reshape(bpc, S, ROPE) for r in R]).reshape(1, B, S, ROPE)
    kr_s = np.stack([r["kr_o"][bpc * S:] for r in R])[None]
    conv_p = np.stack([r["conv_o"][:bpc] for r in R]).reshape(1, B, 3, D)
    conv_s = np.stack([r["conv_o"][bpc] for r in R])[None]
    C_p = np.stack([r["C_o"][:bpc] for r in R]).reshape(1, B, HM, DH, DH)
    C_s = np.stack([r["C_o"][bpc] for r in R])[None]
    n_p = np.stack([r["n_o"][:bpc] for r in R]).reshape(1, B, HM, DH)
    n_s = np.stack([r["n_o"][bpc] for r in R])[None]
    m_p = np.stack([r["m_o"][:bpc] for r in R]).reshape(1, B, HM)
    m_s = np.stack([r["m_o"][bpc] for r in R]).reshape(1, DB, HM)
    f = np.float32
    return tuple(np.ascontiguousarray(a, dtype=f) for a in
                 (yp, ys, ckv_p, kr_p, conv_p, C_p, n_p, m_p, ckv_s, kr_s, conv_s, C_s, n_s, m_s))
```
